# Optimizing a Trainium2 kernel written in Bass

```python
import math
import jax
import jax.numpy as jnp
from jax import lax
import numpy as np

D_MODEL = 1024
BATCH = 4
SEQ = 4096
DEPTH = 2

GRID_W = 64
CTX_LEN = 256
D_MIX = D_MODEL
EPS = 1e-6

MLA_HEADS = 6
MLA_NOPE = 64
MLA_ROPE = 32
MLA_QK = MLA_NOPE + MLA_ROPE
MLA_V = 64
MLA_Q_LORA = 256
MLA_KV_LORA = 128
ROPE_THETA = 10000.0
ROPE_FREQS = MLA_ROPE // 4
Q_BLOCK = 128

SSM_HEADS = 6
SSM_HEAD_DIM = 64
SSM_INNER = SSM_HEADS * SSM_HEAD_DIM
SSM_GROUPS = 2
SSM_STATE = 64
SSM_CONV = 3
SSM_CHUNK = 128
SSM_XBC = SSM_INNER + 2 * SSM_GROUPS * SSM_STATE

HY_CH = D_MIX - MLA_HEADS * MLA_V - SSM_INNER
HY_ORDER = 2
HY_CONV = 3
HY_BANDS = 16
HY_EMB = 1 + 2 * HY_BANDS
HY_HIDDEN = 64
HY_DECAY_PCT_SHORT = 0.3
HY_DECAY_PCT_LONG = 1.5
HY_DECAY_TARGET = 1e-2

D_FF = 4 * D_MODEL

IN_SPLITS = (MLA_Q_LORA, MLA_KV_LORA, MLA_ROPE, SSM_INNER, SSM_XBC, 2 * SSM_HEADS, (HY_ORDER + 1) * HY_CH)
IN_COLS = sum(IN_SPLITS)
IN_OFFSETS = tuple(int(v) for v in np.cumsum(IN_SPLITS)[:-1])

kernel_name = 'hybrid_mla_ssd_hyena_prefix_block'


def rms_norm(x, g):
    xf = x.astype(jnp.float32)
    y = xf * lax.rsqrt(jnp.mean(xf * xf, axis=-1, keepdims=True) + EPS)
    return (y * g.astype(jnp.float32)).astype(x.dtype)


def modulate(h, shift, scale):
    return h * (1 + scale) + shift


def dw_conv(u, w, b):
    pad = w.shape[0] // 2
    y = lax.conv_general_dilated(u, w[:, None, :].astype(u.dtype), window_strides=(1,),
                                 padding=[(pad, pad)], dimension_numbers=('NWC', 'WIO', 'NWC'),
                                 feature_group_count=u.shape[-1])
    return y + b.astype(u.dtype)


def axial_rope_tables(length):
    rows = length // GRID_W
    row = jnp.repeat(jnp.arange(rows), GRID_W).astype(jnp.float32)
    col = (jnp.arange(rows * GRID_W) % GRID_W).astype(jnp.float32)
    inv = ROPE_THETA ** (-jnp.arange(ROPE_FREQS, dtype=jnp.float32) / ROPE_FREQS)
    ang = jnp.stack([row[:, None] * inv, col[:, None] * inv], axis=1)
    return jnp.cos(ang), jnp.sin(ang)


def apply_axial_rope(x, cos, sin):
    xr = x.astype(jnp.float32).reshape(x.shape[:-1] + (2, 2, ROPE_FREQS))
    x1, x2 = xr[..., 0, :], xr[..., 1, :]
    cs, sn = cos[None, :, None], sin[None, :, None]
    out = jnp.stack([x1 * cs - x2 * sn, x2 * cs + x1 * sn], axis=-2)
    return out.reshape(x.shape).astype(x.dtype)


def mla_query(cq, g_cq, w_uq, g_qh, rope):
    b, l, _ = cq.shape
    q = (rms_norm(cq, g_cq) @ w_uq).reshape(b, l, MLA_HEADS, MLA_QK)
    q = rms_norm(q, g_qh)
    if rope is None:
        return q
    return jnp.concatenate([q[..., :MLA_NOPE], apply_axial_rope(q[..., MLA_NOPE:], *rope)], axis=-1)


def mla_keyval(ckv, krope, g_ckv, w_ukv, g_kh, rope):
    b, l, _ = ckv.shape
    kv = (rms_norm(ckv, g_ckv) @ w_ukv).reshape(b, l, MLA_HEADS, MLA_NOPE + MLA_V)
    k_nope, v = kv[..., :MLA_NOPE], kv[..., MLA_NOPE:]
    k_pe = jnp.broadcast_to(krope[:, :, None, :], (b, l, MLA_HEADS, MLA_ROPE))
    k = rms_norm(jnp.concatenate([k_nope, k_pe], axis=-1), g_kh)
    if rope is not None:
        k = jnp.concatenate([k[..., :MLA_NOPE], apply_axial_rope(k[..., MLA_NOPE:], *rope)], axis=-1)
    return k, v


def block_attention(q, k, v):
    b, lq, h, e = q.shape
    scale = 1.0 / math.sqrt(e)
    qb = q.reshape(b, lq // Q_BLOCK, Q_BLOCK, h, e).transpose(1, 0, 2, 3, 4)

    def one_block(qi):
        s = jnp.einsum('bqhe,bkhe->bhqk', qi, k, preferred_element_type=jnp.float32) * scale
        p = jax.nn.softmax(s, axis=-1).astype(v.dtype)
        return jnp.einsum('bhqk,bkhv->bqhv', p, v)

    out = lax.map(one_block, qb)
    return out.transpose(1, 0, 2, 3, 4).reshape(b, lq, h * v.shape[-1])


def ssd_scan(xh, dt, a, bh, ch, init_state):
    b, l, h, p = xh.shape
    n = bh.shape[-1]
    nc = l // SSM_CHUNK
    xc = xh.astype(jnp.float32).reshape(b, nc, SSM_CHUNK, h, p)
    bc = bh.astype(jnp.float32).reshape(b, nc, SSM_CHUNK, h, n)
    cc = ch.astype(jnp.float32).reshape(b, nc, SSM_CHUNK, h, n)
    dtc = dt.reshape(b, nc, SSM_CHUNK, h)
    cum = jnp.cumsum(dtc * a, axis=2)
    tri = jnp.tril(jnp.ones((SSM_CHUNK, SSM_CHUNK), dtype=bool))[None, None, :, :, None]
    seg = cum[:, :, :, None, :] - cum[:, :, None, :, :]
    decay = jnp.where(tri, jnp.exp(jnp.where(tri, seg, 0.0)), 0.0)
    scores = jnp.einsum('bcihn,bcjhn->bcijh', cc, bc) * decay
    y_diag = jnp.einsum('bcijh,bcjhp->bcihp', scores * dtc[:, :, None, :, :], xc)
    w_end = jnp.exp(cum[:, :, -1:, :] - cum) * dtc
    chunk_states = jnp.einsum('bcjhn,bcjh,bcjhp->bchpn', bc, w_end, xc)
    chunk_decay = jnp.exp(cum[:, :, -1, :])

    def step(s, inp):
        st, dec = inp
        return s * dec[:, :, None, None] + st, s

    final, prev = lax.scan(step, init_state.astype(jnp.float32),
                           (jnp.moveaxis(chunk_states, 1, 0), jnp.moveaxis(chunk_decay, 1, 0)))
    prev = jnp.moveaxis(prev, 0, 1)
    y_off = jnp.einsum('bcihn,bchpn->bcihp', cc, prev) * jnp.exp(cum)[..., None]
    return (y_diag + y_off).reshape(b, l, h, p), final


def ssm_prepare(xbc, dt_raw, w_conv, b_conv, dt_bias):
    b, l, _ = xbc.shape
    u = jax.nn.silu(dw_conv(xbc, w_conv, b_conv))
    rep = SSM_HEADS // SSM_GROUPS
    xs = u[..., :SSM_INNER].reshape(b, l, SSM_HEADS, SSM_HEAD_DIM)
    bs = jnp.repeat(u[..., SSM_INNER:SSM_INNER + SSM_GROUPS * SSM_STATE].reshape(b, l, SSM_GROUPS, SSM_STATE), rep, axis=2)
    cs = jnp.repeat(u[..., SSM_INNER + SSM_GROUPS * SSM_STATE:].reshape(b, l, SSM_GROUPS, SSM_STATE), rep, axis=2)
    dt = jax.nn.softplus(dt_raw.astype(jnp.float32).reshape(b, l, 2, SSM_HEADS) + dt_bias.astype(jnp.float32))
    return xs, bs, cs, dt


def bidir_ssd(xs, bs, cs, dt, a, init_fwd, init_bwd):
    flip = lambda t: jnp.flip(t, axis=1)
    y_f, s_f = ssd_scan(xs, dt[:, :, 0], a[0], bs, cs, init_fwd)
    y_b, s_b = ssd_scan(flip(xs), flip(dt[:, :, 1]), a[1], flip(bs), flip(cs), init_bwd)
    return y_f + flip(y_b), s_f, s_b


def ssm_out(y, xs, z, d_skip, g_norm):
    b, l = y.shape[:2]
    y = y + xs.astype(jnp.float32) * d_skip.astype(jnp.float32)[:, None]
    y = y.reshape(b, l, SSM_INNER) * jax.nn.silu(z.astype(jnp.float32))
    y = rms_norm(y.reshape(b, l, SSM_GROUPS, SSM_INNER // SSM_GROUPS), g_norm.reshape(SSM_GROUPS, -1))
    return y.reshape(b, l, SSM_INNER)


def hyena_filters(length, w_f1, b_f1, freq_f1, w_f2, b_f2, freq_f2, w_f3):
    f32 = jnp.float32
    t01 = jnp.linspace(0.0, 1.0, length, dtype=f32)[:, None]
    w = (2.0 * math.pi / length) * jnp.arange(length, dtype=f32)[:, None]
    bands = jnp.linspace(1e-4, HY_BANDS - 1, HY_BANDS, dtype=f32)[None, :]
    feats = jnp.concatenate([t01, jnp.cos(bands * w), -jnp.sin(bands * w)], axis=-1)
    h = jnp.sin(freq_f1.astype(f32) * (feats @ w_f1.astype(f32) + b_f1.astype(f32)))
    h = jnp.sin(freq_f2.astype(f32) * (h @ w_f2.astype(f32) + b_f2.astype(f32)))
    h = (h @ w_f3.astype(f32)).reshape(length, 2, HY_ORDER, HY_CH)
    deltas = jnp.abs(jnp.linspace(math.log(HY_DECAY_TARGET) / HY_DECAY_PCT_LONG,
                                  math.log(HY_DECAY_TARGET) / HY_DECAY_PCT_SHORT, HY_CH, dtype=f32))
    h = h * jnp.exp(-t01 * deltas)[:, None, None, :]
    filt = jnp.concatenate([h[:, 0], jnp.zeros((1, HY_ORDER, HY_CH), f32), jnp.flip(h[1:, 1], axis=0)], axis=0)
    return filt / (jnp.sum(jnp.abs(filt), axis=0, keepdims=True) + EPS)


def fft_long_conv(u, filt, d):
    l = u.shape[1]
    spec = jnp.fft.rfft(u, n=2 * l, axis=1) * jnp.fft.rfft(filt, axis=0)[None]
    return jnp.fft.irfft(spec, n=2 * l, axis=1)[:, :l] + u * d


def hyena_mixer(proj, w_conv, b_conv, w_f1, b_f1, freq_f1, w_f2, b_f2, freq_f2, w_f3, d_skip):
    l = proj.shape[1]
    parts = jnp.split(dw_conv(proj, w_conv, b_conv).astype(jnp.float32), HY_ORDER + 1, axis=-1)
    filt = hyena_filters(l, w_f1, b_f1, freq_f1, w_f2, b_f2, freq_f2, w_f3)
    z = parts[0]
    for o in range(HY_ORDER):
        z = parts[o + 1] * fft_long_conv(z, filt[:, o], d_skip[o].astype(jnp.float32))
    return z


def sq_relu_mlp(h, w1, w2):
    return jnp.square(jax.nn.relu(h @ w1)) @ w2


def setup_inputs(seed: int = 0) -> dict:
    f32 = jnp.float32
    keys = iter(jax.random.split(jax.random.key(seed), 40))

    def normal(shape, scale):
        return jax.random.normal(next(keys), shape, f32) * scale

    def gain(shape):
        return 1.0 + 0.05 * jax.random.normal(next(keys), shape, f32)

    dt0 = jnp.exp(jax.random.uniform(next(keys), (DEPTH, 2, SSM_HEADS), f32, math.log(1e-3), math.log(1e-1)))
    a0 = jax.random.uniform(next(keys), (DEPTH, 2, SSM_HEADS), f32, 1.0, 16.0)
    return {
        'x': normal((BATCH, SEQ, D_MODEL), 1.0),
        'c': normal((BATCH, D_MODEL), 1.0),
        'ctx': normal((BATCH, CTX_LEN, D_MODEL), 1.0),
        'c_ctx': normal((D_MODEL,), 1.0),
        'w_mod': normal((DEPTH, D_MODEL, 6 * D_MODEL), D_MODEL ** -0.5),
        'b_mod': normal((DEPTH, 6 * D_MODEL), 0.02),
        'g_norm_mix': gain((DEPTH, D_MODEL)),
        'g_norm_mlp': gain((DEPTH, D_MODEL)),
        'w_in': normal((DEPTH, D_MODEL, IN_COLS), D_MODEL ** -0.5),
        'w_out': normal((DEPTH, D_MIX, D_MODEL), D_MIX ** -0.5),
        'g_cq': gain((DEPTH, MLA_Q_LORA)),
        'g_ckv': gain((DEPTH, MLA_KV_LORA)),
        'w_uq': normal((DEPTH, MLA_Q_LORA, MLA_HEADS * MLA_QK), MLA_Q_LORA ** -0.5),
        'w_ukv': normal((DEPTH, MLA_KV_LORA, MLA_HEADS * (MLA_NOPE + MLA_V)), MLA_KV_LORA ** -0.5),
        'g_qhead': gain((DEPTH, MLA_QK)),
        'g_khead': gain((DEPTH, MLA_QK)),
        'w_conv_ssm': normal((DEPTH, SSM_CONV, SSM_XBC), SSM_CONV ** -0.5),
        'b_conv_ssm': normal((DEPTH, SSM_XBC), 0.02),
        'a_log': jnp.log(a0),
        'dt_bias': dt0 + jnp.log(-jnp.expm1(-dt0)),
        'd_skip_ssm': gain((DEPTH, SSM_HEADS)),
        'g_ssm_out': gain((DEPTH, SSM_INNER)),
        'w_conv_hy': normal((DEPTH, HY_CONV, (HY_ORDER + 1) * HY_CH), HY_CONV ** -0.5),
        'b_conv_hy': normal((DEPTH, (HY_ORDER + 1) * HY_CH), 0.02),
        'w_f1': normal((DEPTH, HY_EMB, HY_HIDDEN), HY_EMB ** -0.5),
        'b_f1': normal((DEPTH, HY_HIDDEN), 0.02),
        'freq_f1': gain((DEPTH, HY_HIDDEN)),
        'w_f2': normal((DEPTH, HY_HIDDEN, HY_HIDDEN), HY_HIDDEN ** -0.5),
        'b_f2': normal((DEPTH, HY_HIDDEN), 0.02),
        'freq_f2': gain((DEPTH, HY_HIDDEN)),
        'w_f3': normal((DEPTH, HY_HIDDEN, 2 * HY_ORDER * HY_CH), HY_HIDDEN ** -0.5),
        'd_skip_hy': normal((DEPTH, HY_ORDER, HY_CH), 1.0),
        'w_ff1': normal((DEPTH, D_MODEL, D_FF), D_MODEL ** -0.5),
        'w_ff2': normal((DEPTH, D_FF, D_MODEL), D_FF ** -0.5),
    }


def reference(x, c, ctx, c_ctx, w_mod, b_mod, g_norm_mix, g_norm_mlp, w_in, w_out,
              g_cq, g_ckv, w_uq, w_ukv, g_qhead, g_khead,
              w_conv_ssm, b_conv_ssm, a_log, dt_bias, d_skip_ssm, g_ssm_out,
              w_conv_hy, b_conv_hy, w_f1, b_f1, freq_f1, w_f2, b_f2, freq_f2, w_f3, d_skip_hy,
              w_ff1, w_ff2):
    bsz, seq, _ = x.shape
    rope_l = axial_rope_tables(seq)
    xl, xc = x, ctx
    for i in range(DEPTH):
        last = i == DEPTH - 1
        mod_l = jax.nn.silu(c) @ w_mod[i] + b_mod[i]
        mod_c = jax.nn.silu(c_ctx) @ w_mod[i] + b_mod[i]
        sh1_l, sc1_l, ga1_l, sh2_l, sc2_l, ga2_l = jnp.split(mod_l[:, None, :], 6, axis=-1)
        sh1_c, sc1_c, ga1_c, sh2_c, sc2_c, ga2_c = jnp.split(mod_c, 6, axis=-1)

        hl = modulate(rms_norm(xl, g_norm_mix[i]), sh1_l, sc1_l)
        hc = modulate(rms_norm(xc, g_norm_mix[i]), sh1_c, sc1_c)
        cq_l, ckv_l, kr_l, z_l, xbc_l, dt_l, hy_l = jnp.split(hl @ w_in[i], IN_OFFSETS, axis=-1)
        cq_c, ckv_c, kr_c, z_c, xbc_c, dt_c, hy_c = jnp.split(hc @ w_in[i], IN_OFFSETS, axis=-1)

        k_c, v_c = mla_keyval(ckv_c, kr_c, g_ckv[i], w_ukv[i], g_khead[i], None)
        k_l, v_l = mla_keyval(ckv_l, kr_l, g_ckv[i], w_ukv[i], g_khead[i], rope_l)
        q_l = mla_query(cq_l, g_cq[i], w_uq[i], g_qhead[i], rope_l)
        att_l = block_attention(q_l, jnp.concatenate([k_c, k_l], axis=1), jnp.concatenate([v_c, v_l], axis=1))

        a = -jnp.exp(a_log[i].astype(jnp.float32))
        xs_c, bs_c, cs_c, dts_c = ssm_prepare(xbc_c, dt_c, w_conv_ssm[i], b_conv_ssm[i], dt_bias[i])
        xs_l, bs_l, cs_l, dts_l = ssm_prepare(xbc_l, dt_l, w_conv_ssm[i], b_conv_ssm[i], dt_bias[i])
        zero = jnp.zeros((bsz, SSM_HEADS, SSM_HEAD_DIM, SSM_STATE), jnp.float32)
        y_c, s_fwd, s_bwd = bidir_ssd(xs_c, bs_c, cs_c, dts_c, a, zero, zero)
        y_l, _, _ = bidir_ssd(xs_l, bs_l, cs_l, dts_l, a, s_fwd, s_bwd)
        ssm_l = ssm_out(y_l, xs_l, z_l, d_skip_ssm[i], g_ssm_out[i])

        hyp = (w_conv_hy[i], b_conv_hy[i], w_f1[i], b_f1[i], freq_f1[i], w_f2[i], b_f2[i], freq_f2[i], w_f3[i], d_skip_hy[i])
        hyo_l = hyena_mixer(hy_l, *hyp)

        mix_l = jnp.concatenate([att_l, ssm_l.astype(xl.dtype), hyo_l.astype(xl.dtype)], axis=-1) @ w_out[i]
        xl = xl + ga1_l * mix_l
        xl = xl + ga2_l * sq_relu_mlp(modulate(rms_norm(xl, g_norm_mlp[i]), sh2_l, sc2_l), w_ff1[i], w_ff2[i])

        if not last:
            q_c = mla_query(cq_c, g_cq[i], w_uq[i], g_qhead[i], None)
            att_c = block_attention(q_c, k_c, v_c)
            ssm_c = ssm_out(y_c, xs_c, z_c, d_skip_ssm[i], g_ssm_out[i])
            hyo_c = hyena_mixer(hy_c, *hyp)
            mix_c = jnp.concatenate([att_c, ssm_c.astype(xc.dtype), hyo_c.astype(xc.dtype)], axis=-1) @ w_out[i]
            xc = xc + ga1_c * mix_c
            xc = xc + ga2_c * sq_relu_mlp(modulate(rms_norm(xc, g_norm_mlp[i]), sh2_c, sc2_c), w_ff1[i], w_ff2[i])
    return xl
```

```python
import math
import numpy as np
import ml_dtypes
import concourse.bass as bass
import concourse.mybir as mybir
from concourse.bass_utils import run_bass_kernel_spmd

F32 = mybir.dt.float32
BF16 = mybir.dt.bfloat16
AF = mybir.ActivationFunctionType
ALU = mybir.AluOpType
AX = mybir.AxisListType

D = 1024
SEQ = 4096
CTX = 256
T = SEQ + CTX
NCH = T // 128
DEPTH = 2
EPS = 1e-6
NH = 6
QK = 96
IN_COLS = 2220
O_CQ, O_CKV, O_KR, O_Z, O_XBC, O_DT, O_HY = 0, 256, 384, 416, 800, 1440, 1452
SSM_INNER = 384
HY_CH = 256
DFF = 4096


class FW:
    NDSEM = 48

    def __init__(self):
        self.nc = bass.Bass("TRN2", target_bir_lowering=False)
        nc = self.nc
        self.eng = {"pe": nc.tensor, "act": nc.scalar, "dve": nc.vector,
                    "pool": nc.gpsimd, "sp": nc.sync}
        self._ctx = []
        self.sems = {}
        for e in self.eng:
            self.sems[e] = self._enter(nc.semaphore("s_" + e))
        self.dq = {"sp": 24, "pool": 24, "act": 8}
        self.dsems = {}
        self.dcnt = {}
        self.dnext = {q: 0 for q in self.dq}
        for q, n in self.dq.items():
            for i in range(n):
                self.dsems[(q, i)] = self._enter(nc.semaphore(f"d_{q}{i}"))
                self.dcnt[(q, i)] = 0
        self.seq = {e: 0 for e in self.eng}
        self.waited = {e: {} for e in self.eng}
        self.lastw = {}
        self.readers = {}
        self.n_inst = 0
        self.n_wait = 0
        self._rr = 0
        self.psum_keys = set()

    def _enter(self, cm):
        v = cm.__enter__()
        self._ctx.append(cm)
        return v

    def mark(self):
        return len(self._ctx)

    def release(self, mark):
        self.barrier()
        while len(self._ctx) > mark:
            self._ctx.pop().__exit__(None, None, None)
        self.lastw = {k: v for k, v in self.lastw.items() if k.startswith("D:")}
        self.readers = {k: v for k, v in self.readers.items() if k.startswith("D:")}

    def close(self):
        while self._ctx:
            self._ctx.pop().__exit__(None, None, None)

    def sbuf(self, name, shape, dt=F32):
        self._uid = getattr(self, "_uid", 0) + 1
        return self._enter(self.nc.sbuf_tensor(f"sb{self._uid}_{name}", list(shape), dt))

    def psum(self, name, shape, dt=F32):
        self._uid = getattr(self, "_uid", 0) + 1
        full = 512 if dt == F32 else 1024
        t = self._enter(self.nc.psum_tensor(f"ps{self._uid}_{name}", [128, full], dt))
        self.psum_keys.add(name)
        shape = list(shape)
        n = 1
        for d in shape[1:]:
            n *= d
        assert n <= full, (name, shape)
        v = t[0:shape[0], 0:n]
        if len(shape) == 3:
            v = v.rearrange("p (a b) -> p a b", b=shape[2])
        return v

    def _sem(self, key):
        return self.sems[key] if isinstance(key, str) else self.dsems[key]

    def _deps(self, reads, writes):
        deps = {}

        def add(k, v):
            if deps.get(k, 0) < v:
                deps[k] = v
        for r in reads:
            d = self.lastw.get(r)
            if d is not None:
                add(*d)
        for wk in writes:
            d = self.lastw.get(wk)
            if d is not None:
                add(*d)
            for rk, rv in self.readers.get(wk, {}).items():
                add(rk, rv)
        return deps

    def _emit_waits(self, e, deps, skip_self=False):
        w = self.waited[e]
        for k, v in deps.items():
            if skip_self and k == e:
                continue
            if w.get(k, 0) >= v:
                continue
            self.eng[e].wait_ge(self._sem(k), v)
            w[k] = v
            self.n_wait += 1

    def _record(self, dep, reads, writes):
        k, v = dep
        for r in reads:
            self.readers.setdefault(r, {})[k] = v
        for wk in writes:
            self.lastw[wk] = dep
            self.readers[wk] = {}

    def op(self, e, fn, reads=(), writes=(), pe_acc=False):
        if e == "any":
            e = ("dve", "pool", "act")[self._rr % 3]
            self._rr += 1
        px = [r for r in reads if r in self.psum_keys and r not in writes]
        if px:
            reads = [r for r in reads if r not in px]
            writes = list(writes) + px
        deps = self._deps(reads, writes)
        self._emit_waits(e, deps, skip_self=(e == "pe" and pe_acc))
        ins = fn(self.eng[e])
        self.seq[e] += 1
        ins.then_inc(self.sems[e], 1)
        self._record((e, self.seq[e]), reads, writes)
        self.n_inst += 1
        return ins

    def dma(self, out, in_, reads=(), writes=(), q="sp", **kw):
        deps = self._deps(reads, writes)
        i = (q, self.dnext[q])
        self.dnext[q] = (self.dnext[q] + 1) % self.dq[q]
        if self.dcnt[i] > 0:
            deps[i] = max(deps.get(i, 0), 16 * self.dcnt[i])
        self._emit_waits(q, deps)
        ins = self.eng[q].dma_start(out=out, in_=in_, **kw)
        self.dcnt[i] += 1
        ins.then_inc(self.dsems[i], 16)
        self._record((i, 16 * self.dcnt[i]), reads, writes)
        self.n_inst += 1
        return ins

    def barrier(self):
        deps = {e: self.seq[e] for e in self.eng if self.seq[e] > 0}
        for i, c in self.dcnt.items():
            if c > 0:
                deps[i] = 16 * c
        for e in self.eng:
            self._emit_waits(e, deps)

    def finish(self, keys, e="sp"):
        self._emit_waits(e, self._deps(keys, ()))


def _consts():
    c = {}
    c["ident_f"] = np.eye(128, dtype=np.float32)
    c["ones_f"] = np.ones((128, 128), np.float32)
    cos = np.ones((96, T), np.float32); sin = np.zeros((96, T), np.float32)
    t = np.arange(SEQ)
    pos = [(t // 64).astype(np.float32), (t % 64).astype(np.float32)]
    inv = (10000.0 ** (-np.arange(8, dtype=np.float32) / 8)).astype(np.float32)
    RT = np.zeros((96, 96), np.float32)
    for a in range(2):
        ang = pos[a][None, :] * inv[:, None]
        for b in range(2):
            for f in range(8):
                r = 64 + 16 * a + 8 * b + f
                cos[r, CTX:] = np.cos(ang[f]); sin[r, CTX:] = np.sin(ang[f])
        for f in range(8):
            i1 = 64 + 16 * a + f; i2 = i1 + 8
            RT[i2, i1] = -1.0
            RT[i1, i2] = 1.0
    c["rope_cos"] = cos; c["rope_sin"] = sin; c["rope_RT"] = RT
    pk = np.zeros((32, 96), np.float32); pk[np.arange(32), 64 + np.arange(32)] = 1.0
    c["pk_sel"] = pk
    i = np.arange(128)
    triF = (i[:, None] <= i[None, :]).astype(np.float32)
    c["tri"] = np.stack([triF, triF.T.copy()])
    NEGV = -1.0e30
    negF = np.where(i[:, None] <= i[None, :], 0.0, NEGV).astype(np.float32)
    negB = np.where(i[:, None] >= i[None, :], 0.0, NEGV).astype(np.float32)
    c["negmask"] = np.stack([negF, negB])
    gm = np.zeros((3, 3, 128, 128), np.float32)
    for r in range(3):
        for r2 in range(3):
            ga = (r * 128 + i) // 192; gb = (r2 * 128 + i) // 192
            gm[r, r2] = (ga[:, None] == gb[None, :]).astype(np.float32)
    c["gmask"] = gm.reshape(9, 128, 128)
    c["jrev"] = np.ascontiguousarray(np.eye(128, dtype=np.float32)[::-1])
    deltas = np.abs(np.linspace(math.log(1e-2) / 1.5, math.log(1e-2) / 0.3, HY_CH, dtype=np.float32))
    c["hy_ndelta"] = np.ascontiguousarray((-deltas).reshape(2, 128).T.astype(np.float32))
    for L in (CTX, SEQ):
        t01 = np.linspace(0.0, 1.0, L, dtype=np.float32)
        w = (np.float32(2.0 * math.pi / L) * np.arange(L, dtype=np.float32))
        bands = np.linspace(1e-4, 15, 16, dtype=np.float32)
        ang = (bands[None, :] * w[:, None]).astype(np.float32)
        feats = np.concatenate([t01[:, None], np.cos(ang), -np.sin(ang)], axis=1).astype(np.float32)
        fT = feats.T
        c[f"hy_feats_{L}"] = np.ascontiguousarray(np.stack([fT, fT[:, ::-1]]))
        t01r = np.broadcast_to(t01[None, :], (128, L))
        c[f"hy_t01_{L}"] = np.ascontiguousarray(np.stack([t01r, t01r[:, ::-1]]))
    return c


CONST_SPECS = {"ident_f": ([128, 128], F32), "ones_f": ([128, 128], F32)}


class Prog:
    def __init__(self, debug=()):
        self.fw = FW()
        self.nc = self.fw.nc
        self.debug = set(debug)
        self.din = {}
        self.dscr = {}

    def inp(self, name, shape, dt=F32):
        self.din[name] = self.nc.dram_tensor(name, list(shape), dt, kind="ExternalInput").ap()
        return self.din[name]

    def scratch(self, name, shape, dt=F32, out=False):
        kind = "ExternalOutput" if (out or name in self.debug) else "Internal"
        self.dscr[name] = self.nc.dram_tensor(name, list(shape), dt, kind=kind).ap()
        return self.dscr[name]


DSTOP = [None]
ENABLE_SSD = [True]
ENABLE_HYENA = [True]
DFLAG = [0]
SKIP_C = [False]


def build(debug=(), stop_after=None):
    P = Prog(debug)
    fw, nc = P.fw, P.nc
    op, dma = fw.op, fw.dma

    xcat = P.inp("xcat", [T, D])
    cvec = P.inp("cvec", [2, D])
    w_mod = P.inp("w_mod", [DEPTH, D, 6 * D])
    b_mod = P.inp("b_mod", [DEPTH, 6 * D])
    g_norm_mix = P.inp("g_norm_mix", [DEPTH, D])
    g_norm_mlp = P.inp("g_norm_mlp", [DEPTH, D])
    w_in = P.inp("w_in", [DEPTH, D, IN_COLS])
    cst = {k: P.inp(k, s, d) for k, (s, d) in CONST_SPECS.items()}

    XR = P.scratch("XR", [T, D])
    MOD = P.scratch("MOD", [2, 6, 128, D])
    PT = P.scratch("PT", [2224, T])
    DTRAW = P.scratch("DTRAW", [T, 12])
    OUT = P.scratch("out", [SEQ, D], out=True)

    ident_f = fw.sbuf("ident_f", [128, 128])
    ident_b = fw.sbuf("ident_b", [128, 128], BF16)
    ones_f = fw.sbuf("ones_f", [128, 128])
    dma(ident_f[:], cst["ident_f"], writes=["ident_f"])
    dma(ones_f[:], cst["ones_f"], writes=["ones_f"])
    op("dve", lambda e: e.tensor_copy(out=ident_b[:], in_=ident_f[:]), reads=["ident_f"], writes=["ident_b"])

    for ci in range(0, NCH, 2):
        dma(XR[ci * 128:(ci + 2) * 128, :], xcat[ci * 128:(ci + 2) * 128, :],
            writes=[f"D:XR{ci}", f"D:XR{ci + 1}"], q="pool")

    for li in range(DEPTH):
        mk = fw.mark()
        cT = fw.sbuf("cT", [128, 2, 8])
        sT = fw.sbuf("sT", [128, 2, 8])
        srep = fw.sbuf("srep", [128, 2, 8, 128])
        modsb = [fw.sbuf(f"modsb{j}", [128, 6 * D]) for j in range(2)]
        brow = fw.sbuf("brow", [1, 6 * D])
        grow = fw.sbuf("grow", [1, 2 * D])
        grep = fw.sbuf("grep", [128, 2 * D])
        wst = [fw.sbuf(f"wst{s}", [128, 8, 512]) for s in range(2)]
        psA = [fw.psum(f"psA{s}", [128, 512]) for s in range(2)]

        dma(cT[:], cvec.rearrange("j (k p) -> p j k", p=128), writes=["cT"],
            allow_slow_non_contiguous=True)
        dma(brow[:], b_mod[li:li + 1, :], writes=["brow"])
        dma(grow[:, 0:D], g_norm_mix[li:li + 1, :], writes=["grow"])
        dma(grow[:, D:2 * D], g_norm_mlp[li:li + 1, :], writes=["grow"])
        op("act", lambda e: e.activation(out=sT[:], in_=cT[:], func=AF.Silu), reads=["cT"], writes=["sT"])
        for j in range(2):
            for k in range(8):
                op("dve", lambda e: e.tensor_copy(out=srep[:, j, k, :],
                                                  in_=sT[:, j, k:k + 1].to_broadcast([128, 128])),
                   reads=["sT"], writes=["srep"])
        for nb in range(4):
            s = nb % 2
            op("pe", lambda e: e.matmul(psA[s][:], lhsT=ones_f[0:1, :], rhs=grow[0:1, nb * 512:(nb + 1) * 512],
                                        start=True, stop=True),
               reads=["ones_f", "grow"], writes=[f"psA{s}"])
            op("dve", lambda e: e.tensor_copy(out=grep[:, nb * 512:(nb + 1) * 512], in_=psA[s][:]),
               reads=[f"psA{s}"], writes=["grep"])
        cnt = 0
        for nb in range(12):
            ws = nb % 2
            dma(wst[ws][:], w_mod[li, :, nb * 512:(nb + 1) * 512].rearrange("(k p) n -> p k n", p=128),
                writes=[f"wst{ws}"])
            for j in range(2):
                s = cnt % 2
                cnt += 1
                for k in range(8):
                    op("pe", lambda e: e.matmul(psA[s][:], lhsT=srep[:, j, k, :], rhs=wst[ws][:, k, :],
                                                start=(k == 0), stop=False),
                       reads=["srep", f"wst{ws}"], writes=[f"psA{s}"], pe_acc=(k > 0))
                op("pe", lambda e: e.matmul(psA[s][:], lhsT=ones_f[0:1, :], rhs=brow[0:1, nb * 512:(nb + 1) * 512],
                                            start=False, stop=True),
                   reads=["ones_f", "brow"], writes=[f"psA{s}"], pe_acc=True)
                op("act" if j == 0 else "dve",
                   (lambda e: e.copy(out=modsb[j][:, nb * 512:(nb + 1) * 512], in_=psA[s][:])) if j == 0 else
                   (lambda e: e.tensor_copy(out=modsb[j][:, nb * 512:(nb + 1) * 512], in_=psA[s][:])),
                   reads=[f"psA{s}"], writes=[f"modsb{j}"])
        for j in range(2):
            m = modsb[j]
            op("dve", lambda e: e.scalar_tensor_tensor(out=m[:, D:2 * D], in0=m[:, D:2 * D], scalar=1.0,
                                                       in1=grep[:, 0:D], op0=ALU.add, op1=ALU.mult),
               reads=[f"modsb{j}", "grep"], writes=[f"modsb{j}"])
            op("dve", lambda e: e.scalar_tensor_tensor(out=m[:, 4 * D:5 * D], in0=m[:, 4 * D:5 * D], scalar=1.0,
                                                       in1=grep[:, D:2 * D], op0=ALU.add, op1=ALU.mult),
               reads=[f"modsb{j}", "grep"], writes=[f"modsb{j}"])
            for dst, src in ((0, 1), (1, 0), (2, 2), (3, 4), (4, 3), (5, 5)):
                dma(MOD[j, dst], m[:, src * D:(src + 1) * D], reads=[f"modsb{j}"], writes=[f"D:MOD{j}_{dst}"],
                    q="pool")
        fw.release(mk)
        if stop_after == ("A", li):
            break

        mk = fw.mark()
        hT = fw.sbuf("hT", [128, 8, T], BF16)
        winb = fw.sbuf("winb", [128, 8, 2224], BF16)
        wstg = [fw.sbuf(f"wstg{s}", [128, 8, 555]) for s in range(2)]
        A1 = [fw.sbuf(f"A1_{j}", [128, D]) for j in range(2)]
        B1 = [fw.sbuf(f"B1_{j}", [128, D]) for j in range(2)]
        xt = [fw.sbuf(f"xt{s}", [128, D]) for s in range(2)]
        tmp = [fw.sbuf(f"tmp{s}", [128, D]) for s in range(2)]
        hb = [fw.sbuf(f"hb{s}", [128, D], BF16) for s in range(2)]
        junk = fw.sbuf("junk", [128, D], BF16)
        ss = fw.sbuf("ss", [128, NCH])
        rstd = fw.sbuf("rstd", [128, NCH])
        dtst = fw.sbuf("dtst", [128, NCH, 12])
        osb = [fw.sbuf(f"osb{s}", [128, 512]) for s in range(3)]
        pst = [fw.psum(f"pst{s}", [128, 8, 128], BF16) for s in range(2)]
        psd = [fw.psum(f"psd{s}", [128, 16]) for s in range(2)]
        pso = [fw.psum(f"pso{s}", [128, 512]) for s in range(3)]

        for j in range(2):
            dma(A1[j][:], MOD[j, 0], reads=[f"D:MOD{j}_0"], writes=[f"A1_{j}"])
            dma(B1[j][:], MOD[j, 1], reads=[f"D:MOD{j}_1"], writes=[f"B1_{j}"])
        for q4 in range(4):
            s = q4 % 2
            dma(wstg[s][:], w_in[li, :, q4 * 555:(q4 + 1) * 555].rearrange("(k p) n -> p k n", p=128),
                writes=[f"wstg{s}"], q="pool")
            op("any", lambda e: (e.copy if e is nc.scalar else e.tensor_copy)(
                out=winb[:, :, q4 * 555:(q4 + 1) * 555], in_=wstg[s][:]),
               reads=[f"wstg{s}"], writes=["winb"])

        for ci in range(NCH):
            s = ci % 2
            kind = 1 if ci < 2 else 0
            dma(xt[s][:], XR[ci * 128:(ci + 1) * 128, :], reads=[f"D:XR{ci}"], writes=[f"xt{s}"])
            op("act", lambda e: e.activation(out=junk[:], in_=xt[s][:], func=AF.Square,
                                             accum_out=ss[:, ci:ci + 1]),
               reads=[f"xt{s}"], writes=["junk", f"ss{ci}"])
            op("dve", lambda e: e.tensor_scalar(out=rstd[:, ci:ci + 1], in0=ss[:, ci:ci + 1],
                                                scalar1=1.0 / D, scalar2=EPS, op0=ALU.mult, op1=ALU.add),
               reads=[f"ss{ci}"], writes=[f"rstd{ci}"])
            op("act", lambda e: e.sqrt(out=rstd[:, ci:ci + 1], in_=rstd[:, ci:ci + 1]),
               reads=[f"rstd{ci}"], writes=[f"rstd{ci}"])
            op("dve", lambda e: e.reciprocal(out=rstd[:, ci:ci + 1], in_=rstd[:, ci:ci + 1]),
               reads=[f"rstd{ci}"], writes=[f"rstd{ci}"])
            op("dve", lambda e: e.scalar_tensor_tensor(out=tmp[s][:], in0=xt[s][:], scalar=rstd[:, ci:ci + 1],
                                                       in1=A1[kind][:], op0=ALU.mult, op1=ALU.mult),
               reads=[f"xt{s}", f"rstd{ci}", f"A1_{kind}"], writes=[f"tmp{s}"])
            op("pool", lambda e: e.tensor_tensor(out=hb[s][:], in0=tmp[s][:], in1=B1[kind][:], op=ALU.add),
               reads=[f"tmp{s}", f"B1_{kind}"], writes=[f"hb{s}"])
            for k in range(8):
                op("pe", lambda e: e.transpose(out=pst[s][:, k, :], in_=hb[s][:, k * 128:(k + 1) * 128],
                                               identity=ident_b[:]),
                   reads=[f"hb{s}", "ident_b"], writes=[f"pst{s}"], pe_acc=(k > 0))
            op("act", lambda e: e.copy(out=hT[:, :, ci * 128:(ci + 1) * 128], in_=pst[s][:]),
               reads=[f"pst{s}"], writes=[f"hT{ci}"])
            for k in range(8):
                op("pe", lambda e: e.matmul(psd[s][:, 0:12], lhsT=hT[:, k, ci * 128:(ci + 1) * 128],
                                            rhs=winb[:, k, O_DT:O_DT + 12], start=(k == 0), stop=(k == 7)),
                   reads=[f"hT{ci}", "winb"], writes=[f"psd{s}"], pe_acc=(k > 0))
            op("dve", lambda e: e.tensor_copy(out=dtst[:, ci, :], in_=psd[s][:, 0:12]),
               reads=[f"psd{s}"], writes=["dtst"])
        dma(DTRAW.rearrange("(c p) n -> p c n", p=128), dtst[:], reads=["dtst"], writes=["D:DTRAW"], q="pool")

        col_tiles = [(0, 128), (128, 128), (256, 128), (384, 32)]
        col_tiles += [(O_Z + 128 * i, 128) for i in range(3)]
        col_tiles += [(O_XBC + 128 * i, 128) for i in range(5)]
        col_tiles += [(O_HY + 128 * i, 128) for i in range(6)]
        tblocks = [(t0, min(512, T - t0)) for t0 in range(0, T, 512)]
        cnt = 0
        for (c0, cw) in col_tiles:
            for (t0, tn) in tblocks:
                s = cnt % 3
                cnt += 1
                rk = [f"hT{ci}" for ci in range(t0 // 128, (t0 + tn) // 128)] + ["winb"]
                for k in range(8):
                    op("pe", lambda e: e.matmul(pso[s][0:cw, 0:tn], lhsT=winb[:, k, c0:c0 + cw],
                                                rhs=hT[:, k, t0:t0 + tn], start=(k == 0), stop=(k == 7)),
                       reads=rk, writes=[f"pso{s}"], pe_acc=(k > 0))
                if cnt % 2:
                    op("act", lambda e: e.copy(out=osb[s][0:cw, 0:tn], in_=pso[s][0:cw, 0:tn]),
                       reads=[f"pso{s}"], writes=[f"osb{s}"])
                else:
                    op("dve", lambda e: e.tensor_copy(out=osb[s][0:cw, 0:tn], in_=pso[s][0:cw, 0:tn]),
                       reads=[f"pso{s}"], writes=[f"osb{s}"])
                dma(PT[c0:c0 + cw, t0:t0 + tn], osb[s][0:cw, 0:tn], reads=[f"osb{s}"],
                    writes=[f"D:PT{c0}_{t0}"], q="pool" if cnt % 2 else "sp")
        fw.release(mk)
        if stop_after == ("B", li):
            break
        G = dict(P=P, fw=fw, nc=nc, li=li, XR=XR, MOD=MOD, PT=PT, DTRAW=DTRAW, OUT=OUT,
                 ident_f=ident_f, ident_b=ident_b, ones_f=ones_f, cst=cst, dstop=DSTOP[0])
        if not SKIP_C[0]:
            phase_C(G)
        if stop_after == ("C", li):
            break
        if ENABLE_SSD[0]:
            phase_D(G)
        else:
            _zero_fill(G, "SSMT", SSM_INNER)
        if stop_after == ("D", li):
            break
        if ENABLE_HYENA[0]:
            phase_E(G)
        else:
            _zero_fill(G, "HYT", HY_CH)
        if stop_after == ("E", li):
            break
        phase_F(G)
        if stop_after == ("F", li):
            break

    fw.barrier()
    fw.close()
    return P


def make_inputs(inputs, core):
    b = core % 4
    m = {}
    m["xcat"] = np.ascontiguousarray(np.concatenate([inputs["ctx"][b], inputs["x"][b]], axis=0))
    m["cvec"] = np.ascontiguousarray(np.stack([inputs["c"][b], inputs["c_ctx"]], axis=0))
    for k in ("w_mod", "b_mod", "g_norm_mix", "g_norm_mlp", "w_in", "w_out", "g_cq", "g_ckv", "w_uq", "w_ukv",
              "g_qhead", "g_khead", "w_conv_ssm", "b_conv_ssm", "a_log", "dt_bias", "d_skip_ssm", "g_ssm_out",
              "w_conv_hy", "b_conv_hy", "w_f1", "b_f1", "freq_f1", "w_f2", "b_f2", "freq_f2", "w_f3",
              "d_skip_hy", "w_ff1", "w_ff2"):
        m[k] = np.ascontiguousarray(inputs[k])
    m["a_log"] = m["a_log"].reshape(DEPTH, 12); m["dt_bias"] = m["dt_bias"].reshape(DEPTH, 12)
    m["hy_vecs"] = np.ascontiguousarray(np.stack([inputs["b_f1"], inputs["freq_f1"], inputs["b_f2"], inputs["freq_f2"]], axis=1))
    m["d_skip_col"] = np.ascontiguousarray(np.repeat(inputs["d_skip_ssm"], 64, axis=1))
    m.update(_consts())
    return m


def kernel(**inputs):
    inputs = {k: np.asarray(v) for k, v in inputs.items()}
    P = build()
    in_maps = [make_inputs(inputs, c) for c in range(8)]
    in_maps = [{k: v for k, v in m.items() if k in P.din} for m in in_maps]
    res = run_bass_kernel_spmd(P.nc, in_maps, core_ids=list(range(8)))
    return np.stack([res.results[b]["out"] for b in range(4)], axis=0).astype(np.float32)


def _inp(P, name, shape, dt=F32):
    if name in P.din:
        return P.din[name]
    return P.inp(name, shape, dt)


def _scr(P, name, shape, dt=F32):
    if name in P.dscr:
        return P.dscr[name]
    return P.scratch(name, shape, dt)


def _interleave(gens):
    gens = list(gens)
    while gens:
        for g in list(gens):
            try:
                next(g)
            except StopIteration:
                gens.remove(g)


def _rsqrt(fw, dst, src, scale, rk, wk, reads_extra=()):
    fw.op("dve", lambda e: e.tensor_scalar(out=dst, in0=src, scalar1=scale, scalar2=EPS,
                                           op0=ALU.mult, op1=ALU.add), reads=[rk] + list(reads_extra), writes=[wk])
    fw.op("act", lambda e: e.sqrt(out=dst, in_=dst), reads=[wk], writes=[wk])
    fw.op("dve", lambda e: e.reciprocal(out=dst, in_=dst), reads=[wk], writes=[wk])


def _load_cast(fw, dst_bf, src_ap, stg, key_dst, key_stg, q="sp"):
    fw.dma(stg, src_ap, writes=[key_stg], q=q)
    fw.op("any", lambda e: (e.copy if e is fw.nc.scalar else e.tensor_copy)(out=dst_bf, in_=stg),
          reads=[key_stg], writes=[key_dst])


def phase_C(G):
    P, fw, nc, li = G["P"], G["fw"], G["nc"], G["li"]
    op, dma = fw.op, fw.dma
    PT = G["PT"]
    ones_f = G["ones_f"]
    last = li == DEPTH - 1
    g_cq = _inp(P, "g_cq", [DEPTH, 256]); g_ckv = _inp(P, "g_ckv", [DEPTH, 128])
    w_uq = _inp(P, "w_uq", [DEPTH, 256, 576]); w_ukv = _inp(P, "w_ukv", [DEPTH, 128, 768])
    g_qh = _inp(P, "g_qhead", [DEPTH, 96]); g_kh = _inp(P, "g_khead", [DEPTH, 96])
    COS = _inp(P, "rope_cos", [96, T]); SIN = _inp(P, "rope_sin", [96, T])
    RTd = _inp(P, "rope_RT", [96, 96]); PKd = _inp(P, "pk_sel", [32, 96])
    ATT = _scr(P, "ATT", [NH, 64, T], BF16)
    scale = 1.0 / math.sqrt(QK)

    mk0 = fw.mark()
    qT = fw.sbuf("qT", [96, NH, T], BF16)
    kT = fw.sbuf("kT", [96, NH, T], BF16)
    V1 = fw.sbuf("V1", [128, NCH, NH, 65], BF16)
    op("pool", lambda e: e.memset(V1[:], 1.0), writes=["V1"])

    mk = fw.mark()
    wuq_b = fw.sbuf("wuq_b", [128, 2, 576], BF16)
    wk_b = fw.sbuf("wk_b", [128, NH, 96], BF16)
    wv_b = fw.sbuf("wv_b", [128, NH, 64], BF16)
    pk_b = fw.sbuf("pk_b", [32, 96], BF16)
    stg = fw.sbuf("stg", [128, 2, 768])
    RT = fw.sbuf("RT", [96, 96])
    gcq = fw.sbuf("gcq", [128, 2]); gckv = fw.sbuf("gckv", [128, 1])
    gq = fw.sbuf("gq", [96, 1]); gk = fw.sbuf("gk", [96, 1])
    dma(stg[:, :, 0:576], w_uq[li].rearrange("(k p) n -> p k n", p=128), writes=["stg"])
    op("dve", lambda e: e.tensor_copy(out=wuq_b[:], in_=stg[:, :, 0:576]), reads=["stg"], writes=["wuq_b"])
    dma(stg[:, 0, :], w_ukv[li], reads=[], writes=["stg"])
    op("pool", lambda e: e.memset(wk_b[:], 0.0), writes=["wk_b"])
    op("dve", lambda e: e.tensor_copy(out=wk_b[:, :, 0:64],
                                      in_=stg[:, 0, :].rearrange("p (h c) -> p h c", c=128)[:, :, 0:64]),
       reads=["stg"], writes=["wk_b"])
    op("dve", lambda e: e.tensor_copy(out=wv_b[:],
                                      in_=stg[:, 0, :].rearrange("p (h c) -> p h c", c=128)[:, :, 64:128]),
       reads=["stg"], writes=["wv_b"])
    dma(stg[0:32, 1, 0:96], PKd, writes=["stg1"])
    op("dve", lambda e: e.tensor_copy(out=pk_b[:], in_=stg[0:32, 1, 0:96]), reads=["stg1"], writes=["pk_b"])
    dma(RT[:], RTd, writes=["RT"])
    dma(gcq[:], g_cq[li].rearrange("(k p) -> p k", p=128), writes=["gcq"], allow_slow_non_contiguous=True)
    dma(gckv[:], g_ckv[li].rearrange("(p o) -> p o", o=1), writes=["gckv"], allow_slow_non_contiguous=True)
    dma(gq[:], g_qh[li].rearrange("(p o) -> p o", o=1), writes=["gq"], allow_slow_non_contiguous=True)
    dma(gk[:], g_kh[li].rearrange("(p o) -> p o", o=1), writes=["gk"], allow_slow_non_contiguous=True)

    cqb = [fw.sbuf(f"cqb{s}", [128, 2, 512]) for s in range(2)]
    ckb = [fw.sbuf(f"ckb{s}", [128, 512]) for s in range(2)]
    krb = [fw.sbuf(f"krb{s}", [32, 512]) for s in range(2)]
    krb_b = fw.sbuf("krb_b", [32, 512], BF16)
    sq = fw.sbuf("sq", [128, 2, 512])
    rs = fw.sbuf("rs", [128, 512])
    cqn = fw.sbuf("cqn", [128, 2, 512], BF16)
    ckn = fw.sbuf("ckn", [128, 512], BF16)
    cosb = [fw.sbuf(f"cosb{s}", [96, 512]) for s in range(2)]
    sinb = [fw.sbuf(f"sinb{s}", [96, 512]) for s in range(2)]
    hsq = [fw.sbuf(f"hsq{s}", [96, 512]) for s in range(2)]
    hrs = [fw.sbuf(f"hrs{s}", [96, 512]) for s in range(2)]
    hn = [fw.sbuf(f"hn{s}", [96, 512]) for s in range(2)]
    ht1 = [fw.sbuf(f"ht1{s}", [96, 512]) for s in range(2)]
    ht2 = [fw.sbuf(f"ht2{s}", [96, 512]) for s in range(2)]
    ps1 = fw.psum("ps1", [128, 512])
    psh = [fw.psum(f"psh{s}", [96, 512]) for s in range(2)]
    ps2 = [fw.psum(f"ps2{s}", [96, 512]) for s in range(2)]
    psr = [fw.psum(f"psr{s}", [96, 512]) for s in range(2)]
    psv = fw.psum("psv", [128, 384])

    def head_chain(kind, h, s, t0, tn, bs, bi):
        pk = f"psh{s}"
        if kind == "q":
            for k in range(2):
                op("pe", lambda e: e.matmul(psh[s][:, 0:tn], lhsT=wuq_b[:, k, h * 96:(h + 1) * 96],
                                            rhs=cqn[:, k, 0:tn], start=(k == 0), stop=(k == 1)),
                   reads=["cqn", "wuq_b"], writes=[pk], pe_acc=(k > 0))
            gcol, gkey, dst, dkey = gq, "gq", qT[:, h, t0:t0 + tn], f"qT{h}_{bi}"
        else:
            op("pe", lambda e: e.matmul(psh[s][:, 0:tn], lhsT=wk_b[:, h, :], rhs=ckn[:, 0:tn], start=True, stop=False),
               reads=["ckn", "wk_b"], writes=[pk])
            op("pe", lambda e: e.matmul(psh[s][:, 0:tn], lhsT=pk_b[:], rhs=krb_b[:, 0:tn], start=False, stop=True),
               reads=["krb_b", "pk_b"], writes=[pk], pe_acc=True)
            gcol, gkey, dst, dkey = gk, "gk", kT[:, h, t0:t0 + tn], f"kT{h}_{bi}"
        yield
        op("act", lambda e: e.activation(out=hsq[s][:, 0:tn], in_=psh[s][:, 0:tn], func=AF.Square),
           reads=[pk], writes=[f"hsq{s}"])
        yield
        op("pe", lambda e: e.matmul(ps2[s][:, 0:tn], lhsT=ones_f[0:96, 0:96], rhs=hsq[s][:, 0:tn],
                                    start=True, stop=True), reads=[f"hsq{s}", "ones_f"], writes=[f"ps2{s}"])
        yield
        op("dve", lambda e: e.tensor_scalar(out=hrs[s][:, 0:tn], in0=ps2[s][:, 0:tn], scalar1=1.0 / QK, scalar2=EPS,
                                            op0=ALU.mult, op1=ALU.add), reads=[f"ps2{s}"], writes=[f"hrs{s}"])
        yield
        op("act", lambda e: e.sqrt(out=hrs[s][:, 0:tn], in_=hrs[s][:, 0:tn]), reads=[f"hrs{s}"], writes=[f"hrs{s}"])
        yield
        op("dve", lambda e: e.reciprocal(out=hrs[s][:, 0:tn], in_=hrs[s][:, 0:tn]), reads=[f"hrs{s}"], writes=[f"hrs{s}"])
        yield
        op("dve", lambda e: e.scalar_tensor_tensor(out=hn[s][:, 0:tn], in0=psh[s][:, 0:tn], scalar=gcol[:, 0:1],
                                                   in1=hrs[s][:, 0:tn], op0=ALU.mult, op1=ALU.mult),
           reads=[pk, gkey, f"hrs{s}"], writes=[f"hn{s}"])
        yield
        op("pe", lambda e: e.matmul(psr[s][:, 0:tn], lhsT=RT[:], rhs=hn[s][:, 0:tn], start=True, stop=True),
           reads=[f"hn{s}", "RT"], writes=[f"psr{s}"])
        op("pool", lambda e: e.tensor_tensor(out=ht1[s][:, 0:tn], in0=hn[s][:, 0:tn], in1=cosb[bs][:, 0:tn],
                                             op=ALU.mult), reads=[f"hn{s}", f"cosb{bs}"], writes=[f"ht1{s}"])
        yield
        op("dve", lambda e: e.tensor_tensor(out=ht2[s][:, 0:tn], in0=psr[s][:, 0:tn], in1=sinb[bs][:, 0:tn],
                                            op=ALU.mult), reads=[f"psr{s}", f"sinb{bs}"], writes=[f"ht2{s}"])
        yield
        op("pool", lambda e: e.tensor_tensor(out=dst, in0=ht1[s][:, 0:tn], in1=ht2[s][:, 0:tn], op=ALU.add),
           reads=[f"ht1{s}", f"ht2{s}"], writes=[dkey])
        yield

    tblocks = [(t0, min(512, T - t0)) for t0 in range(0, T, 512)]
    for bi, (t0, tn) in enumerate(tblocks):
        bs = bi % 2
        dma(cqb[bs][:, :, 0:tn], PT[0:256, t0:t0 + tn].rearrange("(k p) t -> p k t", p=128),
            reads=[f"D:PT{c}_{t0}" for c in (0, 128)], writes=[f"cqb{bs}"])
        dma(ckb[bs][:, 0:tn], PT[256:384, t0:t0 + tn], reads=[f"D:PT256_{t0}"], writes=[f"ckb{bs}"])
        dma(krb[bs][:, 0:tn], PT[384:416, t0:t0 + tn], reads=[f"D:PT384_{t0}"], writes=[f"krb{bs}"])
        dma(cosb[bs][:, 0:tn], COS[:, t0:t0 + tn], writes=[f"cosb{bs}"], q="pool")
        dma(sinb[bs][:, 0:tn], SIN[:, t0:t0 + tn], writes=[f"sinb{bs}"], q="pool")
        op("act", lambda e: e.activation(out=sq[:, :, 0:tn], in_=cqb[bs][:, :, 0:tn], func=AF.Square),
           reads=[f"cqb{bs}"], writes=["sq"])
        for k in range(2):
            op("pe", lambda e: e.matmul(ps1[:, 0:tn], lhsT=ones_f[:], rhs=sq[:, k, 0:tn], start=(k == 0), stop=(k == 1)),
               reads=["sq", "ones_f"], writes=["ps1"], pe_acc=(k > 0))
        _rsqrt(fw, rs[:, 0:tn], ps1[:, 0:tn], 1.0 / 256, "ps1", "rs")
        for k in range(2):
            op("dve", lambda e: e.scalar_tensor_tensor(out=cqn[:, k, 0:tn], in0=cqb[bs][:, k, 0:tn],
                                                       scalar=gcq[:, k:k + 1], in1=rs[:, 0:tn],
                                                       op0=ALU.mult, op1=ALU.mult),
               reads=[f"cqb{bs}", "gcq", "rs"], writes=["cqn"])
        op("act", lambda e: e.activation(out=sq[:, 0, 0:tn], in_=ckb[bs][:, 0:tn], func=AF.Square),
           reads=[f"ckb{bs}"], writes=["sq"])
        op("pe", lambda e: e.matmul(ps1[:, 0:tn], lhsT=ones_f[:], rhs=sq[:, 0, 0:tn], start=True, stop=True),
           reads=["sq", "ones_f"], writes=["ps1"])
        _rsqrt(fw, rs[:, 0:tn], ps1[:, 0:tn], 1.0 / 128, "ps1", "rs")
        op("dve", lambda e: e.scalar_tensor_tensor(out=ckn[:, 0:tn], in0=ckb[bs][:, 0:tn], scalar=gckv[:, 0:1],
                                                   in1=rs[:, 0:tn], op0=ALU.mult, op1=ALU.mult),
           reads=[f"ckb{bs}", "gckv", "rs"], writes=["ckn"])
        op("act", lambda e: e.copy(out=krb_b[:, 0:tn], in_=krb[bs][:, 0:tn]), reads=[f"krb{bs}"], writes=["krb_b"])
        for kind in ("q", "k"):
            for h in range(0, NH, 2):
                _interleave([head_chain(kind, h, 0, t0, tn, bs, bi), head_chain(kind, h + 1, 1, t0, tn, bs, bi)])
        for c in range(tn // 128):
            ci = t0 // 128 + c
            op("pe", lambda e: e.matmul(psv[:], lhsT=ckn[:, c * 128:(c + 1) * 128],
                                        rhs=wv_b[:].rearrange("p h c -> p (h c)"), start=True, stop=True),
               reads=["ckn", "wv_b"], writes=["psv"])
            op("act", lambda e: e.copy(out=V1[:, ci, :, 0:64], in_=psv[:].rearrange("p (h c) -> p h c", c=64)),
               reads=["psv"], writes=["V1", f"V1_{ci}"])
    fw.release(mk)

    mk = fw.mark()
    NS = 4
    psS = [fw.psum(f"psS{s}", [128, 512]) for s in range(NS)]
    acc = [fw.psum(f"acc{s}", [65, 512]) for s in range(2)]
    psb = fw.psum("psb", [64, 512])
    pT = [fw.sbuf(f"pT{s}", [128, 512], BF16) for s in range(NS)]
    osb = [fw.sbuf(f"aosb{s}", [65, 512]) for s in range(2)]
    rden = [fw.sbuf(f"rden{s}", [65, 512]) for s in range(2)]
    att = [fw.sbuf(f"att{s}", [64, 512], BF16) for s in range(2)]

    groups = []
    if not last:
        groups.append((0, 256, [0, 1]))
    for g in range(4):
        groups.append((CTX + g * 1024, 1024, list(range(NCH))))
    it = 0
    oc = 0
    for (q0, qn, tks) in groups:
        halves = [(q0 + o, min(512, qn - o)) for o in range(0, qn, 512)]
        for h in range(NH):
            items = [(ti, tkc, hi, qs, hn_) for ti, tkc in enumerate(tks) for hi, (qs, hn_) in enumerate(halves)]
            LA = 2
            for idx in range(len(items) + LA):
                if idx < len(items):
                    ti, tkc, hi, qs, hn_ = items[idx]
                    s = (it + idx) % NS
                    op("pe", lambda e: e.matmul(psS[s][:, 0:hn_], lhsT=kT[:, h, tkc * 128:(tkc + 1) * 128],
                                                rhs=qT[:, h, qs:qs + hn_], start=True, stop=True),
                       reads=[], writes=[f"psS{s}"])
                if idx >= LA:
                    ti, tkc, hi, qs, hn_ = items[idx - LA]
                    s = (it + idx - LA) % NS
                    op("act", lambda e: e.activation(out=pT[s][:, 0:hn_], in_=psS[s][:, 0:hn_], func=AF.Exp,
                                                     scale=scale), reads=[f"psS{s}"], writes=[f"pT{s}"])
                    op("pe", lambda e: e.matmul(acc[hi][:, 0:hn_], lhsT=V1[:, tkc, h, :], rhs=pT[s][:, 0:hn_],
                                                start=(ti == 0), stop=(ti == len(tks) - 1)),
                       reads=[f"pT{s}"], writes=[f"acc{hi}"], pe_acc=(ti > 0))
            it += len(items)
            for hi, (qs, hn_) in enumerate(halves):
                o = oc % 2
                oc += 1
                op("dve", lambda e: e.tensor_copy(out=osb[o][:, 0:hn_], in_=acc[hi][:, 0:hn_]),
                   reads=[f"acc{hi}"], writes=[f"aosb{o}"])
                op("dve", lambda e: e.reciprocal(out=rden[o][64:65, 0:hn_], in_=osb[o][64:65, 0:hn_]),
                   reads=[f"aosb{o}"], writes=[f"rden{o}"])
                op("pe", lambda e: e.matmul(psb[:, 0:hn_], lhsT=ones_f[64:65, 0:64], rhs=rden[o][64:65, 0:hn_],
                                            start=True, stop=True), reads=[f"rden{o}", "ones_f"], writes=["psb"])
                op("dve", lambda e: e.tensor_tensor(out=att[o][:, 0:hn_], in0=osb[o][0:64, 0:hn_],
                                                    in1=psb[:, 0:hn_], op=ALU.mult),
                   reads=[f"aosb{o}", "psb"], writes=[f"att{o}"])
                dma(ATT[h, :, qs:qs + hn_], att[o][:, 0:hn_], reads=[f"att{o}"], writes=[f"D:ATT{h}_{qs}"],
                    q="pool")
    fw.release(mk)
    fw.release(mk0)


def _zero_fill(G, name, rows):
    P, fw = G["P"], G["fw"]
    dst = _scr(P, name, [rows, T], BF16)
    mk = fw.mark()
    zt = fw.sbuf("zfill", [128, 512], BF16)
    fw.op("pool", lambda e: e.memset(zt[:], 0.0), writes=["zfill"])
    for r in range(rows // 128):
        for t0 in range(0, T, 512):
            tn = min(512, T - t0)
            fw.dma(dst[r * 128:(r + 1) * 128, t0:t0 + tn], zt[:, 0:tn], reads=["zfill"],
                   writes=[f"D:{name}{r}_{t0}"], q="pool")
    fw.release(mk)


def phase_D(G):
    P, fw, nc, li = G["P"], G["fw"], G["nc"], G["li"]
    op, dma = fw.op, fw.dma
    PT, DTRAW = G["PT"], G["DTRAW"]
    ones_f, ident_f, ident_b = G["ones_f"], G["ident_f"], G["ident_b"]
    w_conv = _inp(P, "w_conv_ssm", [DEPTH, 3, 640]); b_conv = _inp(P, "b_conv_ssm", [DEPTH, 640])
    a_log = _inp(P, "a_log", [DEPTH, 12]); dt_bias = _inp(P, "dt_bias", [DEPTH, 12])
    dcol_d = _inp(P, "d_skip_col", [DEPTH, 384]); g_so = _inp(P, "g_ssm_out", [DEPTH, 384])
    TRI = _inp(P, "tri", [2, 128, 128]); NEG = _inp(P, "negmask", [2, 128, 128])
    GMd = _inp(P, "gmask", [9, 128, 128])
    YF = _scr(P, "YF", [2, 384, T])
    SSMT = _scr(P, "SSMT", [SSM_INNER, T], BF16)
    W = T + 4

    mk0 = fw.mark()
    xsT = fw.sbuf("xsT", [128, 3, T])
    BC = [fw.sbuf(f"BC{i}", [64, T], BF16) for i in range(4)]
    xs_tok = fw.sbuf("xs_tok", [128, NCH, 384], BF16)
    B_tok = fw.sbuf("B_tok", [128, NCH, 128], BF16)
    wc = fw.sbuf("wc", [128, 5, 3]); bc = fw.sbuf("bc", [128, 5])
    wc2 = fw.sbuf("wc2", [64, 4, 3]); bc2 = fw.sbuf("bc2", [64, 4])
    for k in range(3):
        dma(wc[:, :, k], w_conv[li, k].rearrange("(r p) -> p r", p=128), writes=["wc"], allow_slow_non_contiguous=True)
        dma(wc2[:, :, k], w_conv[li, k, 384:640].rearrange("(r p) -> p r", p=64), writes=["wc2"],
            allow_slow_non_contiguous=True)
    dma(bc[:], b_conv[li].rearrange("(r p) -> p r", p=128), writes=["bc"], allow_slow_non_contiguous=True)
    dma(bc2[:], b_conv[li, 384:640].rearrange("(r p) -> p r", p=64), writes=["bc2"], allow_slow_non_contiguous=True)

    mk = fw.mark()
    xb = fw.sbuf("xb", [128, W]); acc = fw.sbuf("cacc", [128, W])
    op("pool", lambda e: e.memset(xb[:], 0.0), writes=["xb"])
    jobs = [(128, O_XBC + 128 * r, wc[:, r, :], bc[:, r:r + 1], ("xs", r)) for r in range(3)]
    jobs += [(64, O_XBC + 384 + 64 * i, wc2[:, i, :], bc2[:, i:i + 1], ("bc", i)) for i in range(4)]
    for (np_, r0, wv, bv, dst) in jobs:
        rk = [k for k in fw.lastw if k.startswith("D:PT") and r0 - 127 <= int(k[4:].split("_")[0]) <= r0 + np_ - 1]
        dma(xb[0:np_, 1:1 + CTX], PT[r0:r0 + np_, 0:CTX], reads=rk, writes=["xb"])
        dma(xb[0:np_, 3 + CTX:3 + T], PT[r0:r0 + np_, CTX:T], reads=rk, writes=["xb"], q="pool")
        op("dve", lambda e: e.tensor_scalar(out=acc[0:np_, 0:W - 2], in0=xb[0:np_, 0:W - 2], scalar1=wv[:, 0:1],
                                            scalar2=None, op0=ALU.mult), reads=["xb", "wc", "wc2"], writes=["cacc"])
        for k in (1, 2):
            op("dve", lambda e: e.scalar_tensor_tensor(out=acc[0:np_, 0:W - 2], in0=xb[0:np_, k:W - 2 + k],
                                                       scalar=wv[:, k:k + 1], in1=acc[0:np_, 0:W - 2],
                                                       op0=ALU.mult, op1=ALU.add),
               reads=["xb", "cacc", "wc", "wc2"], writes=["cacc"])
        for (a0, a1, d0) in ((0, CTX, 0), (2 + CTX, 2 + T, CTX)):
            if dst[0] == "xs":
                o_ = xsT[:, dst[1], d0:d0 + (a1 - a0)]
            else:
                o_ = BC[dst[1]][:, d0:d0 + (a1 - a0)]
            op("act", lambda e: e.activation(out=o_, in_=acc[0:np_, a0:a1], func=AF.Silu, bias=bv),
               reads=["cacc", "bc", "bc2"], writes=["xsT" if dst[0] == "xs" else f"BC{dst[1]}"])
    fw.release(mk)

    if G.get("dstop") == 1:
        fw.release(mk0); return
    mk = fw.mark()
    ptx_ = [fw.psum(f"ptx{s}", [128, 512]) for s in range(2)]
    ptx = [t[:, 0:384].rearrange("p (r c) -> p r c", c=128) for t in ptx_]
    ptb_ = [fw.psum(f"ptb{s}", [128, 1024], BF16) for s in range(2)]
    ptb = [t[:, 0:128].rearrange("p (g c) -> p g c", c=64) for t in ptb_]
    for ci in range(NCH):
        s = ci % 2
        for r in range(3):
            op("pe", lambda e: e.transpose(out=ptx[s][:, r, :], in_=xsT[:, r, ci * 128:(ci + 1) * 128],
                                           identity=ident_f[:]), reads=["xsT", "ident_f"], writes=[f"ptx{s}"],
               pe_acc=(r > 0))
        op("act", lambda e: e.copy(out=xs_tok[:, ci, :], in_=ptx_[s][:, 0:384]),
           reads=[f"ptx{s}"], writes=["xs_tok"])
        for g in range(2):
            op("pe", lambda e: e.transpose(out=ptb[s][:, g, :], in_=BC[g][:, ci * 128:(ci + 1) * 128],
                                           identity=ident_b[0:64, 0:64]), reads=[f"BC{g}", "ident_b"],
               writes=[f"ptb{s}"], pe_acc=(g > 0))
        op("dve", lambda e: e.tensor_copy(out=B_tok[:, ci, :], in_=ptb_[s][:, 0:128]),
           reads=[f"ptb{s}"], writes=["B_tok"])
    fw.release(mk)

    tri = fw.sbuf("tri", [128, 2, 128]); neg = fw.sbuf("neg", [128, 2, 128])
    dma(tri[:], TRI.rearrange("d p i -> p d i"), writes=["tri"])
    dma(neg[:], NEG.rearrange("d p i -> p d i"), writes=["neg"])
    rows = fw.sbuf("rows", [1, 24])
    dma(rows[:, 0:12], a_log[li:li + 1, :], writes=["rows"])
    dma(rows[:, 12:24], dt_bias[li:li + 1, :], writes=["rows"])
    arep = fw.sbuf("arep", [128, 24])
    dtr = fw.sbuf("dtr", [128, NCH, 12]); dt = fw.sbuf("dt", [128, NCH, 12]); dta = fw.sbuf("dta", [128, NCH, 12])
    cum = fw.sbuf("cum", [128, NCH, 12]); tot = fw.sbuf("tot", [128, NCH, 12])
    wend = fw.sbuf("wend", [128, NCH, 12]); cdec = fw.sbuf("cdec", [128, NCH, 12]); ncum = fw.sbuf("ncum", [128, NCH, 12])
    mk = fw.mark()
    psr_ = fw.psum("psrow", [128, 512])[:, 0:24]
    psc = fw.psum("psc", [128, 512])[:, 0:NCH * 12].rearrange("p (c n) -> p c n", n=12)
    op("pe", lambda e: e.matmul(psr_[:], lhsT=ones_f[0:1, :], rhs=rows[0:1, :], start=True, stop=True),
       reads=["rows", "ones_f"], writes=["psrow"])
    op("act", lambda e: e.activation(out=arep[:, 0:12], in_=psr_[:, 0:12], func=AF.Exp), reads=["psrow"], writes=["arep"])
    op("dve", lambda e: e.tensor_scalar(out=arep[:, 0:12], in0=arep[:, 0:12], scalar1=-1.0, scalar2=None, op0=ALU.mult),
       reads=["arep"], writes=["arep"])
    op("dve", lambda e: e.tensor_copy(out=arep[:, 12:24], in_=psr_[:, 12:24]), reads=["psrow"], writes=["arep"])
    dma(dtr[:], DTRAW.rearrange("(c p) n -> p c n", p=128), reads=["D:DTRAW"], writes=["dtr"])
    for ci in range(NCH):
        op("dve", lambda e: e.tensor_tensor(out=dtr[:, ci, :], in0=dtr[:, ci, :], in1=arep[:, 12:24], op=ALU.add),
           reads=["dtr", "arep"], writes=["dtr"])
    op("act", lambda e: e.activation(out=dt[:], in_=dtr[:], func=AF.Exp), reads=["dtr"], writes=["dt"])
    op("act", lambda e: e.activation(out=dt[:], in_=dt[:], func=AF.Ln, bias=1.0), reads=["dt"], writes=["dt"])
    for ci in range(NCH):
        op("dve", lambda e: e.tensor_tensor(out=dta[:, ci, :], in0=dt[:, ci, :], in1=arep[:, 0:12], op=ALU.mult),
           reads=["dt", "arep"], writes=["dta"])
    for d in range(2):
        op("pe", lambda e: e.matmul(psc[:], lhsT=tri[:, d, :], rhs=dta[:], start=True, stop=True),
           reads=["tri", "dta", "cum"], writes=["psc"])
        op("dve", lambda e: e.tensor_copy(out=cum[:, :, d * 6:(d + 1) * 6], in_=psc[:, :, d * 6:(d + 1) * 6]),
           reads=["psc"], writes=["cum"])
    op("pe", lambda e: e.matmul(psc[:], lhsT=ones_f[:], rhs=dta[:], start=True, stop=True),
       reads=["ones_f", "dta", "cum"], writes=["psc"])
    op("dve", lambda e: e.tensor_copy(out=tot[:], in_=psc[:]), reads=["psc"], writes=["tot"])
    op("dve", lambda e: e.tensor_tensor(out=wend[:], in0=tot[:], in1=cum[:], op=ALU.subtract),
       reads=["tot", "cum"], writes=["wend"])
    op("act", lambda e: e.activation(out=wend[:], in_=wend[:], func=AF.Exp), reads=["wend"], writes=["wend"])
    op("dve", lambda e: e.tensor_tensor(out=wend[:], in0=wend[:], in1=dt[:], op=ALU.mult), reads=["wend", "dt"], writes=["wend"])
    op("act", lambda e: e.activation(out=cdec[:], in_=tot[:], func=AF.Exp), reads=["tot"], writes=["cdec"])
    op("dve", lambda e: e.tensor_scalar(out=ncum[:], in0=cum[:], scalar1=-1.0, scalar2=None, op0=ALU.mult),
       reads=["cum"], writes=["ncum"])
    fw.release(mk)

    if G.get("dstop") == 2:
        fw.release(mk0); return
    mk = fw.mark()
    S = fw.sbuf("S", [64, NH, 64]); Sb = fw.sbuf("Sb", [64, NH, 64], BF16)
    Rm = [fw.sbuf(f"Rm{s}", [128, 128]) for s in range(2)]
    Em = [fw.sbuf(f"Em{s}", [64, 128]) for s in range(2)]
    sg = [fw.sbuf(f"sg{s}", [128, 128]) for s in range(2)]
    dc = [fw.sbuf(f"dc{s}", [128, 128]) for s in range(2)]
    MT = [fw.sbuf(f"MT{s}", [128, 128], BF16) for s in range(2)]
    CdT = [fw.sbuf(f"CdT{s}", [64, 128], BF16) for s in range(2)]
    Bw = [fw.sbuf(f"Bw{s}", [128, 64], BF16) for s in range(2)]
    ysb = [fw.sbuf(f"ysb{s}", [64, 128]) for s in range(2)]
    psG = [fw.psum(f"psG{g}", [128, 512])[:, 0:128] for g in range(2)]
    psA = [fw.psum(f"psA{s}", [128, 512])[:, 0:128] for s in range(2)]
    psY = [fw.psum(f"psY{s}", [128, 512])[0:64, 0:128] for s in range(2)]
    psS = [fw.psum(f"psSt{s}", [128, 512])[0:64, 0:64] for s in range(2)]
    it = 0
    for d in range(2):
        order = list(range(NCH)) if d == 0 else [1, 0] + list(range(NCH - 1, 1, -1))
        if DFLAG[0] & 16:
            order = order[:2]
        if DFLAG[0] & 32:
            order = order[:10]
        op("pool", lambda e: e.memset(S[:], 0.0), writes=["S"] + [f"S{h}" for h in range(NH)])
        op("pool", lambda e: e.memset(Sb[:], 0.0), writes=["Sb"] + [f"Sb{h}" for h in range(NH)])
        for ci in order:
            c0 = ci * 128
            for g in range(2):
                op("pe", lambda e: e.matmul(psG[g][:], lhsT=BC[g][:, c0:c0 + 128], rhs=BC[2 + g][:, c0:c0 + 128],
                                            start=True, stop=True), reads=[], writes=[f"psG{g}"])
            def ssd_chain(h, s, ci=ci, c0=c0, d=d):
                g = h // 3
                dh = d * 6 + h
                op("dve", lambda e: e.tensor_scalar(out=Rm[s][:], in0=tri[:, d, :], scalar1=dta[:, ci, dh:dh + 1],
                                                    scalar2=None, op0=ALU.mult), reads=[], writes=[f"Rm{s}"])
                yield
                op("pe", lambda e: e.matmul(psA[s][:], lhsT=ones_f[:], rhs=Rm[s][:], start=True, stop=True),
                   reads=[f"Rm{s}"], writes=[f"psA{s}"])
                yield
                op("act", lambda e: e.activation(out=Em[s][:], in_=psA[s][0:64, :], func=AF.Exp),
                   reads=[f"psA{s}"], writes=[f"Em{s}"])
                yield
                op("dve", lambda e: e.tensor_tensor(out=sg[s][:], in0=psA[s][:], in1=neg[:, d, :], op=ALU.add),
                   reads=[f"psA{s}"], writes=[f"sg{s}"])
                yield
                op("act", lambda e: e.activation(out=dc[s][:], in_=sg[s][:], func=AF.Exp, bias=ncum[:, ci, dh:dh + 1]),
                   reads=[f"sg{s}"], writes=[f"dc{s}"])
                op("pool", lambda e: e.tensor_scalar(out=Bw[s][:], in0=B_tok[:, ci, g * 64:(g + 1) * 64],
                                                     scalar1=wend[:, ci, dh:dh + 1], scalar2=None, op0=ALU.mult),
                   reads=[], writes=[f"Bw{s}"])
                yield
                op("dve", lambda e: e.scalar_tensor_tensor(out=MT[s][:], in0=dc[s][:], scalar=dt[:, ci, dh:dh + 1],
                                                           in1=psG[g][:], op0=ALU.mult, op1=ALU.mult),
                   reads=[f"dc{s}", f"psG{g}"], writes=[f"MT{s}"])
                yield
                op("dve", lambda e: e.tensor_tensor(out=CdT[s][:], in0=BC[2 + g][:, c0:c0 + 128], in1=Em[s][:],
                                                    op=ALU.mult), reads=[f"Em{s}"], writes=[f"CdT{s}"])
                yield
                op("pe", lambda e: e.matmul(psY[s][:], lhsT=xs_tok[:, ci, h * 64:(h + 1) * 64], rhs=MT[s][:],
                                            start=True, stop=False), reads=[f"MT{s}"], writes=[f"psY{s}"])
                op("pe", lambda e: e.matmul(psY[s][:], lhsT=Sb[:, h, :], rhs=CdT[s][:], start=False, stop=True),
                   reads=[f"CdT{s}", f"Sb{h}"], writes=[f"psY{s}"], pe_acc=True)
                op("pe", lambda e: e.matmul(psS[s][:], lhsT=Bw[s][:], rhs=xs_tok[:, ci, h * 64:(h + 1) * 64],
                                            start=True, stop=True), reads=[f"Bw{s}"], writes=[f"psSt{s}"])
                yield
                op("act", lambda e: e.copy(out=ysb[s][:], in_=psY[s][:]), reads=[f"psY{s}"], writes=[f"ysb{s}"])
                yield
                dma(YF[d, h * 64:(h + 1) * 64, c0:c0 + 128], ysb[s][:], reads=[f"ysb{s}"],
                    writes=[f"D:YF{d}_{h}_{ci}"], q="sp" if h % 2 else "pool")
                op("dve", lambda e: e.scalar_tensor_tensor(out=S[:, h, :], in0=S[:, h, :],
                                                           scalar=cdec[0:64, ci, dh:dh + 1], in1=psS[s][:],
                                                           op0=ALU.mult, op1=ALU.add),
                   reads=[f"psSt{s}", f"S{h}"], writes=[f"S{h}"])
                yield
                op("act", lambda e: e.copy(out=Sb[:, h, :], in_=S[:, h, :]), reads=[f"S{h}"], writes=[f"Sb{h}"])
                yield

            for h in range(0, NH, 2):
                _interleave([ssd_chain(h, 0), ssd_chain(h + 1, 1)])
    fw.release(mk)

    if G.get("dstop") == 3:
        fw.release(mk0); return
    mk = fw.mark()
    gm = fw.sbuf("gm", [128, 9, 128])
    dcol = fw.sbuf("dcol", [128, 3]); gso = fw.sbuf("gso", [128, 3])
    dma(gm[:], GMd.rearrange("a p i -> p a i"), writes=["gm"])
    dma(dcol[:], dcol_d[li].rearrange("(r p) -> p r", p=128), writes=["dcol"], allow_slow_non_contiguous=True)
    dma(gso[:], g_so[li].rearrange("(r p) -> p r", p=128), writes=["gso"], allow_slow_non_contiguous=True)
    y0 = [fw.sbuf(f"y0_{r}", [128, 512]) for r in range(3)]
    y1 = [fw.sbuf(f"y1_{r}", [128, 512]) for r in range(3)]
    zt = [fw.sbuf(f"zt_{r}", [128, 512]) for r in range(3)]
    sq = [fw.sbuf(f"dsq_{r}", [128, 512]) for r in range(3)]
    rs = fw.sbuf("drs", [128, 512])
    so = [fw.sbuf(f"so{s}", [128, 512], BF16) for s in range(2)]
    psn = [fw.psum(f"psn{s}", [128, 512]) for s in range(2)]
    tblocks = [(t0, min(512, T - t0)) for t0 in range(0, T, 512)]
    n = 0
    for (t0, tn) in tblocks:
        for r in range(3):
            rk = [f"D:YF{d}_{h}_{ci}" for d in range(2) for h in (2 * r, 2 * r + 1)
                  for ci in range(t0 // 128, (t0 + tn) // 128)]
            dma(y0[r][:, 0:tn], YF[0, r * 128:(r + 1) * 128, t0:t0 + tn], reads=rk, writes=[f"y0_{r}"])
            dma(y1[r][:, 0:tn], YF[1, r * 128:(r + 1) * 128, t0:t0 + tn], reads=rk, writes=[f"y1_{r}"], q="pool")
            dma(zt[r][:, 0:tn], PT[O_Z + r * 128:O_Z + (r + 1) * 128, t0:t0 + tn],
                reads=[f"D:PT{O_Z + r * 128}_{t0}"], writes=[f"zt_{r}"])
            op("pool", lambda e: e.tensor_tensor(out=y0[r][:, 0:tn], in0=y0[r][:, 0:tn], in1=y1[r][:, 0:tn], op=ALU.add),
               reads=[f"y0_{r}", f"y1_{r}"], writes=[f"y0_{r}"])
            op("dve", lambda e: e.scalar_tensor_tensor(out=y0[r][:, 0:tn], in0=xsT[:, r, t0:t0 + tn],
                                                       scalar=dcol[:, r:r + 1], in1=y0[r][:, 0:tn],
                                                       op0=ALU.mult, op1=ALU.add),
               reads=["xsT", "dcol", f"y0_{r}"], writes=[f"y0_{r}"])
            op("act", lambda e: e.activation(out=zt[r][:, 0:tn], in_=zt[r][:, 0:tn], func=AF.Silu),
               reads=[f"zt_{r}"], writes=[f"zt_{r}"])
            op("dve", lambda e: e.tensor_tensor(out=y0[r][:, 0:tn], in0=y0[r][:, 0:tn], in1=zt[r][:, 0:tn], op=ALU.mult),
               reads=[f"y0_{r}", f"zt_{r}"], writes=[f"y0_{r}"])
            op("act", lambda e: e.activation(out=sq[r][:, 0:tn], in_=y0[r][:, 0:tn], func=AF.Square),
               reads=[f"y0_{r}"], writes=[f"dsq_{r}"])
        for r2 in range(3):
            s = n % 2; n += 1
            for r in range(3):
                op("pe", lambda e: e.matmul(psn[s][:, 0:tn], lhsT=gm[:, r * 3 + r2, :], rhs=sq[r][:, 0:tn],
                                            start=(r == 0), stop=(r == 2)),
                   reads=[f"dsq_{r}", "gm"], writes=[f"psn{s}"], pe_acc=(r > 0))
            _rsqrt(fw, rs[:, 0:tn], psn[s][:, 0:tn], 1.0 / 192, f"psn{s}", "drs")
            op("dve", lambda e: e.scalar_tensor_tensor(out=so[s][:, 0:tn], in0=y0[r2][:, 0:tn],
                                                       scalar=gso[:, r2:r2 + 1], in1=rs[:, 0:tn],
                                                       op0=ALU.mult, op1=ALU.mult),
               reads=[f"y0_{r2}", "gso", "drs"], writes=[f"so{s}"])
            dma(SSMT[r2 * 128:(r2 + 1) * 128, t0:t0 + tn], so[s][:, 0:tn], reads=[f"so{s}"],
                writes=[f"D:SSMT{r2}_{t0}"], q="pool")
    fw.release(mk)
    fw.release(mk0)


def _sin5(fw, dst, ps, pskey, f5, fb5, tmps, n, tag):
    s_, s2, t_ = tmps
    fw.op("act", lambda e: e.activation(out=s_[:, 0:n], in_=ps[:, 0:n], func=AF.Sin, scale=f5, bias=fb5),
          reads=[pskey, "hyvec"], writes=[tag + "s"])
    fw.op("act", lambda e: e.activation(out=s2[:, 0:n], in_=s_[:, 0:n], func=AF.Square), reads=[tag + "s"], writes=[tag + "s2"])
    fw.op("dve", lambda e: e.tensor_scalar(out=t_[:, 0:n], in0=s2[:, 0:n], scalar1=16.0, scalar2=-20.0,
                                           op0=ALU.mult, op1=ALU.add), reads=[tag + "s2"], writes=[tag + "t"])
    fw.op("dve", lambda e: e.tensor_tensor(out=t_[:, 0:n], in0=t_[:, 0:n], in1=s2[:, 0:n], op=ALU.mult),
          reads=[tag + "t", tag + "s2"], writes=[tag + "t"])
    fw.op("dve", lambda e: e.scalar_tensor_tensor(out=dst, in0=t_[:, 0:n], scalar=5.0, in1=s_[:, 0:n],
                                                  op0=ALU.add, op1=ALU.mult), reads=[tag + "t", tag + "s"], writes=[tag + "h"])


def phase_E(G):
    P, fw, nc, li = G["P"], G["fw"], G["nc"], G["li"]
    op, dma = fw.op, fw.dma
    PT = G["PT"]
    ones_f, ident_f = G["ones_f"], G["ident_f"]
    last = li == DEPTH - 1
    w_conv = _inp(P, "w_conv_hy", [DEPTH, 3, 768]); b_conv = _inp(P, "b_conv_hy", [DEPTH, 768])
    w_f1 = _inp(P, "w_f1", [DEPTH, 33, 64]); w_f2 = _inp(P, "w_f2", [DEPTH, 64, 64]); w_f3 = _inp(P, "w_f3", [DEPTH, 64, 1024])
    hyvec_d = _inp(P, "hy_vecs", [DEPTH, 4, 64])
    d_skip = _inp(P, "d_skip_hy", [DEPTH, 2, 256])
    ndl_d = _inp(P, "hy_ndelta", [128, 2])
    JR = _inp(P, "jrev", [128, 128])
    HYT = _scr(P, "HYT", [HY_CH, T], BF16)

    seqs = [(SEQ, CTX)] if last else [(CTX, 0), (SEQ, CTX)]
    for (L, tok0) in seqs:
        NJ = L // 128
        FWD = 2 * L
        feats_d = _inp(P, f"hy_feats_{L}", [2, 33, L])
        t01_d = _inp(P, f"hy_t01_{L}", [2, 128, L])
        if f"FD{L}" not in P.dscr:
            P.dh = getattr(P, "dh", {})
            P.dh[f"FD{L}"] = nc.dram_tensor(f"FD{L}", [2, 256, FWD], BF16, kind="ExternalOutput" if "FILT" in P.debug and L == SEQ else "Internal")
            P.dscr[f"FD{L}"] = P.dh[f"FD{L}"].ap()
        FDh = P.dh[f"FD{L}"]; FD = P.dscr[f"FD{L}"]

        mk = fw.mark()
        w1 = fw.sbuf("w1", [33, 64]); w2 = fw.sbuf("w2", [64, 64]); w3 = fw.sbuf("w3", [64, 1024])
        hv = fw.sbuf("hv", [64, 4]); f5 = fw.sbuf("f5", [64, 2]); fb5 = fw.sbuf("fb5", [64, 2])
        ndl = fw.sbuf("ndl", [128, 2])
        dma(w1[:], w_f1[li], writes=["w1"]); dma(w2[:], w_f2[li], writes=["w2"]); dma(w3[:], w_f3[li], writes=["w3"])
        dma(hv[:], hyvec_d[li].rearrange("v p -> p v"), writes=["hv"], allow_slow_non_contiguous=True)
        dma(ndl[:], ndl_d, writes=["ndl"])
        for i in range(2):
            op("dve", lambda e: e.tensor_scalar(out=f5[:, i:i + 1], in0=hv[:, 2 * i + 1:2 * i + 2], scalar1=0.2,
                                                scalar2=None, op0=ALU.mult), reads=["hv"], writes=["hyvec"])
            op("dve", lambda e: e.tensor_tensor(out=fb5[:, i:i + 1], in0=f5[:, i:i + 1], in1=hv[:, 2 * i:2 * i + 1],
                                                op=ALU.mult), reads=["hyvec", "hv"], writes=["hyvec"])
        FB = [[fw.sbuf(f"FB{o}{ch}", [128, FWD]) for ch in range(2)] for o in range(2)]
        featb = [fw.sbuf(f"featb{s}", [33, 512]) for s in range(2)]
        t01b = [fw.sbuf(f"t01b{s}", [128, 512]) for s in range(2)]
        tm = [[fw.sbuf(f"tm{a}{b}", [64, 512]) for b in range(3)] for a in range(2)]
        h1 = fw.sbuf("h1", [64, 512]); h2 = fw.sbuf("h2", [64, 512])
        dec = [fw.sbuf(f"dec{ch}", [128, 512]) for ch in range(2)]
        pm1 = fw.psum("pm1", [64, 512]); pm2 = fw.psum("pm2", [64, 512])
        pm3 = [fw.psum(f"pm3{s}", [128, 512]) for s in range(2)]
        nrm = fw.sbuf("nrm", [128, 4])
        FBb = [fw.sbuf(f"FBb{s}", [128, FWD], BF16) for s in range(2)]
        BL = min(512, L)
        n3 = 0
        for dr in (1, 0):
            for bi, c0 in enumerate(range(0, L, BL)):
                s = bi % 2
                dma(featb[s][:, 0:BL], feats_d[dr, :, c0:c0 + BL], writes=[f"featb{s}"])
                dma(t01b[s][:, 0:BL], t01_d[dr, :, c0:c0 + BL], writes=[f"t01b{s}"], q="pool")
                op("pe", lambda e: e.matmul(pm1[:, 0:BL], lhsT=w1[:], rhs=featb[s][:, 0:BL], start=True, stop=True),
                   reads=["w1", f"featb{s}"], writes=["pm1"])
                _sin5(fw, h1[:, 0:BL], pm1, "pm1", f5[:, 0:1], fb5[:, 0:1], tm[0], BL, "a")
                op("pe", lambda e: e.matmul(pm2[:, 0:BL], lhsT=w2[:], rhs=h1[:, 0:BL], start=True, stop=True),
                   reads=["w2", "ah"], writes=["pm2"])
                _sin5(fw, h2[:, 0:BL], pm2, "pm2", f5[:, 1:2], fb5[:, 1:2], tm[1], BL, "b")
                col0 = (L - 1 + c0) if dr == 0 else c0
                for ch in range(2):
                    op("act", lambda e: e.activation(out=dec[ch][:, 0:BL], in_=t01b[s][:, 0:BL], func=AF.Exp,
                                                     scale=ndl[:, ch:ch + 1]),
                       reads=[f"t01b{s}", "ndl"], writes=[f"dec{ch}"])
                    for o in range(2):
                        p3 = n3 % 2; n3 += 1
                        cb = dr * 512 + o * 256 + ch * 128
                        op("pe", lambda e: e.matmul(pm3[p3][:, 0:BL], lhsT=w3[:, cb:cb + 128], rhs=h2[:, 0:BL],
                                                    start=True, stop=True), reads=["w3", "bh"], writes=[f"pm3{p3}"])
                        op("dve", lambda e: e.tensor_tensor(out=FB[o][ch][:, col0:col0 + BL], in0=pm3[p3][:, 0:BL],
                                                            in1=dec[ch][:, 0:BL], op=ALU.mult),
                           reads=[f"pm3{p3}", f"dec{ch}"], writes=[f"FB{o}{ch}"])
        for o in range(2):
            for ch in range(2):
                k = o * 2 + ch
                op("dve", lambda e: e.tensor_reduce(out=nrm[:, k:k + 1], in_=FB[o][ch][:, 0:2 * L - 1], axis=AX.X,
                                                    op=ALU.add, apply_absolute_value=True),
                   reads=[f"FB{o}{ch}"], writes=[f"nrm{k}"])
                op("dve", lambda e: e.tensor_scalar(out=nrm[:, k:k + 1], in0=nrm[:, k:k + 1], scalar1=EPS, scalar2=None,
                                                    op0=ALU.add), reads=[f"nrm{k}"], writes=[f"nrm{k}"])
                op("dve", lambda e: e.reciprocal(out=nrm[:, k:k + 1], in_=nrm[:, k:k + 1]), reads=[f"nrm{k}"], writes=[f"nrm{k}"])
                op("pool" if k % 2 else "dve",
                   lambda e: e.tensor_scalar(out=FBb[k % 2][:, 0:2 * L - 1], in0=FB[o][ch][:, 0:2 * L - 1],
                                             scalar1=nrm[:, k:k + 1], scalar2=None, op0=ALU.mult),
                   reads=[f"nrm{k}", f"FB{o}{ch}"], writes=[f"FBb{k % 2}"])
                dma(FD[o, ch * 128:(ch + 1) * 128, 0:2 * L - 1], FBb[k % 2][:, 0:2 * L - 1], reads=[f"FBb{k % 2}"],
                    writes=[f"D:FD{L}"], q="sp" if k % 2 else "pool")
        fw.release(mk)

        GW = (2 * NJ - 1) * 128
        CG = 512 // NJ if NJ <= 32 else 16
        CG = min(CG, 16)
        for ch in range(2):
            mk = fw.mark()
            Pq = [fw.sbuf(f"Pq{q}", [128, NJ, 128]) for q in range(3)]
            zb = [fw.sbuf(f"zb{s}", [128, NJ, 128]) for s in range(2)]
            zrev = fw.sbuf("zrev", [128, NJ, 128], BF16)
            zd = fw.sbuf("zd", [128, NJ, 128])
            dsk = fw.sbuf("dsk", [128, 128])
            drow = fw.sbuf("drow", [1, 128])
            jr = fw.sbuf("jr", [128, 128])
            wch = fw.sbuf("wch", [128, 3, 3]); bch = fw.sbuf("bch", [128, 3])
            dma(jr[:], JR, writes=["jr"])
            for q in range(3):
                r0 = q * 256 + ch * 128
                for k in range(3):
                    dma(wch[:, q, k:k + 1], w_conv[li, k, r0:r0 + 128].rearrange("(p o) -> p o", o=1), writes=["wch"],
                        allow_slow_non_contiguous=True)
                dma(bch[:, q:q + 1], b_conv[li, r0:r0 + 128].rearrange("(p o) -> p o", o=1), writes=["bch"],
                    allow_slow_non_contiguous=True)
            mk2 = fw.mark()
            xb = fw.sbuf("hxb", [128, L + 2]); pT_ = fw.sbuf("hpT", [128, L])
            ptr = [fw.psum(f"hptr{s}", [128, 4, 128]) for s in range(2)]
            op("pool", lambda e: e.memset(xb[:], 0.0), writes=["hxb"])
            nt = 0
            for q in range(3):
                r0 = O_HY + q * 256 + ch * 128
                rk = [k for k in fw.lastw if k.startswith("D:PT") and r0 - 127 <= int(k[4:].split("_")[0]) <= r0 + 127]
                dma(xb[:, 1:1 + L], PT[r0:r0 + 128, tok0:tok0 + L], reads=rk, writes=["hxb"])
                op("dve", lambda e: e.tensor_scalar(out=pT_[:], in0=xb[:, 0:L], scalar1=wch[:, q, 0:1], scalar2=bch[:, q:q + 1],
                                                    op0=ALU.mult, op1=ALU.add), reads=["hxb", "wch", "bch"], writes=["hpT"])
                for k in (1, 2):
                    op("dve", lambda e: e.scalar_tensor_tensor(out=pT_[:], in0=xb[:, k:k + L], scalar=wch[:, q, k:k + 1],
                                                               in1=pT_[:], op0=ALU.mult, op1=ALU.add),
                       reads=["hxb", "hpT", "wch"], writes=["hpT"])
                for j0 in range(0, NJ, 4):
                    s = nt % 2; nt += 1
                    jn = min(4, NJ - j0)
                    for j in range(jn):
                        op("pe", lambda e: e.transpose(out=ptr[s][:, j, :], in_=pT_[:, (j0 + j) * 128:(j0 + j + 1) * 128],
                                                       identity=ident_f[:]), reads=["hpT", "ident_f"], writes=[f"hptr{s}"],
                           pe_acc=(j > 0))
                    op("act", lambda e: e.copy(out=Pq[q][:, j0:j0 + jn, :], in_=ptr[s][:, 0:jn, :]),
                       reads=[f"hptr{s}"], writes=[f"Pq{q}"])
            fw.release(mk2)

            NG = 4
            Gt = [fw.sbuf(f"Gt{s}", [128, GW], BF16) for s in range(NG)]
            psJ = [fw.psum(f"psJ{s}", [128, 4, 128]) for s in range(2)]
            Yb = [fw.psum(f"Yb{s}", [128, CG, NJ]) for s in range(2)]
            psd_ = fw.psum("hpsd", [128, 128])
            hout = [fw.sbuf(f"hout{s}", [128, 512], BF16) for s in range(2)]
            zcur = Pq[0]; zkey = "Pq0"
            ng = 0; ny = 0
            for o in range(2):
                dma(drow[:], d_skip[li, o:o + 1, ch * 128:(ch + 1) * 128], writes=["drow"])
                op("pe", lambda e: e.matmul(psd_[:], lhsT=ones_f[0:1, :], rhs=drow[0:1, :], start=True, stop=True),
                   reads=["drow", "ones_f"], writes=["hpsd"])
                op("act", lambda e: e.copy(out=dsk[:], in_=psd_[:]), reads=["hpsd"], writes=["dsk"])
                for j in range(NJ):
                    op("pool" if j % 2 else "dve", lambda e: e.tensor_tensor(out=zd[:, j, :], in0=zcur[:, j, :], in1=dsk[:],
                                                                             op=ALU.mult),
                       reads=[zkey, "dsk"], writes=["zd"])
                for j0 in range(0, NJ, 4):
                    s = (j0 // 4) % 2
                    jn = min(4, NJ - j0)
                    op("pe", lambda e: e.matmul(psJ[s][:, 0:jn, :], lhsT=jr[:], rhs=zcur[:, j0:j0 + jn, :], start=True, stop=True),
                       reads=[zkey, "jr"], writes=[f"psJ{s}"])
                    op("act", lambda e: e.copy(out=zrev[:, j0:j0 + jn, :], in_=psJ[s][:, 0:jn, :]),
                       reads=[f"psJ{s}"], writes=["zrev"])
                znext = zb[o]; nkey = f"zb{o}"
                for c0 in range(0, 128, CG):
                    yb = ny % 2; ny += 1
                    for cc in range(CG):
                        c = c0 + cc
                        gs = ng % NG; ng += 1
                        row = (o * 256 + ch * 128 + c) * FWD
                        dma(Gt[gs][:], bass.AP(tensor=FDh, offset=row, ap=[[1, 128], [1, GW]]),
                            reads=[f"D:FD{L}"], writes=[f"Gt{gs}"], q="sp" if ng % 2 else "act")
                        ds_ = [0] + [d for d in range(-(NJ - 1), NJ) if d != 0]
                        for di, d in enumerate(ds_):
                            J0 = max(0, -d); J1 = min(NJ, NJ - d)
                            op("pe", lambda e: e.matmul(Yb[yb][:, cc, J0 + d:J1 + d],
                                                        lhsT=Gt[gs][:, (d + NJ - 1) * 128:(d + NJ) * 128],
                                                        rhs=zrev[:, J0:J1, c], start=(di == 0), stop=(di == len(ds_) - 1)),
                               reads=[f"Gt{gs}", "zrev"], writes=[f"Yb{yb}"], pe_acc=(di > 0))
                    zv = znext[:, :, c0:c0 + CG].rearrange("p j c -> p c j")
                    op("dve", lambda e: e.tensor_tensor(out=zv, in0=Yb[yb][:], in1=zd[:, :, c0:c0 + CG].rearrange("p j c -> p c j"),
                                                        op=ALU.add), reads=[f"Yb{yb}", "zd"], writes=[nkey])
                    op("pool", lambda e: e.tensor_tensor(out=zv, in0=zv, in1=Pq[o + 1][:, :, c0:c0 + CG].rearrange("p j c -> p c j"),
                                                         op=ALU.mult), reads=[nkey, f"Pq{o + 1}"], writes=[nkey])
                zcur = znext; zkey = nkey
            for j0 in range(0, NJ, 4):
                s = (j0 // 4) % 2
                jn = min(4, NJ - j0)
                for j in range(jn):
                    op("pe", lambda e: e.transpose(out=psJ[s][:, j, :], in_=zcur[:, j0 + j, :], identity=ident_f[:]),
                       reads=[zkey, "ident_f"], writes=[f"psJ{s}"], pe_acc=(j > 0))
                op("act", lambda e: e.copy(out=hout[s][:, 0:jn * 128], in_=psJ[s][:, 0:jn, :].rearrange("p j t -> p (j t)")),
                   reads=[f"psJ{s}"], writes=[f"hout{s}"])
                dma(HYT[ch * 128:(ch + 1) * 128, tok0 + j0 * 128:tok0 + (j0 + jn) * 128], hout[s][:, 0:jn * 128],
                    reads=[f"hout{s}"], writes=[f"D:HYT{ch}_{tok0 + j0 * 128}"], q="pool")
            fw.release(mk)
    if last:
        pass


def phase_F(G):
    P, fw, nc, li = G["P"], G["fw"], G["nc"], G["li"]
    op, dma = fw.op, fw.dma
    XR, MOD, OUT = G["XR"], G["MOD"], G["OUT"]
    ident_b = G["ident_b"]
    last = li == DEPTH - 1
    w_out = _inp(P, "w_out", [DEPTH, D, D])
    w_ff1 = _inp(P, "w_ff1", [DEPTH, D, DFF]); w_ff2 = _inp(P, "w_ff2", [DEPTH, DFF, D])
    ATT = _scr(P, "ATT", [NH, 64, T], BF16)
    SSMT = _scr(P, "SSMT", [SSM_INNER, T], BF16)
    HYT = _scr(P, "HYT", [HY_CH, T], BF16)

    mk0 = fw.mark()
    w1b = fw.sbuf("w1b", [128, 8, DFF], BF16)
    w2b = fw.sbuf("w2b", [128, 32, D], BF16)
    wo_att = fw.sbuf("wo_att", [64, NH, D], BF16)
    wo_rest = fw.sbuf("wo_rest", [128, 5, D], BF16)
    mk = fw.mark()
    stg = [fw.sbuf(f"fstg{s}", [128, 8, 512]) for s in range(2)]
    n = 0
    for nb in range(8):
        s = n % 2; n += 1
        _load_cast(fw, w1b[:, :, nb * 512:(nb + 1) * 512], w_ff1[li, :, nb * 512:(nb + 1) * 512]
                   .rearrange("(k p) n -> p k n", p=128), stg[s][:], "w1b", f"fstg{s}", q="sp" if n % 2 else "pool")
    for kb in range(4):
        for half in range(2):
            s = n % 2; n += 1
            _load_cast(fw, w2b[:, kb * 8:(kb + 1) * 8, half * 512:(half + 1) * 512],
                       w_ff2[li, kb * 1024:(kb + 1) * 1024, half * 512:(half + 1) * 512]
                       .rearrange("(k p) n -> p k n", p=128), stg[s][:], "w2b", f"fstg{s}",
                       q="sp" if n % 2 else "pool")
    for half in range(2):
        s = n % 2; n += 1
        _load_cast(fw, wo_att[:, :, half * 512:(half + 1) * 512],
                   w_out[li, 0:384, half * 512:(half + 1) * 512].rearrange("(h c) n -> c h n", c=64),
                   stg[s][0:64, 0:6, :], "wo_att", f"fstg{s}")
        s = n % 2; n += 1
        _load_cast(fw, wo_rest[:, :, half * 512:(half + 1) * 512],
                   w_out[li, 384:1024, half * 512:(half + 1) * 512].rearrange("(k p) n -> p k n", p=128),
                   stg[s][:, 0:5, :], "wo_rest", f"fstg{s}")
    fw.release(mk)

    mods = [fw.sbuf(f"fmod{d}", [128, D]) for d in range(4)]
    attb = [fw.sbuf(f"attb{s}", [64, NH, 128], BF16) for s in range(2)]
    ssmb = [fw.sbuf(f"ssmb{s}", [128, 3, 128], BF16) for s in range(2)]
    hyb = [fw.sbuf(f"hyb{s}", [128, 2, 128], BF16) for s in range(2)]
    xt = [fw.sbuf(f"fxt{s}", [128, D]) for s in range(2)]
    xm = [fw.sbuf(f"fxm{s}", [128, D]) for s in range(2)]
    hb = fw.sbuf("fhb", [128, D], BF16)
    h2T = fw.sbuf("h2T", [128, 8, 128], BF16)
    aT = fw.sbuf("aT", [128, 32, 128], BF16)
    rr = [fw.sbuf(f"frr{s}", [128, 128]) for s in range(2)]
    ss = fw.sbuf("fss", [128, NCH]); rstd = fw.sbuf("frstd", [128, NCH])
    ot = [fw.sbuf(f"fot{s}", [128, 512]) for s in range(2)]
    psm = [fw.psum(f"psm{s}", [128, 512]) for s in range(2)]
    pst = fw.psum("fpst", [128, 8, 128], BF16)
    psf = [fw.psum(f"psf{s}", [128, 128]) for s in range(2)]
    pso = [fw.psum(f"fpso{s}", [128, 512]) for s in range(2)]

    chunks = list(range(2 if last else 0, NCH))
    state = {"kind": None, "n": 0}

    def front(ci, b):
        kind = 1 if ci < 2 else 0
        t0 = ci * 128
        if kind != state["kind"]:
            for d, src in enumerate((2, 3, 4, 5)):
                dma(mods[d][:], MOD[kind, src], reads=[f"D:MOD{kind}_{src}"], writes=[f"fmod{d}"])
            state["kind"] = kind
        dma(attb[b][:], ATT[:, :, t0:t0 + 128].rearrange("h c t -> c h t"),
            reads=[k for k in fw.lastw if k.startswith("D:ATT")], writes=[f"attb{b}"])
        dma(ssmb[b][:], SSMT[:, t0:t0 + 128].rearrange("(k p) t -> p k t", p=128),
            reads=[k for k in fw.lastw if k.startswith("D:SSMT")], writes=[f"ssmb{b}"], q="pool")
        dma(hyb[b][:], HYT[:, t0:t0 + 128].rearrange("(k p) t -> p k t", p=128),
            reads=[k for k in fw.lastw if k.startswith("D:HYT")], writes=[f"hyb{b}"], q="pool")
        dma(xt[b][:], XR[t0:t0 + 128, :], reads=[f"D:XR{ci}"], writes=[f"fxt{b}"])
        for half in range(2):
            ops_ = [(attb[b][:, h, :], wo_att[:, h, half * 512:(half + 1) * 512], f"attb{b}", "wo_att") for h in range(NH)]
            ops_ += [(ssmb[b][:, k, :], wo_rest[:, k, half * 512:(half + 1) * 512], f"ssmb{b}", "wo_rest") for k in range(3)]
            ops_ += [(hyb[b][:, k, :], wo_rest[:, 3 + k, half * 512:(half + 1) * 512], f"hyb{b}", "wo_rest") for k in range(2)]
            for i, (l_, r_, lk, rk) in enumerate(ops_):
                op("pe", lambda e: e.matmul(psm[half][:], lhsT=l_, rhs=r_, start=(i == 0), stop=(i == len(ops_) - 1)),
                   reads=[lk, rk], writes=[f"psm{half}"], pe_acc=(i > 0))
            op("dve", lambda e: e.tensor_tensor(out=xm[b][:, half * 512:(half + 1) * 512], in0=psm[half][:],
                                                in1=mods[0][:, half * 512:(half + 1) * 512], op=ALU.mult),
               reads=[f"psm{half}", "fmod0"], writes=[f"fxm{b}"])
        op("pool", lambda e: e.tensor_tensor(out=xm[b][:], in0=xm[b][:], in1=xt[b][:], op=ALU.add),
           reads=[f"fxm{b}", f"fxt{b}"], writes=[f"fxm{b}"])
        op("act", lambda e: e.activation(out=hb[:], in_=xm[b][:], func=AF.Square, accum_out=ss[:, ci:ci + 1]),
           reads=[f"fxm{b}"], writes=["fhb", f"fss{ci}"])
        _rsqrt(fw, rstd[:, ci:ci + 1], ss[:, ci:ci + 1], 1.0 / D, f"fss{ci}", f"frstd{ci}")
        op("dve", lambda e: e.scalar_tensor_tensor(out=xt[b][:], in0=xm[b][:], scalar=rstd[:, ci:ci + 1],
                                                   in1=mods[1][:], op0=ALU.mult, op1=ALU.mult),
           reads=[f"fxm{b}", f"frstd{ci}", "fmod1"], writes=[f"fxt{b}"])
        op("pool", lambda e: e.tensor_tensor(out=hb[:], in0=xt[b][:], in1=mods[2][:], op=ALU.add),
           reads=[f"fxt{b}", "fmod2"], writes=["fhb"])

    def mid(ci, b):
        for k in range(8):
            op("pe", lambda e: e.transpose(out=pst[:, k, :], in_=hb[:, k * 128:(k + 1) * 128], identity=ident_b[:]),
               reads=["fhb", "ident_b"], writes=["fpst"], pe_acc=(k > 0))
        op("act", lambda e: e.copy(out=h2T[:], in_=pst[:]), reads=["fpst"], writes=["h2T"])
        for j in range(32):
            s = j % 2
            for k in range(8):
                op("pe", lambda e: e.matmul(psf[s][:], lhsT=w1b[:, k, j * 128:(j + 1) * 128], rhs=h2T[:, k, :],
                                            start=(k == 0), stop=(k == 7)),
                   reads=["h2T", "w1b"], writes=[f"psf{s}"], pe_acc=(k > 0))
            op("act", lambda e: e.activation(out=rr[s][:], in_=psf[s][:], func=AF.Relu),
               reads=[f"psf{s}"], writes=[f"frr{s}"])
            op("dve" if j % 2 else "pool", lambda e: e.tensor_tensor(out=aT[:, j, :], in0=rr[s][:], in1=rr[s][:],
                                                                     op=ALU.mult),
               reads=[f"frr{s}"], writes=[f"aT{j}"])

    def back(ci, b):
        t0 = ci * 128
        for half in range(2):
            s = state["n"] % 2; state["n"] += 1
            for j in range(32):
                op("pe", lambda e: e.matmul(pso[s][:], lhsT=aT[:, j, :], rhs=w2b[:, j, half * 512:(half + 1) * 512],
                                            start=(j == 0), stop=(j == 31)),
                   reads=[f"aT{j}", "w2b"], writes=[f"fpso{s}"], pe_acc=(j > 0))
            op("dve", lambda e: e.tensor_tensor(out=ot[s][:], in0=pso[s][:],
                                                in1=mods[3][:, half * 512:(half + 1) * 512], op=ALU.mult),
               reads=[f"fpso{s}", "fmod3"], writes=[f"fot{s}"])
            op("pool", lambda e: e.tensor_tensor(out=ot[s][:], in0=ot[s][:], in1=xm[b][:, half * 512:(half + 1) * 512],
                                                 op=ALU.add), reads=[f"fot{s}", f"fxm{b}"], writes=[f"fot{s}"])
            if last:
                dma(OUT[t0 - CTX:t0 - CTX + 128, half * 512:(half + 1) * 512], ot[s][:], reads=[f"fot{s}"],
                    writes=[f"D:OUT{ci}_{half}"], q="pool")
            else:
                dma(XR[t0:t0 + 128, half * 512:(half + 1) * 512], ot[s][:], reads=[f"fot{s}"],
                    writes=[f"D:XR{ci}"], q="pool")

    front(chunks[0], 0)
    for i, ci in enumerate(chunks):
        b = i % 2
        mid(ci, b)
        nxt = chunks[i + 1] if i + 1 < len(chunks) else None
        same_kind = nxt is not None and ((nxt < 2) == (ci < 2))
        if nxt is not None and same_kind:
            front(nxt, 1 - b)
            back(ci, b)
        else:
            back(ci, b)
            if nxt is not None:
                front(nxt, 1 - b)
    fw.release(mk0)
```

```python
import math
import numpy as np
import ml_dtypes
import concourse.bass as bass
import concourse.mybir as mybir
from concourse.bass_utils import run_bass_kernel_spmd

F32 = mybir.dt.float32
BF16 = mybir.dt.bfloat16
AF = mybir.ActivationFunctionType
ALU = mybir.AluOpType
AX = mybir.AxisListType

D = 1024
SEQ = 4096
CTX = 256
T = SEQ + CTX
NCH = T // 128
DEPTH = 2
EPS = 1e-6
NH = 6
QK = 96
IN_COLS = 2220
O_CQ, O_CKV, O_KR, O_Z, O_XBC, O_DT, O_HY = 0, 256, 384, 416, 800, 1440, 1452
SSM_INNER = 384
HY_CH = 256
DFF = 4096


class FW:
    NDSEM = 48

    def __init__(self):
        self.nc = bass.Bass("TRN2", target_bir_lowering=False)
        nc = self.nc
        self.eng = {"pe": nc.tensor, "act": nc.scalar, "dve": nc.vector,
                    "pool": nc.gpsimd, "sp": nc.sync}
        self._ctx = []
        self.sems = {}
        for e in self.eng:
            self.sems[e] = self._enter(nc.semaphore("s_" + e))
        self.dq = {"sp": 36, "pool": 36, "act": 8}
        self.dsems = {}
        self.dcnt = {}
        self.dnext = {q: 0 for q in self.dq}
        for q, n in self.dq.items():
            for i in range(n):
                self.dsems[(q, i)] = self._enter(nc.semaphore(f"d_{q}{i}"))
                self.dcnt[(q, i)] = 0
        self.seq = {e: 0 for e in self.eng}
        self.waited = {e: {} for e in self.eng}
        self.lastw = {}
        self.readers = {}
        self.n_inst = 0
        self.n_wait = 0
        self._rr = 0
        self.psum_keys = set()

    def _enter(self, cm):
        v = cm.__enter__()
        self._ctx.append(cm)
        return v

    def mark(self):
        return len(self._ctx)

    def release(self, mark):
        self.barrier()
        while len(self._ctx) > mark:
            self._ctx.pop().__exit__(None, None, None)
        self.lastw = {k: v for k, v in self.lastw.items() if k.startswith("D:")}
        self.readers = {k: v for k, v in self.readers.items() if k.startswith("D:")}

    def close(self):
        while self._ctx:
            self._ctx.pop().__exit__(None, None, None)

    def sbuf(self, name, shape, dt=F32):
        self._uid = getattr(self, "_uid", 0) + 1
        return self._enter(self.nc.sbuf_tensor(f"sb{self._uid}_{name}", list(shape), dt))

    def psum(self, name, shape, dt=F32):
        self._uid = getattr(self, "_uid", 0) + 1
        full = 512 if dt == F32 else 1024
        t = self._enter(self.nc.psum_tensor(f"ps{self._uid}_{name}", [128, full], dt))
        self.psum_keys.add(name)
        shape = list(shape)
        n = 1
        for d in shape[1:]:
            n *= d
        assert n <= full, (name, shape)
        v = t[0:shape[0], 0:n]
        if len(shape) == 3:
            v = v.rearrange("p (a b) -> p a b", b=shape[2])
        return v

    def _sem(self, key):
        return self.sems[key] if isinstance(key, str) else self.dsems[key]

    def _deps(self, reads, writes):
        deps = {}

        def add(k, v):
            if deps.get(k, 0) < v:
                deps[k] = v
        for r in reads:
            d = self.lastw.get(r)
            if d is not None:
                add(*d)
        for wk in writes:
            d = self.lastw.get(wk)
            if d is not None:
                add(*d)
            for rk, rv in self.readers.get(wk, {}).items():
                add(rk, rv)
        return deps

    def _emit_waits(self, e, deps, skip_self=False):
        w = self.waited[e]
        for k, v in deps.items():
            if skip_self and k == e:
                continue
            if w.get(k, 0) >= v:
                continue
            self.eng[e].wait_ge(self._sem(k), v)
            w[k] = v
            self.n_wait += 1

    def _record(self, dep, reads, writes):
        k, v = dep
        for r in reads:
            self.readers.setdefault(r, {})[k] = v
        for wk in writes:
            self.lastw[wk] = dep
            self.readers[wk] = {}

    def op(self, e, fn, reads=(), writes=(), pe_acc=False):
        if e == "any":
            e = ("dve", "pool", "act")[self._rr % 3]
            self._rr += 1
        px = [r for r in reads if r in self.psum_keys and r not in writes]
        if px:
            reads = [r for r in reads if r not in px]
            writes = list(writes) + px
        deps = self._deps(reads, writes)
        self._emit_waits(e, deps, skip_self=(e == "pe" and pe_acc))
        ins = fn(self.eng[e])
        self.seq[e] += 1
        ins.then_inc(self.sems[e], 1)
        self._record((e, self.seq[e]), reads, writes)
        self.n_inst += 1
        return ins

    def dma(self, out, in_, reads=(), writes=(), q="sp", **kw):
        deps = self._deps(reads, writes)
        i = (q, self.dnext[q])
        self.dnext[q] = (self.dnext[q] + 1) % self.dq[q]
        if self.dcnt[i] > 0:
            deps[i] = max(deps.get(i, 0), 16 * self.dcnt[i])
        self._emit_waits(q, deps)
        ins = self.eng[q].dma_start(out=out, in_=in_, **kw)
        self.dcnt[i] += 1
        ins.then_inc(self.dsems[i], 16)
        self._record((i, 16 * self.dcnt[i]), reads, writes)
        self.n_inst += 1
        return ins

    def barrier(self):
        deps = {e: self.seq[e] for e in self.eng if self.seq[e] > 0}
        for i, c in self.dcnt.items():
            if c > 0:
                deps[i] = 16 * c
        for e in self.eng:
            self._emit_waits(e, deps)

    def finish(self, keys, e="sp"):
        self._emit_waits(e, self._deps(keys, ()))


def _consts():
    c = {}
    c["ident_f"] = np.eye(128, dtype=np.float32)
    c["ones_f"] = np.ones((128, 128), np.float32)
    cos = np.ones((96, T), np.float32); sin = np.zeros((96, T), np.float32)
    t = np.arange(SEQ)
    pos = [(t // 64).astype(np.float32), (t % 64).astype(np.float32)]
    inv = (10000.0 ** (-np.arange(8, dtype=np.float32) / 8)).astype(np.float32)
    RT = np.zeros((96, 96), np.float32)
    for a in range(2):
        ang = pos[a][None, :] * inv[:, None]
        for b in range(2):
            for f in range(8):
                r = 64 + 16 * a + 8 * b + f
                cos[r, CTX:] = np.cos(ang[f]); sin[r, CTX:] = np.sin(ang[f])
        for f in range(8):
            i1 = 64 + 16 * a + f; i2 = i1 + 8
            RT[i2, i1] = -1.0
            RT[i1, i2] = 1.0
    c["rope_cos"] = cos; c["rope_sin"] = sin; c["rope_RT"] = RT
    pk = np.zeros((32, 96), np.float32); pk[np.arange(32), 64 + np.arange(32)] = 1.0
    c["pk_sel"] = pk
    i = np.arange(128)
    triF = (i[:, None] <= i[None, :]).astype(np.float32)
    c["tri"] = np.stack([triF, triF.T.copy()])
    NEGV = -1.0e30
    negF = np.where(i[:, None] <= i[None, :], 0.0, NEGV).astype(np.float32)
    negB = np.where(i[:, None] >= i[None, :], 0.0, NEGV).astype(np.float32)
    c["negmask"] = np.stack([negF, negB])
    gm = np.zeros((3, 3, 128, 128), np.float32)
    for r in range(3):
        for r2 in range(3):
            ga = (r * 128 + i) // 192; gb = (r2 * 128 + i) // 192
            gm[r, r2] = (ga[:, None] == gb[None, :]).astype(np.float32)
    c["gmask"] = gm.reshape(9, 128, 128)
    c["jrev"] = np.ascontiguousarray(np.eye(128, dtype=np.float32)[::-1])
    deltas = np.abs(np.linspace(math.log(1e-2) / 1.5, math.log(1e-2) / 0.3, HY_CH, dtype=np.float32))
    c["hy_ndelta"] = np.ascontiguousarray((-deltas).reshape(2, 128).T.astype(np.float32))
    for L in (CTX, SEQ):
        t01 = np.linspace(0.0, 1.0, L, dtype=np.float32)
        w = (np.float32(2.0 * math.pi / L) * np.arange(L, dtype=np.float32))
        bands = np.linspace(1e-4, 15, 16, dtype=np.float32)
        ang = (bands[None, :] * w[:, None]).astype(np.float32)
        feats = np.concatenate([t01[:, None], np.cos(ang), -np.sin(ang)], axis=1).astype(np.float32)
        fT = feats.T
        c[f"hy_feats_{L}"] = np.ascontiguousarray(np.stack([fT, fT[:, ::-1]]))
        t01r = np.broadcast_to(t01[None, :], (128, L))
        c[f"hy_t01_{L}"] = np.ascontiguousarray(np.stack([t01r, t01r[:, ::-1]]))
    return c


CONST_SPECS = {"ident_f": ([128, 128], F32), "ones_f": ([128, 128], F32)}


class Prog:
    def __init__(self, debug=()):
        self.fw = FW()
        self.nc = self.fw.nc
        self.debug = set(debug)
        self.din = {}
        self.dscr = {}

    def inp(self, name, shape, dt=F32):
        self.din[name] = self.nc.dram_tensor(name, list(shape), dt, kind="ExternalInput").ap()
        return self.din[name]

    def scratch(self, name, shape, dt=F32, out=False):
        kind = "ExternalOutput" if (out or name in self.debug) else "Internal"
        self.dscr[name] = self.nc.dram_tensor(name, list(shape), dt, kind=kind).ap()
        return self.dscr[name]


DSTOP = [None]
ENABLE_SSD = [True]
ENABLE_HYENA = [True]
DFLAG = [0]
SKIP_C = [False]


def build(debug=(), stop_after=None):
    P = Prog(debug)
    fw, nc = P.fw, P.nc
    op, dma = fw.op, fw.dma

    xcat = P.inp("xcat", [T, D])
    cvec = P.inp("cvec", [2, D])
    w_mod = P.inp("w_mod", [DEPTH, D, 6 * D])
    b_mod = P.inp("b_mod", [DEPTH, 6 * D])
    g_norm_mix = P.inp("g_norm_mix", [DEPTH, D])
    g_norm_mlp = P.inp("g_norm_mlp", [DEPTH, D])
    w_in = P.inp("w_in", [DEPTH, D, IN_COLS])
    cst = {k: P.inp(k, s, d) for k, (s, d) in CONST_SPECS.items()}

    XR = P.scratch("XR", [T, D])
    MOD = P.scratch("MOD", [2, 6, 128, D])
    PT = P.scratch("PT", [2224, T])
    DTRAW = P.scratch("DTRAW", [T, 12])
    OUT = P.scratch("out", [SEQ, D], out=True)

    ident_f = fw.sbuf("ident_f", [128, 128])
    ident_b = fw.sbuf("ident_b", [128, 128], BF16)
    ones_f = fw.sbuf("ones_f", [128, 128])
    dma(ident_f[:], cst["ident_f"], writes=["ident_f"])
    dma(ones_f[:], cst["ones_f"], writes=["ones_f"])
    op("dve", lambda e: e.tensor_copy(out=ident_b[:], in_=ident_f[:]), reads=["ident_f"], writes=["ident_b"])

    for ci in range(0, NCH, 2):
        dma(XR[ci * 128:(ci + 2) * 128, :], xcat[ci * 128:(ci + 2) * 128, :],
            writes=[f"D:XR{ci}", f"D:XR{ci + 1}"], q="pool")

    for li in range(DEPTH):
        mk = fw.mark()
        cT = fw.sbuf("cT", [128, 2, 8])
        sT = fw.sbuf("sT", [128, 2, 8])
        srep = fw.sbuf("srep", [128, 2, 8, 128])
        modsb = [fw.sbuf(f"modsb{j}", [128, 6 * D]) for j in range(2)]
        brow = fw.sbuf("brow", [1, 6 * D])
        grow = fw.sbuf("grow", [1, 2 * D])
        grep = fw.sbuf("grep", [128, 2 * D])
        wst = [fw.sbuf(f"wst{s}", [128, 8, 512]) for s in range(2)]
        psA = [fw.psum(f"psA{s}", [128, 512]) for s in range(2)]

        dma(cT[:], cvec.rearrange("j (k p) -> p j k", p=128), writes=["cT"],
            allow_slow_non_contiguous=True)
        dma(brow[:], b_mod[li:li + 1, :], writes=["brow"])
        dma(grow[:, 0:D], g_norm_mix[li:li + 1, :], writes=["grow"])
        dma(grow[:, D:2 * D], g_norm_mlp[li:li + 1, :], writes=["grow"])
        op("act", lambda e: e.activation(out=sT[:], in_=cT[:], func=AF.Silu), reads=["cT"], writes=["sT"])
        for j in range(2):
            for k in range(8):
                op("dve", lambda e: e.tensor_copy(out=srep[:, j, k, :],
                                                  in_=sT[:, j, k:k + 1].to_broadcast([128, 128])),
                   reads=["sT"], writes=["srep"])
        for nb in range(4):
            s = nb % 2
            op("pe", lambda e: e.matmul(psA[s][:], lhsT=ones_f[0:1, :], rhs=grow[0:1, nb * 512:(nb + 1) * 512],
                                        start=True, stop=True),
               reads=["ones_f", "grow"], writes=[f"psA{s}"])
            op("dve", lambda e: e.tensor_copy(out=grep[:, nb * 512:(nb + 1) * 512], in_=psA[s][:]),
               reads=[f"psA{s}"], writes=["grep"])
        cnt = 0
        for nb in range(12):
            ws = nb % 2
            dma(wst[ws][:], w_mod[li, :, nb * 512:(nb + 1) * 512].rearrange("(k p) n -> p k n", p=128),
                writes=[f"wst{ws}"])
            for j in range(2):
                s = cnt % 2
                cnt += 1
                for k in range(8):
                    op("pe", lambda e: e.matmul(psA[s][:], lhsT=srep[:, j, k, :], rhs=wst[ws][:, k, :],
                                                start=(k == 0), stop=False),
                       reads=["srep", f"wst{ws}"], writes=[f"psA{s}"], pe_acc=(k > 0))
                op("pe", lambda e: e.matmul(psA[s][:], lhsT=ones_f[0:1, :], rhs=brow[0:1, nb * 512:(nb + 1) * 512],
                                            start=False, stop=True),
                   reads=["ones_f", "brow"], writes=[f"psA{s}"], pe_acc=True)
                op("act" if j == 0 else "dve",
                   (lambda e: e.copy(out=modsb[j][:, nb * 512:(nb + 1) * 512], in_=psA[s][:])) if j == 0 else
                   (lambda e: e.tensor_copy(out=modsb[j][:, nb * 512:(nb + 1) * 512], in_=psA[s][:])),
                   reads=[f"psA{s}"], writes=[f"modsb{j}"])
        for j in range(2):
            m = modsb[j]
            op("dve", lambda e: e.scalar_tensor_tensor(out=m[:, D:2 * D], in0=m[:, D:2 * D], scalar=1.0,
                                                       in1=grep[:, 0:D], op0=ALU.add, op1=ALU.mult),
               reads=[f"modsb{j}", "grep"], writes=[f"modsb{j}"])
            op("dve", lambda e: e.scalar_tensor_tensor(out=m[:, 4 * D:5 * D], in0=m[:, 4 * D:5 * D], scalar=1.0,
                                                       in1=grep[:, D:2 * D], op0=ALU.add, op1=ALU.mult),
               reads=[f"modsb{j}", "grep"], writes=[f"modsb{j}"])
            for dst, src in ((0, 1), (1, 0), (2, 2), (3, 4), (4, 3), (5, 5)):
                dma(MOD[j, dst], m[:, src * D:(src + 1) * D], reads=[f"modsb{j}"], writes=[f"D:MOD{j}_{dst}"],
                    q="pool")
        fw.release(mk)
        if stop_after == ("A", li):
            break

        mk = fw.mark()
        hT = fw.sbuf("hT", [128, 8, T], BF16)
        winb = fw.sbuf("winb", [128, 8, 2224], BF16)
        wstg = [fw.sbuf(f"wstg{s}", [128, 8, 555]) for s in range(2)]
        A1 = [fw.sbuf(f"A1_{j}", [128, D]) for j in range(2)]
        B1 = [fw.sbuf(f"B1_{j}", [128, D]) for j in range(2)]
        xt = [fw.sbuf(f"xt{s}", [128, D]) for s in range(2)]
        tmp = [fw.sbuf(f"tmp{s}", [128, D]) for s in range(2)]
        hb = [fw.sbuf(f"hb{s}", [128, D], BF16) for s in range(2)]
        junk = fw.sbuf("junk", [128, D], BF16)
        ss = fw.sbuf("ss", [128, NCH])
        rstd = fw.sbuf("rstd", [128, NCH])
        dtst = fw.sbuf("dtst", [128, NCH, 12])
        osb = [fw.sbuf(f"osb{s}", [128, 512]) for s in range(3)]
        pst = [fw.psum(f"pst{s}", [128, 8, 128], BF16) for s in range(2)]
        psd = [fw.psum(f"psd{s}", [128, 16]) for s in range(2)]
        pso = [fw.psum(f"pso{s}", [128, 512]) for s in range(3)]

        for j in range(2):
            dma(A1[j][:], MOD[j, 0], reads=[f"D:MOD{j}_0"], writes=[f"A1_{j}"])
            dma(B1[j][:], MOD[j, 1], reads=[f"D:MOD{j}_1"], writes=[f"B1_{j}"])
        for q4 in range(4):
            s = q4 % 2
            dma(wstg[s][:], w_in[li, :, q4 * 555:(q4 + 1) * 555].rearrange("(k p) n -> p k n", p=128),
                writes=[f"wstg{s}"], q="pool")
            op("any", lambda e: (e.copy if e is nc.scalar else e.tensor_copy)(
                out=winb[:, :, q4 * 555:(q4 + 1) * 555], in_=wstg[s][:]),
               reads=[f"wstg{s}"], writes=["winb"])

        for ci in range(NCH):
            s = ci % 2
            kind = 1 if ci < 2 else 0
            dma(xt[s][:], XR[ci * 128:(ci + 1) * 128, :], reads=[f"D:XR{ci}"], writes=[f"xt{s}"])
            op("act", lambda e: e.activation(out=junk[:], in_=xt[s][:], func=AF.Square,
                                             accum_out=ss[:, ci:ci + 1]),
               reads=[f"xt{s}"], writes=["junk", f"ss{ci}"])
            op("dve", lambda e: e.tensor_scalar(out=rstd[:, ci:ci + 1], in0=ss[:, ci:ci + 1],
                                                scalar1=1.0 / D, scalar2=EPS, op0=ALU.mult, op1=ALU.add),
               reads=[f"ss{ci}"], writes=[f"rstd{ci}"])
            op("act", lambda e: e.sqrt(out=rstd[:, ci:ci + 1], in_=rstd[:, ci:ci + 1]),
               reads=[f"rstd{ci}"], writes=[f"rstd{ci}"])
            op("dve", lambda e: e.reciprocal(out=rstd[:, ci:ci + 1], in_=rstd[:, ci:ci + 1]),
               reads=[f"rstd{ci}"], writes=[f"rstd{ci}"])
            op("dve", lambda e: e.scalar_tensor_tensor(out=tmp[s][:], in0=xt[s][:], scalar=rstd[:, ci:ci + 1],
                                                       in1=A1[kind][:], op0=ALU.mult, op1=ALU.mult),
               reads=[f"xt{s}", f"rstd{ci}", f"A1_{kind}"], writes=[f"tmp{s}"])
            op("pool", lambda e: e.tensor_tensor(out=hb[s][:], in0=tmp[s][:], in1=B1[kind][:], op=ALU.add),
               reads=[f"tmp{s}", f"B1_{kind}"], writes=[f"hb{s}"])
            for k in range(8):
                op("pe", lambda e: e.transpose(out=pst[s][:, k, :], in_=hb[s][:, k * 128:(k + 1) * 128],
                                               identity=ident_b[:]),
                   reads=[f"hb{s}", "ident_b"], writes=[f"pst{s}"], pe_acc=(k > 0))
            op("act", lambda e: e.copy(out=hT[:, :, ci * 128:(ci + 1) * 128], in_=pst[s][:]),
               reads=[f"pst{s}"], writes=[f"hT{ci}"])
            for k in range(8):
                op("pe", lambda e: e.matmul(psd[s][:, 0:12], lhsT=hT[:, k, ci * 128:(ci + 1) * 128],
                                            rhs=winb[:, k, O_DT:O_DT + 12], start=(k == 0), stop=(k == 7)),
                   reads=[f"hT{ci}", "winb"], writes=[f"psd{s}"], pe_acc=(k > 0))
            op("dve", lambda e: e.tensor_copy(out=dtst[:, ci, :], in_=psd[s][:, 0:12]),
               reads=[f"psd{s}"], writes=["dtst"])
        dma(DTRAW.rearrange("(c p) n -> p c n", p=128), dtst[:], reads=["dtst"], writes=["D:DTRAW"], q="pool")

        col_tiles = [(0, 128), (128, 128), (256, 128), (384, 32)]
        col_tiles += [(O_Z + 128 * i, 128) for i in range(3)]
        col_tiles += [(O_XBC + 128 * i, 128) for i in range(5)]
        col_tiles += [(O_HY + 128 * i, 128) for i in range(6)]
        tblocks = [(t0, min(512, T - t0)) for t0 in range(0, T, 512)]
        cnt = 0
        for (c0, cw) in col_tiles:
            for (t0, tn) in tblocks:
                s = cnt % 3
                cnt += 1
                rk = [f"hT{ci}" for ci in range(t0 // 128, (t0 + tn) // 128)] + ["winb"]
                for k in range(8):
                    op("pe", lambda e: e.matmul(pso[s][0:cw, 0:tn], lhsT=winb[:, k, c0:c0 + cw],
                                                rhs=hT[:, k, t0:t0 + tn], start=(k == 0), stop=(k == 7)),
                       reads=rk, writes=[f"pso{s}"], pe_acc=(k > 0))
                if cnt % 2:
                    op("act", lambda e: e.copy(out=osb[s][0:cw, 0:tn], in_=pso[s][0:cw, 0:tn]),
                       reads=[f"pso{s}"], writes=[f"osb{s}"])
                else:
                    op("dve", lambda e: e.tensor_copy(out=osb[s][0:cw, 0:tn], in_=pso[s][0:cw, 0:tn]),
                       reads=[f"pso{s}"], writes=[f"osb{s}"])
                dma(PT[c0:c0 + cw, t0:t0 + tn], osb[s][0:cw, 0:tn], reads=[f"osb{s}"],
                    writes=[f"D:PT{c0}_{t0}"], q="pool" if cnt % 2 else "sp")
        fw.release(mk)
        if stop_after == ("B", li):
            break
        G = dict(P=P, fw=fw, nc=nc, li=li, XR=XR, MOD=MOD, PT=PT, DTRAW=DTRAW, OUT=OUT,
                 ident_f=ident_f, ident_b=ident_b, ones_f=ones_f, cst=cst, dstop=DSTOP[0])
        if not SKIP_C[0]:
            phase_C(G)
        if stop_after == ("C", li):
            break
        if ENABLE_SSD[0]:
            phase_D(G)
        else:
            _zero_fill(G, "SSMT", SSM_INNER)
        if stop_after == ("D", li):
            break
        if ENABLE_HYENA[0]:
            phase_E(G)
        else:
            _zero_fill(G, "HYT", HY_CH)
        if stop_after == ("E", li):
            break
        phase_F(G)
        if stop_after == ("F", li):
            break

    fw.barrier()
    fw.close()
    return P


def make_inputs(inputs, core):
    b = core % 4
    m = {}
    m["xcat"] = np.ascontiguousarray(np.concatenate([inputs["ctx"][b], inputs["x"][b]], axis=0))
    m["cvec"] = np.ascontiguousarray(np.stack([inputs["c"][b], inputs["c_ctx"]], axis=0))
    for k in ("w_mod", "b_mod", "g_norm_mix", "g_norm_mlp", "w_in", "w_out", "g_cq", "g_ckv", "w_uq", "w_ukv",
              "g_qhead", "g_khead", "w_conv_ssm", "b_conv_ssm", "a_log", "dt_bias", "d_skip_ssm", "g_ssm_out",
              "w_conv_hy", "b_conv_hy", "w_f1", "b_f1", "freq_f1", "w_f2", "b_f2", "freq_f2", "w_f3",
              "d_skip_hy", "w_ff1", "w_ff2"):
        m[k] = np.ascontiguousarray(inputs[k])
    m["a_log"] = m["a_log"].reshape(DEPTH, 12); m["dt_bias"] = m["dt_bias"].reshape(DEPTH, 12)
    m["hy_vecs"] = np.ascontiguousarray(np.stack([inputs["b_f1"], inputs["freq_f1"], inputs["b_f2"], inputs["freq_f2"]], axis=1))
    m["d_skip_col"] = np.ascontiguousarray(np.repeat(inputs["d_skip_ssm"], 64, axis=1))
    m.update(_consts())
    return m


def kernel(**inputs):
    inputs = {k: np.asarray(v) for k, v in inputs.items()}
    P = build()
    in_maps = [make_inputs(inputs, c) for c in range(8)]
    in_maps = [{k: v for k, v in m.items() if k in P.din} for m in in_maps]
    res = run_bass_kernel_spmd(P.nc, in_maps, core_ids=list(range(8)))
    return np.stack([res.results[b]["out"] for b in range(4)], axis=0).astype(np.float32)


def _inp(P, name, shape, dt=F32):
    if name in P.din:
        return P.din[name]
    return P.inp(name, shape, dt)


def _scr(P, name, shape, dt=F32):
    if name in P.dscr:
        return P.dscr[name]
    return P.scratch(name, shape, dt)


def _interleave(gens):
    gens = list(gens)
    while gens:
        for g in list(gens):
            try:
                next(g)
            except StopIteration:
                gens.remove(g)


def _rsqrt(fw, dst, src, scale, rk, wk, reads_extra=()):
    fw.op("dve", lambda e: e.tensor_scalar(out=dst, in0=src, scalar1=scale, scalar2=EPS,
                                           op0=ALU.mult, op1=ALU.add), reads=[rk] + list(reads_extra), writes=[wk])
    fw.op("act", lambda e: e.sqrt(out=dst, in_=dst), reads=[wk], writes=[wk])
    fw.op("dve", lambda e: e.reciprocal(out=dst, in_=dst), reads=[wk], writes=[wk])


def _load_cast(fw, dst_bf, src_ap, stg, key_dst, key_stg, q="sp"):
    fw.dma(stg, src_ap, writes=[key_stg], q=q)
    fw.op("any", lambda e: (e.copy if e is fw.nc.scalar else e.tensor_copy)(out=dst_bf, in_=stg),
          reads=[key_stg], writes=[key_dst])


def phase_C(G):
    P, fw, nc, li = G["P"], G["fw"], G["nc"], G["li"]
    op, dma = fw.op, fw.dma
    PT = G["PT"]
    ones_f = G["ones_f"]
    last = li == DEPTH - 1
    g_cq = _inp(P, "g_cq", [DEPTH, 256]); g_ckv = _inp(P, "g_ckv", [DEPTH, 128])
    w_uq = _inp(P, "w_uq", [DEPTH, 256, 576]); w_ukv = _inp(P, "w_ukv", [DEPTH, 128, 768])
    g_qh = _inp(P, "g_qhead", [DEPTH, 96]); g_kh = _inp(P, "g_khead", [DEPTH, 96])
    COS = _inp(P, "rope_cos", [96, T]); SIN = _inp(P, "rope_sin", [96, T])
    RTd = _inp(P, "rope_RT", [96, 96]); PKd = _inp(P, "pk_sel", [32, 96])
    ATT = _scr(P, "ATT", [NH, 64, T], BF16)
    scale = 1.0 / math.sqrt(QK)

    mk0 = fw.mark()
    qT = fw.sbuf("qT", [96, NH, T], BF16)
    kT = fw.sbuf("kT", [96, NH, T], BF16)
    V1 = fw.sbuf("V1", [128, NCH, NH, 65], BF16)
    op("pool", lambda e: e.memset(V1[:], 1.0), writes=["V1"])

    mk = fw.mark()
    wuq_b = fw.sbuf("wuq_b", [128, 2, 576], BF16)
    wk_b = fw.sbuf("wk_b", [128, NH, 96], BF16)
    wv_b = fw.sbuf("wv_b", [128, NH, 64], BF16)
    pk_b = fw.sbuf("pk_b", [32, 96], BF16)
    stg = fw.sbuf("stg", [128, 2, 768])
    RT = fw.sbuf("RT", [96, 96])
    gcq = fw.sbuf("gcq", [128, 2]); gckv = fw.sbuf("gckv", [128, 1])
    gq = fw.sbuf("gq", [96, 1]); gk = fw.sbuf("gk", [96, 1])
    dma(stg[:, :, 0:576], w_uq[li].rearrange("(k p) n -> p k n", p=128), writes=["stg"])
    op("dve", lambda e: e.tensor_copy(out=wuq_b[:], in_=stg[:, :, 0:576]), reads=["stg"], writes=["wuq_b"])
    dma(stg[:, 0, :], w_ukv[li], reads=[], writes=["stg"])
    op("pool", lambda e: e.memset(wk_b[:], 0.0), writes=["wk_b"])
    op("dve", lambda e: e.tensor_copy(out=wk_b[:, :, 0:64],
                                      in_=stg[:, 0, :].rearrange("p (h c) -> p h c", c=128)[:, :, 0:64]),
       reads=["stg"], writes=["wk_b"])
    op("dve", lambda e: e.tensor_copy(out=wv_b[:],
                                      in_=stg[:, 0, :].rearrange("p (h c) -> p h c", c=128)[:, :, 64:128]),
       reads=["stg"], writes=["wv_b"])
    dma(stg[0:32, 1, 0:96], PKd, writes=["stg1"])
    op("dve", lambda e: e.tensor_copy(out=pk_b[:], in_=stg[0:32, 1, 0:96]), reads=["stg1"], writes=["pk_b"])
    dma(RT[:], RTd, writes=["RT"])
    dma(gcq[:], g_cq[li].rearrange("(k p) -> p k", p=128), writes=["gcq"], allow_slow_non_contiguous=True)
    dma(gckv[:], g_ckv[li].rearrange("(p o) -> p o", o=1), writes=["gckv"], allow_slow_non_contiguous=True)
    dma(gq[:], g_qh[li].rearrange("(p o) -> p o", o=1), writes=["gq"], allow_slow_non_contiguous=True)
    dma(gk[:], g_kh[li].rearrange("(p o) -> p o", o=1), writes=["gk"], allow_slow_non_contiguous=True)

    cqb = [fw.sbuf(f"cqb{s}", [128, 2, 512]) for s in range(2)]
    ckb = [fw.sbuf(f"ckb{s}", [128, 512]) for s in range(2)]
    krb = [fw.sbuf(f"krb{s}", [32, 512]) for s in range(2)]
    krb_b = fw.sbuf("krb_b", [32, 512], BF16)
    sq = fw.sbuf("sq", [128, 2, 512])
    rs = fw.sbuf("rs", [128, 512])
    cqn = fw.sbuf("cqn", [128, 2, 512], BF16)
    ckn = fw.sbuf("ckn", [128, 512], BF16)
    cosb = [fw.sbuf(f"cosb{s}", [96, 512]) for s in range(2)]
    sinb = [fw.sbuf(f"sinb{s}", [96, 512]) for s in range(2)]
    hsq = [fw.sbuf(f"hsq{s}", [96, 512]) for s in range(2)]
    hrs = [fw.sbuf(f"hrs{s}", [96, 512]) for s in range(2)]
    hn = [fw.sbuf(f"hn{s}", [96, 512]) for s in range(2)]
    ht1 = [fw.sbuf(f"ht1{s}", [96, 512]) for s in range(2)]
    ht2 = [fw.sbuf(f"ht2{s}", [96, 512]) for s in range(2)]
    ps1 = fw.psum("ps1", [128, 512])
    psh = [fw.psum(f"psh{s}", [96, 512]) for s in range(2)]
    ps2 = [fw.psum(f"ps2{s}", [96, 512]) for s in range(2)]
    psr = [fw.psum(f"psr{s}", [96, 512]) for s in range(2)]
    psv = fw.psum("psv", [128, 384])

    def head_chain(kind, h, s, t0, tn, bs, bi):
        pk = f"psh{s}"
        if kind == "q":
            for k in range(2):
                op("pe", lambda e: e.matmul(psh[s][:, 0:tn], lhsT=wuq_b[:, k, h * 96:(h + 1) * 96],
                                            rhs=cqn[:, k, 0:tn], start=(k == 0), stop=(k == 1)),
                   reads=["cqn", "wuq_b"], writes=[pk], pe_acc=(k > 0))
            gcol, gkey, dst, dkey = gq, "gq", qT[:, h, t0:t0 + tn], f"qT{h}_{bi}"
        else:
            op("pe", lambda e: e.matmul(psh[s][:, 0:tn], lhsT=wk_b[:, h, :], rhs=ckn[:, 0:tn], start=True, stop=False),
               reads=["ckn", "wk_b"], writes=[pk])
            op("pe", lambda e: e.matmul(psh[s][:, 0:tn], lhsT=pk_b[:], rhs=krb_b[:, 0:tn], start=False, stop=True),
               reads=["krb_b", "pk_b"], writes=[pk], pe_acc=True)
            gcol, gkey, dst, dkey = gk, "gk", kT[:, h, t0:t0 + tn], f"kT{h}_{bi}"
        yield
        op("act", lambda e: e.activation(out=hsq[s][:, 0:tn], in_=psh[s][:, 0:tn], func=AF.Square),
           reads=[pk], writes=[f"hsq{s}"])
        yield
        op("pe", lambda e: e.matmul(ps2[s][:, 0:tn], lhsT=ones_f[0:96, 0:96], rhs=hsq[s][:, 0:tn],
                                    start=True, stop=True), reads=[f"hsq{s}", "ones_f"], writes=[f"ps2{s}"])
        yield
        op("dve", lambda e: e.tensor_scalar(out=hrs[s][:, 0:tn], in0=ps2[s][:, 0:tn], scalar1=1.0 / QK, scalar2=EPS,
                                            op0=ALU.mult, op1=ALU.add), reads=[f"ps2{s}"], writes=[f"hrs{s}"])
        yield
        op("act", lambda e: e.sqrt(out=hrs[s][:, 0:tn], in_=hrs[s][:, 0:tn]), reads=[f"hrs{s}"], writes=[f"hrs{s}"])
        yield
        op("dve", lambda e: e.reciprocal(out=hrs[s][:, 0:tn], in_=hrs[s][:, 0:tn]), reads=[f"hrs{s}"], writes=[f"hrs{s}"])
        yield
        op("dve", lambda e: e.scalar_tensor_tensor(out=hn[s][:, 0:tn], in0=psh[s][:, 0:tn], scalar=gcol[:, 0:1],
                                                   in1=hrs[s][:, 0:tn], op0=ALU.mult, op1=ALU.mult),
           reads=[pk, gkey, f"hrs{s}"], writes=[f"hn{s}"])
        yield
        op("pe", lambda e: e.matmul(psr[s][:, 0:tn], lhsT=RT[:], rhs=hn[s][:, 0:tn], start=True, stop=True),
           reads=[f"hn{s}", "RT"], writes=[f"psr{s}"])
        op("pool", lambda e: e.tensor_tensor(out=ht1[s][:, 0:tn], in0=hn[s][:, 0:tn], in1=cosb[bs][:, 0:tn],
                                             op=ALU.mult), reads=[f"hn{s}", f"cosb{bs}"], writes=[f"ht1{s}"])
        yield
        op("dve", lambda e: e.tensor_tensor(out=ht2[s][:, 0:tn], in0=psr[s][:, 0:tn], in1=sinb[bs][:, 0:tn],
                                            op=ALU.mult), reads=[f"psr{s}", f"sinb{bs}"], writes=[f"ht2{s}"])
        yield
        op("pool", lambda e: e.tensor_tensor(out=dst, in0=ht1[s][:, 0:tn], in1=ht2[s][:, 0:tn], op=ALU.add),
           reads=[f"ht1{s}", f"ht2{s}"], writes=[dkey])
        yield

    tblocks = [(t0, min(512, T - t0)) for t0 in range(0, T, 512)]
    for bi, (t0, tn) in enumerate(tblocks):
        bs = bi % 2
        dma(cqb[bs][:, :, 0:tn], PT[0:256, t0:t0 + tn].rearrange("(k p) t -> p k t", p=128),
            reads=[f"D:PT{c}_{t0}" for c in (0, 128)], writes=[f"cqb{bs}"])
        dma(ckb[bs][:, 0:tn], PT[256:384, t0:t0 + tn], reads=[f"D:PT256_{t0}"], writes=[f"ckb{bs}"])
        dma(krb[bs][:, 0:tn], PT[384:416, t0:t0 + tn], reads=[f"D:PT384_{t0}"], writes=[f"krb{bs}"])
        dma(cosb[bs][:, 0:tn], COS[:, t0:t0 + tn], writes=[f"cosb{bs}"], q="pool")
        dma(sinb[bs][:, 0:tn], SIN[:, t0:t0 + tn], writes=[f"sinb{bs}"], q="pool")
        op("act", lambda e: e.activation(out=sq[:, :, 0:tn], in_=cqb[bs][:, :, 0:tn], func=AF.Square),
           reads=[f"cqb{bs}"], writes=["sq"])
        for k in range(2):
            op("pe", lambda e: e.matmul(ps1[:, 0:tn], lhsT=ones_f[:], rhs=sq[:, k, 0:tn], start=(k == 0), stop=(k == 1)),
               reads=["sq", "ones_f"], writes=["ps1"], pe_acc=(k > 0))
        _rsqrt(fw, rs[:, 0:tn], ps1[:, 0:tn], 1.0 / 256, "ps1", "rs")
        for k in range(2):
            op("dve", lambda e: e.scalar_tensor_tensor(out=cqn[:, k, 0:tn], in0=cqb[bs][:, k, 0:tn],
                                                       scalar=gcq[:, k:k + 1], in1=rs[:, 0:tn],
                                                       op0=ALU.mult, op1=ALU.mult),
               reads=[f"cqb{bs}", "gcq", "rs"], writes=["cqn"])
        op("act", lambda e: e.activation(out=sq[:, 0, 0:tn], in_=ckb[bs][:, 0:tn], func=AF.Square),
           reads=[f"ckb{bs}"], writes=["sq"])
        op("pe", lambda e: e.matmul(ps1[:, 0:tn], lhsT=ones_f[:], rhs=sq[:, 0, 0:tn], start=True, stop=True),
           reads=["sq", "ones_f"], writes=["ps1"])
        _rsqrt(fw, rs[:, 0:tn], ps1[:, 0:tn], 1.0 / 128, "ps1", "rs")
        op("dve", lambda e: e.scalar_tensor_tensor(out=ckn[:, 0:tn], in0=ckb[bs][:, 0:tn], scalar=gckv[:, 0:1],
                                                   in1=rs[:, 0:tn], op0=ALU.mult, op1=ALU.mult),
           reads=[f"ckb{bs}", "gckv", "rs"], writes=["ckn"])
        op("act", lambda e: e.copy(out=krb_b[:, 0:tn], in_=krb[bs][:, 0:tn]), reads=[f"krb{bs}"], writes=["krb_b"])
        for kind in ("q", "k"):
            for h in range(0, NH, 2):
                _interleave([head_chain(kind, h, 0, t0, tn, bs, bi), head_chain(kind, h + 1, 1, t0, tn, bs, bi)])
        for c in range(tn // 128):
            ci = t0 // 128 + c
            op("pe", lambda e: e.matmul(psv[:], lhsT=ckn[:, c * 128:(c + 1) * 128],
                                        rhs=wv_b[:].rearrange("p h c -> p (h c)"), start=True, stop=True),
               reads=["ckn", "wv_b"], writes=["psv"])
            op("act", lambda e: e.copy(out=V1[:, ci, :, 0:64], in_=psv[:].rearrange("p (h c) -> p h c", c=64)),
               reads=["psv"], writes=["V1", f"V1_{ci}"])
    fw.release(mk)

    mk = fw.mark()
    NS = 4
    psS = [fw.psum(f"psS{s}", [128, 512]) for s in range(NS)]
    acc = [fw.psum(f"acc{s}", [65, 512]) for s in range(2)]
    psb = fw.psum("psb", [64, 512])
    pT = [fw.sbuf(f"pT{s}", [128, 512], BF16) for s in range(NS)]
    osb = [fw.sbuf(f"aosb{s}", [65, 512]) for s in range(2)]
    rden = [fw.sbuf(f"rden{s}", [65, 512]) for s in range(2)]
    att = [fw.sbuf(f"att{s}", [64, 512], BF16) for s in range(2)]

    groups = []
    if not last:
        groups.append((0, 256, [0, 1]))
    for g in range(4):
        groups.append((CTX + g * 1024, 1024, list(range(NCH))))
    it = 0
    oc = 0
    for (q0, qn, tks) in groups:
        halves = [(q0 + o, min(512, qn - o)) for o in range(0, qn, 512)]
        for h in range(NH):
            items = [(ti, tkc, hi, qs, hn_) for ti, tkc in enumerate(tks) for hi, (qs, hn_) in enumerate(halves)]
            LA = 2
            for idx in range(len(items) + LA):
                if idx < len(items):
                    ti, tkc, hi, qs, hn_ = items[idx]
                    s = (it + idx) % NS
                    op("pe", lambda e: e.matmul(psS[s][:, 0:hn_], lhsT=kT[:, h, tkc * 128:(tkc + 1) * 128],
                                                rhs=qT[:, h, qs:qs + hn_], start=True, stop=True),
                       reads=[], writes=[f"psS{s}"])
                if idx >= LA:
                    ti, tkc, hi, qs, hn_ = items[idx - LA]
                    s = (it + idx - LA) % NS
                    op("act", lambda e: e.activation(out=pT[s][:, 0:hn_], in_=psS[s][:, 0:hn_], func=AF.Exp,
                                                     scale=scale), reads=[f"psS{s}"], writes=[f"pT{s}"])
                    op("pe", lambda e: e.matmul(acc[hi][:, 0:hn_], lhsT=V1[:, tkc, h, :], rhs=pT[s][:, 0:hn_],
                                                start=(ti == 0), stop=(ti == len(tks) - 1)),
                       reads=[f"pT{s}"], writes=[f"acc{hi}"], pe_acc=(ti > 0))
            it += len(items)
            for hi, (qs, hn_) in enumerate(halves):
                o = oc % 2
                oc += 1
                op("dve", lambda e: e.tensor_copy(out=osb[o][:, 0:hn_], in_=acc[hi][:, 0:hn_]),
                   reads=[f"acc{hi}"], writes=[f"aosb{o}"])
                op("dve", lambda e: e.reciprocal(out=rden[o][64:65, 0:hn_], in_=osb[o][64:65, 0:hn_]),
                   reads=[f"aosb{o}"], writes=[f"rden{o}"])
                op("pe", lambda e: e.matmul(psb[:, 0:hn_], lhsT=ones_f[64:65, 0:64], rhs=rden[o][64:65, 0:hn_],
                                            start=True, stop=True), reads=[f"rden{o}", "ones_f"], writes=["psb"])
                op("dve", lambda e: e.tensor_tensor(out=att[o][:, 0:hn_], in0=osb[o][0:64, 0:hn_],
                                                    in1=psb[:, 0:hn_], op=ALU.mult),
                   reads=[f"aosb{o}", "psb"], writes=[f"att{o}"])
                dma(ATT[h, :, qs:qs + hn_], att[o][:, 0:hn_], reads=[f"att{o}"], writes=[f"D:ATT{h}_{qs}"],
                    q="pool")
    fw.release(mk)
    fw.release(mk0)


def _zero_fill(G, name, rows):
    P, fw = G["P"], G["fw"]
    dst = _scr(P, name, [rows, T], BF16)
    mk = fw.mark()
    zt = fw.sbuf("zfill", [128, 512], BF16)
    fw.op("pool", lambda e: e.memset(zt[:], 0.0), writes=["zfill"])
    for r in range(rows // 128):
        for t0 in range(0, T, 512):
            tn = min(512, T - t0)
            fw.dma(dst[r * 128:(r + 1) * 128, t0:t0 + tn], zt[:, 0:tn], reads=["zfill"],
                   writes=[f"D:{name}{r}_{t0}"], q="pool")
    fw.release(mk)


def phase_D(G):
    P, fw, nc, li = G["P"], G["fw"], G["nc"], G["li"]
    op, dma = fw.op, fw.dma
    PT, DTRAW = G["PT"], G["DTRAW"]
    ones_f, ident_f, ident_b = G["ones_f"], G["ident_f"], G["ident_b"]
    w_conv = _inp(P, "w_conv_ssm", [DEPTH, 3, 640]); b_conv = _inp(P, "b_conv_ssm", [DEPTH, 640])
    a_log = _inp(P, "a_log", [DEPTH, 12]); dt_bias = _inp(P, "dt_bias", [DEPTH, 12])
    dcol_d = _inp(P, "d_skip_col", [DEPTH, 384]); g_so = _inp(P, "g_ssm_out", [DEPTH, 384])
    TRI = _inp(P, "tri", [2, 128, 128]); NEG = _inp(P, "negmask", [2, 128, 128])
    GMd = _inp(P, "gmask", [9, 128, 128])
    YF = _scr(P, "YF", [2, 384, T])
    SSMT = _scr(P, "SSMT", [SSM_INNER, T], BF16)
    W = T + 4

    mk0 = fw.mark()
    xsT = fw.sbuf("xsT", [128, 3, T])
    BC = [fw.sbuf(f"BC{i}", [64, T], BF16) for i in range(4)]
    xs_tok = fw.sbuf("xs_tok", [128, NCH, 384], BF16)
    B_tok = fw.sbuf("B_tok", [128, NCH, 128], BF16)
    wc = fw.sbuf("wc", [128, 5, 3]); bc = fw.sbuf("bc", [128, 5])
    wc2 = fw.sbuf("wc2", [64, 4, 3]); bc2 = fw.sbuf("bc2", [64, 4])
    for k in range(3):
        dma(wc[:, :, k], w_conv[li, k].rearrange("(r p) -> p r", p=128), writes=["wc"], allow_slow_non_contiguous=True)
        dma(wc2[:, :, k], w_conv[li, k, 384:640].rearrange("(r p) -> p r", p=64), writes=["wc2"],
            allow_slow_non_contiguous=True)
    dma(bc[:], b_conv[li].rearrange("(r p) -> p r", p=128), writes=["bc"], allow_slow_non_contiguous=True)
    dma(bc2[:], b_conv[li, 384:640].rearrange("(r p) -> p r", p=64), writes=["bc2"], allow_slow_non_contiguous=True)

    mk = fw.mark()
    xb = fw.sbuf("xb", [128, W]); acc = fw.sbuf("cacc", [128, W])
    op("pool", lambda e: e.memset(xb[:], 0.0), writes=["xb"])
    jobs = [(128, O_XBC + 128 * r, wc[:, r, :], bc[:, r:r + 1], ("xs", r)) for r in range(3)]
    jobs += [(64, O_XBC + 384 + 64 * i, wc2[:, i, :], bc2[:, i:i + 1], ("bc", i)) for i in range(4)]
    for (np_, r0, wv, bv, dst) in jobs:
        rk = [k for k in fw.lastw if k.startswith("D:PT") and r0 - 127 <= int(k[4:].split("_")[0]) <= r0 + np_ - 1]
        dma(xb[0:np_, 1:1 + CTX], PT[r0:r0 + np_, 0:CTX], reads=rk, writes=["xb"])
        dma(xb[0:np_, 3 + CTX:3 + T], PT[r0:r0 + np_, CTX:T], reads=rk, writes=["xb"], q="pool")
        op("dve", lambda e: e.tensor_scalar(out=acc[0:np_, 0:W - 2], in0=xb[0:np_, 0:W - 2], scalar1=wv[:, 0:1],
                                            scalar2=None, op0=ALU.mult), reads=["xb", "wc", "wc2"], writes=["cacc"])
        for k in (1, 2):
            op("dve", lambda e: e.scalar_tensor_tensor(out=acc[0:np_, 0:W - 2], in0=xb[0:np_, k:W - 2 + k],
                                                       scalar=wv[:, k:k + 1], in1=acc[0:np_, 0:W - 2],
                                                       op0=ALU.mult, op1=ALU.add),
               reads=["xb", "cacc", "wc", "wc2"], writes=["cacc"])
        for (a0, a1, d0) in ((0, CTX, 0), (2 + CTX, 2 + T, CTX)):
            if dst[0] == "xs":
                o_ = xsT[:, dst[1], d0:d0 + (a1 - a0)]
            else:
                o_ = BC[dst[1]][:, d0:d0 + (a1 - a0)]
            op("act", lambda e: e.activation(out=o_, in_=acc[0:np_, a0:a1], func=AF.Silu, bias=bv),
               reads=["cacc", "bc", "bc2"], writes=["xsT" if dst[0] == "xs" else f"BC{dst[1]}"])
    fw.release(mk)

    if G.get("dstop") == 1:
        fw.release(mk0); return
    mk = fw.mark()
    ptx_ = [fw.psum(f"ptx{s}", [128, 512]) for s in range(2)]
    ptx = [t[:, 0:384].rearrange("p (r c) -> p r c", c=128) for t in ptx_]
    ptb_ = [fw.psum(f"ptb{s}", [128, 1024], BF16) for s in range(2)]
    ptb = [t[:, 0:128].rearrange("p (g c) -> p g c", c=64) for t in ptb_]
    for ci in range(NCH):
        s = ci % 2
        for r in range(3):
            op("pe", lambda e: e.transpose(out=ptx[s][:, r, :], in_=xsT[:, r, ci * 128:(ci + 1) * 128],
                                           identity=ident_f[:]), reads=["xsT", "ident_f"], writes=[f"ptx{s}"],
               pe_acc=(r > 0))
        op("act", lambda e: e.copy(out=xs_tok[:, ci, :], in_=ptx_[s][:, 0:384]),
           reads=[f"ptx{s}"], writes=["xs_tok"])
        for g in range(2):
            op("pe", lambda e: e.transpose(out=ptb[s][:, g, :], in_=BC[g][:, ci * 128:(ci + 1) * 128],
                                           identity=ident_b[0:64, 0:64]), reads=[f"BC{g}", "ident_b"],
               writes=[f"ptb{s}"], pe_acc=(g > 0))
        op("dve", lambda e: e.tensor_copy(out=B_tok[:, ci, :], in_=ptb_[s][:, 0:128]),
           reads=[f"ptb{s}"], writes=["B_tok"])
    fw.release(mk)

    tri = fw.sbuf("tri", [128, 2, 128]); neg = fw.sbuf("neg", [128, 2, 128])
    dma(tri[:], TRI.rearrange("d p i -> p d i"), writes=["tri"])
    dma(neg[:], NEG.rearrange("d p i -> p d i"), writes=["neg"])
    rows = fw.sbuf("rows", [1, 24])
    dma(rows[:, 0:12], a_log[li:li + 1, :], writes=["rows"])
    dma(rows[:, 12:24], dt_bias[li:li + 1, :], writes=["rows"])
    arep = fw.sbuf("arep", [128, 24])
    dtr = fw.sbuf("dtr", [128, NCH, 12]); dt = fw.sbuf("dt", [128, NCH, 12]); dta = fw.sbuf("dta", [128, NCH, 12])
    cum = fw.sbuf("cum", [128, NCH, 12]); tot = fw.sbuf("tot", [128, NCH, 12])
    wend = fw.sbuf("wend", [128, NCH, 12]); cdec = fw.sbuf("cdec", [128, NCH, 12]); ncum = fw.sbuf("ncum", [128, NCH, 12])
    mk = fw.mark()
    psr_ = fw.psum("psrow", [128, 512])[:, 0:24]
    psc = fw.psum("psc", [128, 512])[:, 0:NCH * 12].rearrange("p (c n) -> p c n", n=12)
    op("pe", lambda e: e.matmul(psr_[:], lhsT=ones_f[0:1, :], rhs=rows[0:1, :], start=True, stop=True),
       reads=["rows", "ones_f"], writes=["psrow"])
    op("act", lambda e: e.activation(out=arep[:, 0:12], in_=psr_[:, 0:12], func=AF.Exp), reads=["psrow"], writes=["arep"])
    op("dve", lambda e: e.tensor_scalar(out=arep[:, 0:12], in0=arep[:, 0:12], scalar1=-1.0, scalar2=None, op0=ALU.mult),
       reads=["arep"], writes=["arep"])
    op("dve", lambda e: e.tensor_copy(out=arep[:, 12:24], in_=psr_[:, 12:24]), reads=["psrow"], writes=["arep"])
    dma(dtr[:], DTRAW.rearrange("(c p) n -> p c n", p=128), reads=["D:DTRAW"], writes=["dtr"])
    for ci in range(NCH):
        op("dve", lambda e: e.tensor_tensor(out=dtr[:, ci, :], in0=dtr[:, ci, :], in1=arep[:, 12:24], op=ALU.add),
           reads=["dtr", "arep"], writes=["dtr"])
    op("act", lambda e: e.activation(out=dt[:], in_=dtr[:], func=AF.Exp), reads=["dtr"], writes=["dt"])
    op("act", lambda e: e.activation(out=dt[:], in_=dt[:], func=AF.Ln, bias=1.0), reads=["dt"], writes=["dt"])
    for ci in range(NCH):
        op("dve", lambda e: e.tensor_tensor(out=dta[:, ci, :], in0=dt[:, ci, :], in1=arep[:, 0:12], op=ALU.mult),
           reads=["dt", "arep"], writes=["dta"])
    for d in range(2):
        op("pe", lambda e: e.matmul(psc[:], lhsT=tri[:, d, :], rhs=dta[:], start=True, stop=True),
           reads=["tri", "dta", "cum"], writes=["psc"])
        op("dve", lambda e: e.tensor_copy(out=cum[:, :, d * 6:(d + 1) * 6], in_=psc[:, :, d * 6:(d + 1) * 6]),
           reads=["psc"], writes=["cum"])
    op("pe", lambda e: e.matmul(psc[:], lhsT=ones_f[:], rhs=dta[:], start=True, stop=True),
       reads=["ones_f", "dta", "cum"], writes=["psc"])
    op("dve", lambda e: e.tensor_copy(out=tot[:], in_=psc[:]), reads=["psc"], writes=["tot"])
    op("dve", lambda e: e.tensor_tensor(out=wend[:], in0=tot[:], in1=cum[:], op=ALU.subtract),
       reads=["tot", "cum"], writes=["wend"])
    op("act", lambda e: e.activation(out=wend[:], in_=wend[:], func=AF.Exp), reads=["wend"], writes=["wend"])
    op("dve", lambda e: e.tensor_tensor(out=wend[:], in0=wend[:], in1=dt[:], op=ALU.mult), reads=["wend", "dt"], writes=["wend"])
    op("act", lambda e: e.activation(out=cdec[:], in_=tot[:], func=AF.Exp), reads=["tot"], writes=["cdec"])
    op("dve", lambda e: e.tensor_scalar(out=ncum[:], in0=cum[:], scalar1=-1.0, scalar2=None, op0=ALU.mult),
       reads=["cum"], writes=["ncum"])
    fw.release(mk)

    if G.get("dstop") == 2:
        fw.release(mk0); return
    mk = fw.mark()
    S = fw.sbuf("S", [64, NH, 64]); Sb = fw.sbuf("Sb", [64, NH, 64], BF16)
    Rm = [fw.sbuf(f"Rm{s}", [128, 128]) for s in range(2)]
    Em = [fw.sbuf(f"Em{s}", [64, 128]) for s in range(2)]
    sg = [fw.sbuf(f"sg{s}", [128, 128]) for s in range(2)]
    dc = [fw.sbuf(f"dc{s}", [128, 128]) for s in range(2)]
    MT = [fw.sbuf(f"MT{s}", [128, 128], BF16) for s in range(2)]
    CdT = [fw.sbuf(f"CdT{s}", [64, 128], BF16) for s in range(2)]
    Bw = [fw.sbuf(f"Bw{s}", [128, 64], BF16) for s in range(2)]
    ysb = [fw.sbuf(f"ysb{s}", [64, 128]) for s in range(2)]
    psG = [fw.psum(f"psG{g}", [128, 512])[:, 0:128] for g in range(2)]
    psA = [fw.psum(f"psA{s}", [128, 512])[:, 0:128] for s in range(2)]
    psY = [fw.psum(f"psY{s}", [128, 512])[0:64, 0:128] for s in range(2)]
    psS = [fw.psum(f"psSt{s}", [128, 512])[0:64, 0:64] for s in range(2)]
    it = 0
    for d in range(2):
        order = list(range(NCH)) if d == 0 else [1, 0] + list(range(NCH - 1, 1, -1))
        if DFLAG[0] & 16:
            order = order[:2]
        if DFLAG[0] & 32:
            order = order[:10]
        op("pool", lambda e: e.memset(S[:], 0.0), writes=["S"] + [f"S{h}" for h in range(NH)])
        op("pool", lambda e: e.memset(Sb[:], 0.0), writes=["Sb"] + [f"Sb{h}" for h in range(NH)])
        for ci in order:
            c0 = ci * 128
            for g in range(2):
                op("pe", lambda e: e.matmul(psG[g][:], lhsT=BC[g][:, c0:c0 + 128], rhs=BC[2 + g][:, c0:c0 + 128],
                                            start=True, stop=True), reads=[], writes=[f"psG{g}"])
            def ssd_chain(h, s, ci=ci, c0=c0, d=d):
                g = h // 3
                dh = d * 6 + h
                op("dve", lambda e: e.tensor_scalar(out=Rm[s][:], in0=tri[:, d, :], scalar1=dta[:, ci, dh:dh + 1],
                                                    scalar2=None, op0=ALU.mult), reads=[], writes=[f"Rm{s}"])
                yield
                op("pe", lambda e: e.matmul(psA[s][:], lhsT=ones_f[:], rhs=Rm[s][:], start=True, stop=True),
                   reads=[f"Rm{s}"], writes=[f"psA{s}"])
                yield
                op("act", lambda e: e.activation(out=Em[s][:], in_=psA[s][0:64, :], func=AF.Exp),
                   reads=[f"psA{s}"], writes=[f"Em{s}"])
                yield
                op("dve", lambda e: e.tensor_tensor(out=sg[s][:], in0=psA[s][:], in1=neg[:, d, :], op=ALU.add),
                   reads=[f"psA{s}"], writes=[f"sg{s}"])
                yield
                op("act", lambda e: e.activation(out=dc[s][:], in_=sg[s][:], func=AF.Exp, bias=ncum[:, ci, dh:dh + 1]),
                   reads=[f"sg{s}"], writes=[f"dc{s}"])
                op("pool", lambda e: e.tensor_scalar(out=Bw[s][:], in0=B_tok[:, ci, g * 64:(g + 1) * 64],
                                                     scalar1=wend[:, ci, dh:dh + 1], scalar2=None, op0=ALU.mult),
                   reads=[], writes=[f"Bw{s}"])
                yield
                op("dve", lambda e: e.scalar_tensor_tensor(out=MT[s][:], in0=dc[s][:], scalar=dt[:, ci, dh:dh + 1],
                                                           in1=psG[g][:], op0=ALU.mult, op1=ALU.mult),
                   reads=[f"dc{s}", f"psG{g}"], writes=[f"MT{s}"])
                yield
                op("dve", lambda e: e.tensor_tensor(out=CdT[s][:], in0=BC[2 + g][:, c0:c0 + 128], in1=Em[s][:],
                                                    op=ALU.mult), reads=[f"Em{s}"], writes=[f"CdT{s}"])
                yield
                op("pe", lambda e: e.matmul(psY[s][:], lhsT=xs_tok[:, ci, h * 64:(h + 1) * 64], rhs=MT[s][:],
                                            start=True, stop=False), reads=[f"MT{s}"], writes=[f"psY{s}"])
                op("pe", lambda e: e.matmul(psY[s][:], lhsT=Sb[:, h, :], rhs=CdT[s][:], start=False, stop=True),
                   reads=[f"CdT{s}", f"Sb{h}"], writes=[f"psY{s}"], pe_acc=True)
                op("pe", lambda e: e.matmul(psS[s][:], lhsT=Bw[s][:], rhs=xs_tok[:, ci, h * 64:(h + 1) * 64],
                                            start=True, stop=True), reads=[f"Bw{s}"], writes=[f"psSt{s}"])
                yield
                op("act", lambda e: e.copy(out=ysb[s][:], in_=psY[s][:]), reads=[f"psY{s}"], writes=[f"ysb{s}"])
                yield
                dma(YF[d, h * 64:(h + 1) * 64, c0:c0 + 128], ysb[s][:], reads=[f"ysb{s}"],
                    writes=[f"D:YF{d}_{h}_{ci}"], q="sp" if h % 2 else "pool")
                op("dve", lambda e: e.scalar_tensor_tensor(out=S[:, h, :], in0=S[:, h, :],
                                                           scalar=cdec[0:64, ci, dh:dh + 1], in1=psS[s][:],
                                                           op0=ALU.mult, op1=ALU.add),
                   reads=[f"psSt{s}", f"S{h}"], writes=[f"S{h}"])
                yield
                op("act", lambda e: e.copy(out=Sb[:, h, :], in_=S[:, h, :]), reads=[f"S{h}"], writes=[f"Sb{h}"])
                yield

            for h in range(0, NH, 2):
                _interleave([ssd_chain(h, 0), ssd_chain(h + 1, 1)])
    fw.release(mk)

    if G.get("dstop") == 3:
        fw.release(mk0); return
    mk = fw.mark()
    gm = fw.sbuf("gm", [128, 9, 128])
    dcol = fw.sbuf("dcol", [128, 3]); gso = fw.sbuf("gso", [128, 3])
    dma(gm[:], GMd.rearrange("a p i -> p a i"), writes=["gm"])
    dma(dcol[:], dcol_d[li].rearrange("(r p) -> p r", p=128), writes=["dcol"], allow_slow_non_contiguous=True)
    dma(gso[:], g_so[li].rearrange("(r p) -> p r", p=128), writes=["gso"], allow_slow_non_contiguous=True)
    y0 = [fw.sbuf(f"y0_{r}", [128, 512]) for r in range(3)]
    y1 = [fw.sbuf(f"y1_{r}", [128, 512]) for r in range(3)]
    zt = [fw.sbuf(f"zt_{r}", [128, 512]) for r in range(3)]
    sq = [fw.sbuf(f"dsq_{r}", [128, 512]) for r in range(3)]
    rs = fw.sbuf("drs", [128, 512])
    so = [fw.sbuf(f"so{s}", [128, 512], BF16) for s in range(2)]
    psn = [fw.psum(f"psn{s}", [128, 512]) for s in range(2)]
    tblocks = [(t0, min(512, T - t0)) for t0 in range(0, T, 512)]
    n = 0
    for (t0, tn) in tblocks:
        for r in range(3):
            rk = [f"D:YF{d}_{h}_{ci}" for d in range(2) for h in (2 * r, 2 * r + 1)
                  for ci in range(t0 // 128, (t0 + tn) // 128)]
            dma(y0[r][:, 0:tn], YF[0, r * 128:(r + 1) * 128, t0:t0 + tn], reads=rk, writes=[f"y0_{r}"])
            dma(y1[r][:, 0:tn], YF[1, r * 128:(r + 1) * 128, t0:t0 + tn], reads=rk, writes=[f"y1_{r}"], q="pool")
            dma(zt[r][:, 0:tn], PT[O_Z + r * 128:O_Z + (r + 1) * 128, t0:t0 + tn],
                reads=[f"D:PT{O_Z + r * 128}_{t0}"], writes=[f"zt_{r}"])
            op("pool", lambda e: e.tensor_tensor(out=y0[r][:, 0:tn], in0=y0[r][:, 0:tn], in1=y1[r][:, 0:tn], op=ALU.add),
               reads=[f"y0_{r}", f"y1_{r}"], writes=[f"y0_{r}"])
            op("dve", lambda e: e.scalar_tensor_tensor(out=y0[r][:, 0:tn], in0=xsT[:, r, t0:t0 + tn],
                                                       scalar=dcol[:, r:r + 1], in1=y0[r][:, 0:tn],
                                                       op0=ALU.mult, op1=ALU.add),
               reads=["xsT", "dcol", f"y0_{r}"], writes=[f"y0_{r}"])
            op("act", lambda e: e.activation(out=zt[r][:, 0:tn], in_=zt[r][:, 0:tn], func=AF.Silu),
               reads=[f"zt_{r}"], writes=[f"zt_{r}"])
            op("dve", lambda e: e.tensor_tensor(out=y0[r][:, 0:tn], in0=y0[r][:, 0:tn], in1=zt[r][:, 0:tn], op=ALU.mult),
               reads=[f"y0_{r}", f"zt_{r}"], writes=[f"y0_{r}"])
            op("act", lambda e: e.activation(out=sq[r][:, 0:tn], in_=y0[r][:, 0:tn], func=AF.Square),
               reads=[f"y0_{r}"], writes=[f"dsq_{r}"])
        for r2 in range(3):
            s = n % 2; n += 1
            for r in range(3):
                op("pe", lambda e: e.matmul(psn[s][:, 0:tn], lhsT=gm[:, r * 3 + r2, :], rhs=sq[r][:, 0:tn],
                                            start=(r == 0), stop=(r == 2)),
                   reads=[f"dsq_{r}", "gm"], writes=[f"psn{s}"], pe_acc=(r > 0))
            _rsqrt(fw, rs[:, 0:tn], psn[s][:, 0:tn], 1.0 / 192, f"psn{s}", "drs")
            op("dve", lambda e: e.scalar_tensor_tensor(out=so[s][:, 0:tn], in0=y0[r2][:, 0:tn],
                                                       scalar=gso[:, r2:r2 + 1], in1=rs[:, 0:tn],
                                                       op0=ALU.mult, op1=ALU.mult),
               reads=[f"y0_{r2}", "gso", "drs"], writes=[f"so{s}"])
            dma(SSMT[r2 * 128:(r2 + 1) * 128, t0:t0 + tn], so[s][:, 0:tn], reads=[f"so{s}"],
                writes=[f"D:SSMT{r2}_{t0}"], q="pool")
    fw.release(mk)
    fw.release(mk0)


def _sin5(fw, dst, ps, pskey, f5, fb5, tmps, n, tag):
    s_, s2, t_ = tmps
    fw.op("act", lambda e: e.activation(out=s_[:, 0:n], in_=ps[:, 0:n], func=AF.Sin, scale=f5, bias=fb5),
          reads=[pskey, "hyvec"], writes=[tag + "s"])
    fw.op("act", lambda e: e.activation(out=s2[:, 0:n], in_=s_[:, 0:n], func=AF.Square), reads=[tag + "s"], writes=[tag + "s2"])
    fw.op("dve", lambda e: e.tensor_scalar(out=t_[:, 0:n], in0=s2[:, 0:n], scalar1=16.0, scalar2=-20.0,
                                           op0=ALU.mult, op1=ALU.add), reads=[tag + "s2"], writes=[tag + "t"])
    fw.op("dve", lambda e: e.tensor_tensor(out=t_[:, 0:n], in0=t_[:, 0:n], in1=s2[:, 0:n], op=ALU.mult),
          reads=[tag + "t", tag + "s2"], writes=[tag + "t"])
    fw.op("dve", lambda e: e.scalar_tensor_tensor(out=dst, in0=t_[:, 0:n], scalar=5.0, in1=s_[:, 0:n],
                                                  op0=ALU.add, op1=ALU.mult), reads=[tag + "t", tag + "s"], writes=[tag + "h"])


def phase_E(G):
    P, fw, nc, li = G["P"], G["fw"], G["nc"], G["li"]
    op, dma = fw.op, fw.dma
    PT = G["PT"]
    ones_f, ident_f = G["ones_f"], G["ident_f"]
    last = li == DEPTH - 1
    w_conv = _inp(P, "w_conv_hy", [DEPTH, 3, 768]); b_conv = _inp(P, "b_conv_hy", [DEPTH, 768])
    w_f1 = _inp(P, "w_f1", [DEPTH, 33, 64]); w_f2 = _inp(P, "w_f2", [DEPTH, 64, 64]); w_f3 = _inp(P, "w_f3", [DEPTH, 64, 1024])
    hyvec_d = _inp(P, "hy_vecs", [DEPTH, 4, 64])
    d_skip = _inp(P, "d_skip_hy", [DEPTH, 2, 256])
    ndl_d = _inp(P, "hy_ndelta", [128, 2])
    JR = _inp(P, "jrev", [128, 128])
    HYT = _scr(P, "HYT", [HY_CH, T], BF16)

    seqs = [(SEQ, CTX)] if last else [(CTX, 0), (SEQ, CTX)]
    for (L, tok0) in seqs:
        NJ = L // 128
        FWD = 2 * L
        feats_d = _inp(P, f"hy_feats_{L}", [2, 33, L])
        t01_d = _inp(P, f"hy_t01_{L}", [2, 128, L])
        if f"FD{L}" not in P.dscr:
            P.dh = getattr(P, "dh", {})
            P.dh[f"FD{L}"] = nc.dram_tensor(f"FD{L}", [2, 256, FWD], BF16, kind="ExternalOutput" if "FILT" in P.debug and L == SEQ else "Internal")
            P.dscr[f"FD{L}"] = P.dh[f"FD{L}"].ap()
        FDh = P.dh[f"FD{L}"]; FD = P.dscr[f"FD{L}"]

        mk = fw.mark()
        w1 = fw.sbuf("w1", [33, 64]); w2 = fw.sbuf("w2", [64, 64]); w3 = fw.sbuf("w3", [64, 1024])
        hv = fw.sbuf("hv", [64, 4]); f5 = fw.sbuf("f5", [64, 2]); fb5 = fw.sbuf("fb5", [64, 2])
        ndl = fw.sbuf("ndl", [128, 2])
        dma(w1[:], w_f1[li], writes=["w1"]); dma(w2[:], w_f2[li], writes=["w2"]); dma(w3[:], w_f3[li], writes=["w3"])
        dma(hv[:], hyvec_d[li].rearrange("v p -> p v"), writes=["hv"], allow_slow_non_contiguous=True)
        dma(ndl[:], ndl_d, writes=["ndl"])
        for i in range(2):
            op("dve", lambda e: e.tensor_scalar(out=f5[:, i:i + 1], in0=hv[:, 2 * i + 1:2 * i + 2], scalar1=0.2,
                                                scalar2=None, op0=ALU.mult), reads=["hv"], writes=["hyvec"])
            op("dve", lambda e: e.tensor_tensor(out=fb5[:, i:i + 1], in0=f5[:, i:i + 1], in1=hv[:, 2 * i:2 * i + 1],
                                                op=ALU.mult), reads=["hyvec", "hv"], writes=["hyvec"])
        FB = [[fw.sbuf(f"FB{o}{ch}", [128, FWD]) for ch in range(2)] for o in range(2)]
        featb = [fw.sbuf(f"featb{s}", [33, 512]) for s in range(2)]
        t01b = [fw.sbuf(f"t01b{s}", [128, 512]) for s in range(2)]
        tm = [[fw.sbuf(f"tm{a}{b}", [64, 512]) for b in range(3)] for a in range(2)]
        h1 = fw.sbuf("h1", [64, 512]); h2 = fw.sbuf("h2", [64, 512])
        dec = [fw.sbuf(f"dec{ch}", [128, 512]) for ch in range(2)]
        pm1 = fw.psum("pm1", [64, 512]); pm2 = fw.psum("pm2", [64, 512])
        pm3 = [fw.psum(f"pm3{s}", [128, 512]) for s in range(2)]
        nrm = fw.sbuf("nrm", [128, 4])
        FBb = [fw.sbuf(f"FBb{s}", [128, FWD], BF16) for s in range(2)]
        BL = min(512, L)
        n3 = 0
        for dr in (1, 0):
            for bi, c0 in enumerate(range(0, L, BL)):
                s = bi % 2
                dma(featb[s][:, 0:BL], feats_d[dr, :, c0:c0 + BL], writes=[f"featb{s}"])
                dma(t01b[s][:, 0:BL], t01_d[dr, :, c0:c0 + BL], writes=[f"t01b{s}"], q="pool")
                op("pe", lambda e: e.matmul(pm1[:, 0:BL], lhsT=w1[:], rhs=featb[s][:, 0:BL], start=True, stop=True),
                   reads=["w1", f"featb{s}"], writes=["pm1"])
                _sin5(fw, h1[:, 0:BL], pm1, "pm1", f5[:, 0:1], fb5[:, 0:1], tm[0], BL, "a")
                op("pe", lambda e: e.matmul(pm2[:, 0:BL], lhsT=w2[:], rhs=h1[:, 0:BL], start=True, stop=True),
                   reads=["w2", "ah"], writes=["pm2"])
                _sin5(fw, h2[:, 0:BL], pm2, "pm2", f5[:, 1:2], fb5[:, 1:2], tm[1], BL, "b")
                col0 = (L - 1 + c0) if dr == 0 else c0
                for ch in range(2):
                    op("act", lambda e: e.activation(out=dec[ch][:, 0:BL], in_=t01b[s][:, 0:BL], func=AF.Exp,
                                                     scale=ndl[:, ch:ch + 1]),
                       reads=[f"t01b{s}", "ndl"], writes=[f"dec{ch}"])
                    for o in range(2):
                        p3 = n3 % 2; n3 += 1
                        cb = dr * 512 + o * 256 + ch * 128
                        op("pe", lambda e: e.matmul(pm3[p3][:, 0:BL], lhsT=w3[:, cb:cb + 128], rhs=h2[:, 0:BL],
                                                    start=True, stop=True), reads=["w3", "bh"], writes=[f"pm3{p3}"])
                        op("dve", lambda e: e.tensor_tensor(out=FB[o][ch][:, col0:col0 + BL], in0=pm3[p3][:, 0:BL],
                                                            in1=dec[ch][:, 0:BL], op=ALU.mult),
                           reads=[f"pm3{p3}", f"dec{ch}"], writes=[f"FB{o}{ch}"])
        for o in range(2):
            for ch in range(2):
                k = o * 2 + ch
                op("dve", lambda e: e.tensor_reduce(out=nrm[:, k:k + 1], in_=FB[o][ch][:, 0:2 * L - 1], axis=AX.X,
                                                    op=ALU.add, apply_absolute_value=True),
                   reads=[f"FB{o}{ch}"], writes=[f"nrm{k}"])
                op("dve", lambda e: e.tensor_scalar(out=nrm[:, k:k + 1], in0=nrm[:, k:k + 1], scalar1=EPS, scalar2=None,
                                                    op0=ALU.add), reads=[f"nrm{k}"], writes=[f"nrm{k}"])
                op("dve", lambda e: e.reciprocal(out=nrm[:, k:k + 1], in_=nrm[:, k:k + 1]), reads=[f"nrm{k}"], writes=[f"nrm{k}"])
                op("pool" if k % 2 else "dve",
                   lambda e: e.tensor_scalar(out=FBb[k % 2][:, 0:2 * L - 1], in0=FB[o][ch][:, 0:2 * L - 1],
                                             scalar1=nrm[:, k:k + 1], scalar2=None, op0=ALU.mult),
                   reads=[f"nrm{k}", f"FB{o}{ch}"], writes=[f"FBb{k % 2}"])
                dma(FD[o, ch * 128:(ch + 1) * 128, 0:2 * L - 1], FBb[k % 2][:, 0:2 * L - 1], reads=[f"FBb{k % 2}"],
                    writes=[f"D:FD{L}"], q="sp" if k % 2 else "pool")
        fw.release(mk)

        GW = (2 * NJ - 1) * 128
        CG = 512 // NJ if NJ <= 32 else 16
        CG = min(CG, 16)
        for ch in range(2):
            mk = fw.mark()
            Pq = [fw.sbuf(f"Pq{q}", [128, NJ, 128]) for q in range(3)]
            zb = [fw.sbuf(f"zb{s}", [128, NJ, 128]) for s in range(2)]
            zrev = fw.sbuf("zrev", [128, NJ, 128], BF16)
            zd = fw.sbuf("zd", [128, NJ, 128])
            dsk = fw.sbuf("dsk", [128, 128])
            drow = fw.sbuf("drow", [1, 128])
            jr = fw.sbuf("jr", [128, 128])
            wch = fw.sbuf("wch", [128, 3, 3]); bch = fw.sbuf("bch", [128, 3])
            dma(jr[:], JR, writes=["jr"])
            for q in range(3):
                r0 = q * 256 + ch * 128
                for k in range(3):
                    dma(wch[:, q, k:k + 1], w_conv[li, k, r0:r0 + 128].rearrange("(p o) -> p o", o=1), writes=["wch"],
                        allow_slow_non_contiguous=True)
                dma(bch[:, q:q + 1], b_conv[li, r0:r0 + 128].rearrange("(p o) -> p o", o=1), writes=["bch"],
                    allow_slow_non_contiguous=True)
            mk2 = fw.mark()
            xb = fw.sbuf("hxb", [128, L + 2]); pT_ = fw.sbuf("hpT", [128, L])
            ptr = [fw.psum(f"hptr{s}", [128, 4, 128]) for s in range(2)]
            op("pool", lambda e: e.memset(xb[:], 0.0), writes=["hxb"])
            nt = 0
            for q in range(3):
                r0 = O_HY + q * 256 + ch * 128
                rk = [k for k in fw.lastw if k.startswith("D:PT") and r0 - 127 <= int(k[4:].split("_")[0]) <= r0 + 127]
                dma(xb[:, 1:1 + L], PT[r0:r0 + 128, tok0:tok0 + L], reads=rk, writes=["hxb"])
                op("dve", lambda e: e.tensor_scalar(out=pT_[:], in0=xb[:, 0:L], scalar1=wch[:, q, 0:1], scalar2=bch[:, q:q + 1],
                                                    op0=ALU.mult, op1=ALU.add), reads=["hxb", "wch", "bch"], writes=["hpT"])
                for k in (1, 2):
                    op("dve", lambda e: e.scalar_tensor_tensor(out=pT_[:], in0=xb[:, k:k + L], scalar=wch[:, q, k:k + 1],
                                                               in1=pT_[:], op0=ALU.mult, op1=ALU.add),
                       reads=["hxb", "hpT", "wch"], writes=["hpT"])
                for j0 in range(0, NJ, 4):
                    s = nt % 2; nt += 1
                    jn = min(4, NJ - j0)
                    for j in range(jn):
                        op("pe", lambda e: e.transpose(out=ptr[s][:, j, :], in_=pT_[:, (j0 + j) * 128:(j0 + j + 1) * 128],
                                                       identity=ident_f[:]), reads=["hpT", "ident_f"], writes=[f"hptr{s}"],
                           pe_acc=(j > 0))
                    op("act", lambda e: e.copy(out=Pq[q][:, j0:j0 + jn, :], in_=ptr[s][:, 0:jn, :]),
                       reads=[f"hptr{s}"], writes=[f"Pq{q}"])
            fw.release(mk2)

            NG = 4
            Gt = [fw.sbuf(f"Gt{s}", [128, GW], BF16) for s in range(NG)]
            psJ = [fw.psum(f"psJ{s}", [128, 4, 128]) for s in range(2)]
            Yb = [fw.psum(f"Yb{s}", [128, CG, NJ]) for s in range(2)]
            psd_ = fw.psum("hpsd", [128, 128])
            hout = [fw.sbuf(f"hout{s}", [128, 512], BF16) for s in range(2)]
            zcur = Pq[0]; zkey = "Pq0"
            ng = 0; ny = 0
            for o in range(2):
                dma(drow[:], d_skip[li, o:o + 1, ch * 128:(ch + 1) * 128], writes=["drow"])
                op("pe", lambda e: e.matmul(psd_[:], lhsT=ones_f[0:1, :], rhs=drow[0:1, :], start=True, stop=True),
                   reads=["drow", "ones_f"], writes=["hpsd"])
                op("act", lambda e: e.copy(out=dsk[:], in_=psd_[:]), reads=["hpsd"], writes=["dsk"])
                for j in range(NJ):
                    op("pool" if j % 2 else "dve", lambda e: e.tensor_tensor(out=zd[:, j, :], in0=zcur[:, j, :], in1=dsk[:],
                                                                             op=ALU.mult),
                       reads=[zkey, "dsk"], writes=["zd"])
                for j0 in range(0, NJ, 4):
                    s = (j0 // 4) % 2
                    jn = min(4, NJ - j0)
                    op("pe", lambda e: e.matmul(psJ[s][:, 0:jn, :], lhsT=jr[:], rhs=zcur[:, j0:j0 + jn, :], start=True, stop=True),
                       reads=[zkey, "jr"], writes=[f"psJ{s}"])
                    op("act", lambda e: e.copy(out=zrev[:, j0:j0 + jn, :], in_=psJ[s][:, 0:jn, :]),
                       reads=[f"psJ{s}"], writes=["zrev"])
                znext = zb[o]; nkey = f"zb{o}"
                for c0 in range(0, 128, CG):
                    yb = ny % 2; ny += 1
                    for cc in range(CG):
                        c = c0 + cc
                        gs = ng % NG; ng += 1
                        row = (o * 256 + ch * 128 + c) * FWD
                        dma(Gt[gs][:], bass.AP(tensor=FDh, offset=row, ap=[[1, 128], [1, GW]]),
                            reads=[f"D:FD{L}"], writes=[f"Gt{gs}"], q="sp" if ng % 2 else "act")
                        ds_ = [0] + [d for d in range(-(NJ - 1), NJ) if d != 0]
                        for di, d in enumerate(ds_):
                            J0 = max(0, -d); J1 = min(NJ, NJ - d)
                            op("pe", lambda e: e.matmul(Yb[yb][:, cc, J0 + d:J1 + d],
                                                        lhsT=Gt[gs][:, (d + NJ - 1) * 128:(d + NJ) * 128],
                                                        rhs=zrev[:, J0:J1, c], start=(di == 0), stop=(di == len(ds_) - 1)),
                               reads=[f"Gt{gs}", "zrev"], writes=[f"Yb{yb}"], pe_acc=(di > 0))
                    zv = znext[:, :, c0:c0 + CG].rearrange("p j c -> p c j")
                    op("dve", lambda e: e.tensor_tensor(out=zv, in0=Yb[yb][:], in1=zd[:, :, c0:c0 + CG].rearrange("p j c -> p c j"),
                                                        op=ALU.add), reads=[f"Yb{yb}", "zd"], writes=[nkey])
                    op("pool", lambda e: e.tensor_tensor(out=zv, in0=zv, in1=Pq[o + 1][:, :, c0:c0 + CG].rearrange("p j c -> p c j"),
                                                         op=ALU.mult), reads=[nkey, f"Pq{o + 1}"], writes=[nkey])
                zcur = znext; zkey = nkey
            for j0 in range(0, NJ, 4):
                s = (j0 // 4) % 2
                jn = min(4, NJ - j0)
                for j in range(jn):
                    op("pe", lambda e: e.transpose(out=psJ[s][:, j, :], in_=zcur[:, j0 + j, :], identity=ident_f[:]),
                       reads=[zkey, "ident_f"], writes=[f"psJ{s}"], pe_acc=(j > 0))
                op("act", lambda e: e.copy(out=hout[s][:, 0:jn * 128], in_=psJ[s][:, 0:jn, :].rearrange("p j t -> p (j t)")),
                   reads=[f"psJ{s}"], writes=[f"hout{s}"])
                dma(HYT[ch * 128:(ch + 1) * 128, tok0 + j0 * 128:tok0 + (j0 + jn) * 128], hout[s][:, 0:jn * 128],
                    reads=[f"hout{s}"], writes=[f"D:HYT{ch}_{tok0 + j0 * 128}"], q="pool")
            fw.release(mk)
    if last:
        pass


def phase_F(G):
    P, fw, nc, li = G["P"], G["fw"], G["nc"], G["li"]
    op, dma = fw.op, fw.dma
    XR, MOD, OUT = G["XR"], G["MOD"], G["OUT"]
    ident_b = G["ident_b"]
    last = li == DEPTH - 1
    w_out = _inp(P, "w_out", [DEPTH, D, D])
    w_ff1 = _inp(P, "w_ff1", [DEPTH, D, DFF]); w_ff2 = _inp(P, "w_ff2", [DEPTH, DFF, D])
    ATT = _scr(P, "ATT", [NH, 64, T], BF16)
    SSMT = _scr(P, "SSMT", [SSM_INNER, T], BF16)
    HYT = _scr(P, "HYT", [HY_CH, T], BF16)

    mk0 = fw.mark()
    w1b = fw.sbuf("w1b", [128, 8, DFF], BF16)
    w2b = fw.sbuf("w2b", [128, 32, D], BF16)
    wo_att = fw.sbuf("wo_att", [64, NH, D], BF16)
    wo_rest = fw.sbuf("wo_rest", [128, 5, D], BF16)
    mk = fw.mark()
    stg = [fw.sbuf(f"fstg{s}", [128, 8, 512]) for s in range(2)]
    n = 0
    for nb in range(8):
        s = n % 2; n += 1
        _load_cast(fw, w1b[:, :, nb * 512:(nb + 1) * 512], w_ff1[li, :, nb * 512:(nb + 1) * 512]
                   .rearrange("(k p) n -> p k n", p=128), stg[s][:], "w1b", f"fstg{s}", q="sp" if n % 2 else "pool")
    for kb in range(4):
        for half in range(2):
            s = n % 2; n += 1
            _load_cast(fw, w2b[:, kb * 8:(kb + 1) * 8, half * 512:(half + 1) * 512],
                       w_ff2[li, kb * 1024:(kb + 1) * 1024, half * 512:(half + 1) * 512]
                       .rearrange("(k p) n -> p k n", p=128), stg[s][:], "w2b", f"fstg{s}",
                       q="sp" if n % 2 else "pool")
    for half in range(2):
        s = n % 2; n += 1
        _load_cast(fw, wo_att[:, :, half * 512:(half + 1) * 512],
                   w_out[li, 0:384, half * 512:(half + 1) * 512].rearrange("(h c) n -> c h n", c=64),
                   stg[s][0:64, 0:6, :], "wo_att", f"fstg{s}")
        s = n % 2; n += 1
        _load_cast(fw, wo_rest[:, :, half * 512:(half + 1) * 512],
                   w_out[li, 384:1024, half * 512:(half + 1) * 512].rearrange("(k p) n -> p k n", p=128),
                   stg[s][:, 0:5, :], "wo_rest", f"fstg{s}")
    fw.release(mk)

    mods = [fw.sbuf(f"fmod{d}", [128, D]) for d in range(4)]
    attb = [fw.sbuf(f"attb{s}", [64, NH, 128], BF16) for s in range(2)]
    ssmb = [fw.sbuf(f"ssmb{s}", [128, 3, 128], BF16) for s in range(2)]
    hyb = [fw.sbuf(f"hyb{s}", [128, 2, 128], BF16) for s in range(2)]
    xt = [fw.sbuf(f"fxt{s}", [128, D]) for s in range(2)]
    xm = [fw.sbuf(f"fxm{s}", [128, D]) for s in range(2)]
    hb = fw.sbuf("fhb", [128, D], BF16)
    h2T = fw.sbuf("h2T", [128, 8, 128], BF16)
    aT = fw.sbuf("aT", [128, 32, 128], BF16)
    rr = [fw.sbuf(f"frr{s}", [128, 128]) for s in range(2)]
    ss = fw.sbuf("fss", [128, NCH]); rstd = fw.sbuf("frstd", [128, NCH])
    ot = [fw.sbuf(f"fot{s}", [128, 512]) for s in range(2)]
    psm = [fw.psum(f"psm{s}", [128, 512]) for s in range(2)]
    pst = fw.psum("fpst", [128, 8, 128], BF16)
    psf = [fw.psum(f"psf{s}", [128, 128]) for s in range(2)]
    pso = [fw.psum(f"fpso{s}", [128, 512]) for s in range(2)]

    chunks = list(range(2 if last else 0, NCH))
    state = {"kind": None, "n": 0}

    def front(ci, b):
        kind = 1 if ci < 2 else 0
        t0 = ci * 128
        if kind != state["kind"]:
            for d, src in enumerate((2, 3, 4, 5)):
                dma(mods[d][:], MOD[kind, src], reads=[f"D:MOD{kind}_{src}"], writes=[f"fmod{d}"])
            state["kind"] = kind
        dma(attb[b][:], ATT[:, :, t0:t0 + 128].rearrange("h c t -> c h t"),
            reads=[k for k in fw.lastw if k.startswith("D:ATT")], writes=[f"attb{b}"])
        dma(ssmb[b][:], SSMT[:, t0:t0 + 128].rearrange("(k p) t -> p k t", p=128),
            reads=[k for k in fw.lastw if k.startswith("D:SSMT")], writes=[f"ssmb{b}"], q="pool")
        dma(hyb[b][:], HYT[:, t0:t0 + 128].rearrange("(k p) t -> p k t", p=128),
            reads=[k for k in fw.lastw if k.startswith("D:HYT")], writes=[f"hyb{b}"], q="pool")
        dma(xt[b][:], XR[t0:t0 + 128, :], reads=[f"D:XR{ci}"], writes=[f"fxt{b}"])
        for half in range(2):
            ops_ = [(attb[b][:, h, :], wo_att[:, h, half * 512:(half + 1) * 512], f"attb{b}", "wo_att") for h in range(NH)]
            ops_ += [(ssmb[b][:, k, :], wo_rest[:, k, half * 512:(half + 1) * 512], f"ssmb{b}", "wo_rest") for k in range(3)]
            ops_ += [(hyb[b][:, k, :], wo_rest[:, 3 + k, half * 512:(half + 1) * 512], f"hyb{b}", "wo_rest") for k in range(2)]
            for i, (l_, r_, lk, rk) in enumerate(ops_):
                op("pe", lambda e: e.matmul(psm[half][:], lhsT=l_, rhs=r_, start=(i == 0), stop=(i == len(ops_) - 1)),
                   reads=[lk, rk], writes=[f"psm{half}"], pe_acc=(i > 0))
            op("dve", lambda e: e.tensor_tensor(out=xm[b][:, half * 512:(half + 1) * 512], in0=psm[half][:],
                                                in1=mods[0][:, half * 512:(half + 1) * 512], op=ALU.mult),
               reads=[f"psm{half}", "fmod0"], writes=[f"fxm{b}"])
        op("pool", lambda e: e.tensor_tensor(out=xm[b][:], in0=xm[b][:], in1=xt[b][:], op=ALU.add),
           reads=[f"fxm{b}", f"fxt{b}"], writes=[f"fxm{b}"])
        op("act", lambda e: e.activation(out=hb[:], in_=xm[b][:], func=AF.Square, accum_out=ss[:, ci:ci + 1]),
           reads=[f"fxm{b}"], writes=["fhb", f"fss{ci}"])
        _rsqrt(fw, rstd[:, ci:ci + 1], ss[:, ci:ci + 1], 1.0 / D, f"fss{ci}", f"frstd{ci}")
        op("dve", lambda e: e.scalar_tensor_tensor(out=xt[b][:], in0=xm[b][:], scalar=rstd[:, ci:ci + 1],
                                                   in1=mods[1][:], op0=ALU.mult, op1=ALU.mult),
           reads=[f"fxm{b}", f"frstd{ci}", "fmod1"], writes=[f"fxt{b}"])
        op("pool", lambda e: e.tensor_tensor(out=hb[:], in0=xt[b][:], in1=mods[2][:], op=ALU.add),
           reads=[f"fxt{b}", "fmod2"], writes=["fhb"])

    def mid(ci, b):
        for k in range(8):
            op("pe", lambda e: e.transpose(out=pst[:, k, :], in_=hb[:, k * 128:(k + 1) * 128], identity=ident_b[:]),
               reads=["fhb", "ident_b"], writes=["fpst"], pe_acc=(k > 0))
        op("act", lambda e: e.copy(out=h2T[:], in_=pst[:]), reads=["fpst"], writes=["h2T"])
        for j in range(32):
            s = j % 2
            for k in range(8):
                op("pe", lambda e: e.matmul(psf[s][:], lhsT=w1b[:, k, j * 128:(j + 1) * 128], rhs=h2T[:, k, :],
                                            start=(k == 0), stop=(k == 7)),
                   reads=["h2T", "w1b"], writes=[f"psf{s}"], pe_acc=(k > 0))
            op("act", lambda e: e.activation(out=rr[s][:], in_=psf[s][:], func=AF.Relu),
               reads=[f"psf{s}"], writes=[f"frr{s}"])
            op("dve" if j % 2 else "pool", lambda e: e.tensor_tensor(out=aT[:, j, :], in0=rr[s][:], in1=rr[s][:],
                                                                     op=ALU.mult),
               reads=[f"frr{s}"], writes=[f"aT{j}"])

    def back(ci, b):
        t0 = ci * 128
        for half in range(2):
            s = state["n"] % 2; state["n"] += 1
            for j in range(32):
                op("pe", lambda e: e.matmul(pso[s][:], lhsT=aT[:, j, :], rhs=w2b[:, j, half * 512:(half + 1) * 512],
                                            start=(j == 0), stop=(j == 31)),
                   reads=[f"aT{j}", "w2b"], writes=[f"fpso{s}"], pe_acc=(j > 0))
            op("dve", lambda e: e.tensor_tensor(out=ot[s][:], in0=pso[s][:],
                                                in1=mods[3][:, half * 512:(half + 1) * 512], op=ALU.mult),
               reads=[f"fpso{s}", "fmod3"], writes=[f"fot{s}"])
            op("pool", lambda e: e.tensor_tensor(out=ot[s][:], in0=ot[s][:], in1=xm[b][:, half * 512:(half + 1) * 512],
                                                 op=ALU.add), reads=[f"fot{s}", f"fxm{b}"], writes=[f"fot{s}"])
            if last:
                dma(OUT[t0 - CTX:t0 - CTX + 128, half * 512:(half + 1) * 512], ot[s][:], reads=[f"fot{s}"],
                    writes=[f"D:OUT{ci}_{half}"], q="pool")
            else:
                dma(XR[t0:t0 + 128, half * 512:(half + 1) * 512], ot[s][:], reads=[f"fot{s}"],
                    writes=[f"D:XR{ci}"], q="pool")

    front(chunks[0], 0)
    for i, ci in enumerate(chunks):
        b = i % 2
        mid(ci, b)
        nxt = chunks[i + 1] if i + 1 < len(chunks) else None
        same_kind = nxt is not None and ((nxt < 2) == (ci < 2))
        if nxt is not None and same_kind:
            front(nxt, 1 - b)
            back(ci, b)
        else:
            back(ci, b)
            if nxt is not None:
                front(nxt, 1 - b)
    fw.release(mk0)
```

```python
import math
import numpy as np
import ml_dtypes
import concourse.bass as bass
import concourse.mybir as mybir
from concourse.bass_utils import run_bass_kernel_spmd

F32 = mybir.dt.float32
BF16 = mybir.dt.bfloat16
AF = mybir.ActivationFunctionType
ALU = mybir.AluOpType
AX = mybir.AxisListType

D = 1024
SEQ = 4096
CTX = 256
T = SEQ + CTX
NCH = T // 128
DEPTH = 2
EPS = 1e-6
NH = 6
QK = 96
IN_COLS = 2220
O_CQ, O_CKV, O_KR, O_Z, O_XBC, O_DT, O_HY = 0, 256, 384, 416, 800, 1440, 1452
SSM_INNER = 384
HY_CH = 256
DFF = 4096


class FW:
    NDSEM = 48

    def __init__(self):
        self.nc = bass.Bass("TRN2", target_bir_lowering=False)
        nc = self.nc
        self.eng = {"pe": nc.tensor, "act": nc.scalar, "dve": nc.vector,
                    "pool": nc.gpsimd, "sp": nc.sync}
        self._ctx = []
        self.sems = {}
        for e in self.eng:
            self.sems[e] = self._enter(nc.semaphore("s_" + e))
        self.dq = {"sp": 36, "pool": 36, "act": 8}
        self.dsems = {}
        self.dcnt = {}
        self.dnext = {q: 0 for q in self.dq}
        for q, n in self.dq.items():
            for i in range(n):
                self.dsems[(q, i)] = self._enter(nc.semaphore(f"d_{q}{i}"))
                self.dcnt[(q, i)] = 0
        self.seq = {e: 0 for e in self.eng}
        self.waited = {e: {} for e in self.eng}
        self.lastw = {}
        self.readers = {}
        self.n_inst = 0
        self.n_wait = 0
        self._rr = 0
        self.psum_keys = set()

    def _enter(self, cm):
        v = cm.__enter__()
        self._ctx.append(cm)
        return v

    def mark(self):
        return len(self._ctx)

    def release(self, mark):
        self.barrier()
        while len(self._ctx) > mark:
            self._ctx.pop().__exit__(None, None, None)
        self.lastw = {k: v for k, v in self.lastw.items() if k.startswith("D:")}
        self.readers = {k: v for k, v in self.readers.items() if k.startswith("D:")}

    def close(self):
        while self._ctx:
            self._ctx.pop().__exit__(None, None, None)

    def sbuf(self, name, shape, dt=F32):
        self._uid = getattr(self, "_uid", 0) + 1
        return self._enter(self.nc.sbuf_tensor(f"sb{self._uid}_{name}", list(shape), dt))

    def psum(self, name, shape, dt=F32):
        self._uid = getattr(self, "_uid", 0) + 1
        full = 512 if dt == F32 else 1024
        t = self._enter(self.nc.psum_tensor(f"ps{self._uid}_{name}", [128, full], dt))
        self.psum_keys.add(name)
        shape = list(shape)
        n = 1
        for d in shape[1:]:
            n *= d
        assert n <= full, (name, shape)
        v = t[0:shape[0], 0:n]
        if len(shape) == 3:
            v = v.rearrange("p (a b) -> p a b", b=shape[2])
        return v

    def _sem(self, key):
        return self.sems[key] if isinstance(key, str) else self.dsems[key]

    def _deps(self, reads, writes):
        deps = {}

        def add(k, v):
            if deps.get(k, 0) < v:
                deps[k] = v
        for r in reads:
            d = self.lastw.get(r)
            if d is not None:
                add(*d)
        for wk in writes:
            d = self.lastw.get(wk)
            if d is not None:
                add(*d)
            for rk, rv in self.readers.get(wk, {}).items():
                add(rk, rv)
        return deps

    def _emit_waits(self, e, deps, skip_self=False):
        w = self.waited[e]
        for k, v in deps.items():
            if skip_self and k == e:
                continue
            if w.get(k, 0) >= v:
                continue
            self.eng[e].wait_ge(self._sem(k), v)
            w[k] = v
            self.n_wait += 1

    def _record(self, dep, reads, writes):
        k, v = dep
        for r in reads:
            self.readers.setdefault(r, {})[k] = v
        for wk in writes:
            self.lastw[wk] = dep
            self.readers[wk] = {}

    def op(self, e, fn, reads=(), writes=(), pe_acc=False):
        if e == "any":
            e = ("dve", "pool", "act")[self._rr % 3]
            self._rr += 1
        px = [r for r in reads if r in self.psum_keys and r not in writes]
        if px:
            reads = [r for r in reads if r not in px]
            writes = list(writes) + px
        deps = self._deps(reads, writes)
        self._emit_waits(e, deps, skip_self=(e == "pe" and pe_acc))
        ins = fn(self.eng[e])
        self.seq[e] += 1
        ins.then_inc(self.sems[e], 1)
        self._record((e, self.seq[e]), reads, writes)
        self.n_inst += 1
        return ins

    def dma(self, out, in_, reads=(), writes=(), q="sp", **kw):
        deps = self._deps(reads, writes)
        i = (q, self.dnext[q])
        self.dnext[q] = (self.dnext[q] + 1) % self.dq[q]
        if self.dcnt[i] > 0:
            deps[i] = max(deps.get(i, 0), 16 * self.dcnt[i])
        self._emit_waits(q, deps)
        ins = self.eng[q].dma_start(out=out, in_=in_, **kw)
        self.dcnt[i] += 1
        ins.then_inc(self.dsems[i], 16)
        self._record((i, 16 * self.dcnt[i]), reads, writes)
        self.n_inst += 1
        return ins

    def barrier(self):
        deps = {e: self.seq[e] for e in self.eng if self.seq[e] > 0}
        for i, c in self.dcnt.items():
            if c > 0:
                deps[i] = 16 * c
        for e in self.eng:
            self._emit_waits(e, deps)

    def finish(self, keys, e="sp"):
        self._emit_waits(e, self._deps(keys, ()))


def _consts():
    c = {}
    c["ident_f"] = np.eye(128, dtype=np.float32)
    c["ones_f"] = np.ones((128, 128), np.float32)
    cos = np.ones((96, T), np.float32); sin = np.zeros((96, T), np.float32)
    t = np.arange(SEQ)
    pos = [(t // 64).astype(np.float32), (t % 64).astype(np.float32)]
    inv = (10000.0 ** (-np.arange(8, dtype=np.float32) / 8)).astype(np.float32)
    RT = np.zeros((96, 96), np.float32)
    for a in range(2):
        ang = pos[a][None, :] * inv[:, None]
        for b in range(2):
            for f in range(8):
                r = 64 + 16 * a + 8 * b + f
                cos[r, CTX:] = np.cos(ang[f]); sin[r, CTX:] = np.sin(ang[f])
        for f in range(8):
            i1 = 64 + 16 * a + f; i2 = i1 + 8
            RT[i2, i1] = -1.0
            RT[i1, i2] = 1.0
    c["rope_cos"] = cos; c["rope_sin"] = sin; c["rope_RT"] = RT
    pk = np.zeros((32, 96), np.float32); pk[np.arange(32), 64 + np.arange(32)] = 1.0
    c["pk_sel"] = pk
    i = np.arange(128)
    triF = (i[:, None] <= i[None, :]).astype(np.float32)
    c["tri"] = np.stack([triF, triF.T.copy()])
    NEGV = -1.0e30
    negF = np.where(i[:, None] <= i[None, :], 0.0, NEGV).astype(np.float32)
    negB = np.where(i[:, None] >= i[None, :], 0.0, NEGV).astype(np.float32)
    c["negmask"] = np.stack([negF, negB])
    gm = np.zeros((3, 3, 128, 128), np.float32)
    for r in range(3):
        for r2 in range(3):
            ga = (r * 128 + i) // 192; gb = (r2 * 128 + i) // 192
            gm[r, r2] = (ga[:, None] == gb[None, :]).astype(np.float32)
    c["gmask"] = gm.reshape(9, 128, 128)
    c["jrev"] = np.ascontiguousarray(np.eye(128, dtype=np.float32)[::-1])
    deltas = np.abs(np.linspace(math.log(1e-2) / 1.5, math.log(1e-2) / 0.3, HY_CH, dtype=np.float32))
    c["hy_ndelta"] = np.ascontiguousarray((-deltas).reshape(2, 128).T.astype(np.float32))
    for L in (CTX, SEQ):
        t01 = np.linspace(0.0, 1.0, L, dtype=np.float32)
        w = (np.float32(2.0 * math.pi / L) * np.arange(L, dtype=np.float32))
        bands = np.linspace(1e-4, 15, 16, dtype=np.float32)
        ang = (bands[None, :] * w[:, None]).astype(np.float32)
        feats = np.concatenate([t01[:, None], np.cos(ang), -np.sin(ang)], axis=1).astype(np.float32)
        fT = feats.T
        c[f"hy_feats_{L}"] = np.ascontiguousarray(np.stack([fT, fT[:, ::-1]]))
        t01r = np.broadcast_to(t01[None, :], (128, L))
        c[f"hy_t01_{L}"] = np.ascontiguousarray(np.stack([t01r, t01r[:, ::-1]]))
    return c


CONST_SPECS = {"ident_f": ([128, 128], F32), "ones_f": ([128, 128], F32)}


class Prog:
    def __init__(self, debug=()):
        self.fw = FW()
        self.nc = self.fw.nc
        self.debug = set(debug)
        self.din = {}
        self.dscr = {}

    def inp(self, name, shape, dt=F32):
        self.din[name] = self.nc.dram_tensor(name, list(shape), dt, kind="ExternalInput").ap()
        return self.din[name]

    def scratch(self, name, shape, dt=F32, out=False):
        kind = "ExternalOutput" if (out or name in self.debug) else "Internal"
        self.dscr[name] = self.nc.dram_tensor(name, list(shape), dt, kind=kind).ap()
        return self.dscr[name]


DSTOP = [None]
ENABLE_SSD = [True]
ENABLE_HYENA = [True]
DFLAG = [0]
SKIP_C = [False]


def build(debug=(), stop_after=None):
    P = Prog(debug)
    fw, nc = P.fw, P.nc
    op, dma = fw.op, fw.dma

    xcat = P.inp("xcat", [T, D])
    cvec = P.inp("cvec", [2, D])
    w_mod = P.inp("w_mod", [DEPTH, D, 6 * D])
    b_mod = P.inp("b_mod", [DEPTH, 6 * D])
    g_norm_mix = P.inp("g_norm_mix", [DEPTH, D])
    g_norm_mlp = P.inp("g_norm_mlp", [DEPTH, D])
    w_in = P.inp("w_in", [DEPTH, D, IN_COLS])
    cst = {k: P.inp(k, s, d) for k, (s, d) in CONST_SPECS.items()}

    XR = P.scratch("XR", [T, D])
    MOD = P.scratch("MOD", [2, 6, 128, D])
    PT = P.scratch("PT", [2224, T])
    DTRAW = P.scratch("DTRAW", [T, 12])
    OUT = P.scratch("out", [SEQ, D], out=True)

    ident_f = fw.sbuf("ident_f", [128, 128])
    ident_b = fw.sbuf("ident_b", [128, 128], BF16)
    ones_f = fw.sbuf("ones_f", [128, 128])
    dma(ident_f[:], cst["ident_f"], writes=["ident_f"])
    dma(ones_f[:], cst["ones_f"], writes=["ones_f"])
    op("dve", lambda e: e.tensor_copy(out=ident_b[:], in_=ident_f[:]), reads=["ident_f"], writes=["ident_b"])

    for ci in range(0, NCH, 2):
        dma(XR[ci * 128:(ci + 2) * 128, :], xcat[ci * 128:(ci + 2) * 128, :],
            writes=[f"D:XR{ci}", f"D:XR{ci + 1}"], q="pool")

    for li in range(DEPTH):
        mk = fw.mark()
        cT = fw.sbuf("cT", [128, 2, 8])
        sT = fw.sbuf("sT", [128, 2, 8])
        srep = fw.sbuf("srep", [128, 2, 8, 128])
        modsb = [fw.sbuf(f"modsb{j}", [128, 6 * D]) for j in range(2)]
        brow = fw.sbuf("brow", [1, 6 * D])
        grow = fw.sbuf("grow", [1, 2 * D])
        grep = fw.sbuf("grep", [128, 2 * D])
        wst = [fw.sbuf(f"wst{s}", [128, 8, 512]) for s in range(2)]
        psA = [fw.psum(f"psA{s}", [128, 512]) for s in range(2)]

        dma(cT[:], cvec.rearrange("j (k p) -> p j k", p=128), writes=["cT"],
            allow_slow_non_contiguous=True)
        dma(brow[:], b_mod[li:li + 1, :], writes=["brow"])
        dma(grow[:, 0:D], g_norm_mix[li:li + 1, :], writes=["grow"])
        dma(grow[:, D:2 * D], g_norm_mlp[li:li + 1, :], writes=["grow"])
        op("act", lambda e: e.activation(out=sT[:], in_=cT[:], func=AF.Silu), reads=["cT"], writes=["sT"])
        for j in range(2):
            for k in range(8):
                op("dve", lambda e: e.tensor_copy(out=srep[:, j, k, :],
                                                  in_=sT[:, j, k:k + 1].to_broadcast([128, 128])),
                   reads=["sT"], writes=["srep"])
        for nb in range(4):
            s = nb % 2
            op("pe", lambda e: e.matmul(psA[s][:], lhsT=ones_f[0:1, :], rhs=grow[0:1, nb * 512:(nb + 1) * 512],
                                        start=True, stop=True),
               reads=["ones_f", "grow"], writes=[f"psA{s}"])
            op("dve", lambda e: e.tensor_copy(out=grep[:, nb * 512:(nb + 1) * 512], in_=psA[s][:]),
               reads=[f"psA{s}"], writes=["grep"])
        cnt = 0
        for nb in range(12):
            ws = nb % 2
            dma(wst[ws][:], w_mod[li, :, nb * 512:(nb + 1) * 512].rearrange("(k p) n -> p k n", p=128),
                writes=[f"wst{ws}"])
            for j in range(2):
                s = cnt % 2
                cnt += 1
                for k in range(8):
                    op("pe", lambda e: e.matmul(psA[s][:], lhsT=srep[:, j, k, :], rhs=wst[ws][:, k, :],
                                                start=(k == 0), stop=False),
                       reads=["srep", f"wst{ws}"], writes=[f"psA{s}"], pe_acc=(k > 0))
                op("pe", lambda e: e.matmul(psA[s][:], lhsT=ones_f[0:1, :], rhs=brow[0:1, nb * 512:(nb + 1) * 512],
                                            start=False, stop=True),
                   reads=["ones_f", "brow"], writes=[f"psA{s}"], pe_acc=True)
                op("act" if j == 0 else "dve",
                   (lambda e: e.copy(out=modsb[j][:, nb * 512:(nb + 1) * 512], in_=psA[s][:])) if j == 0 else
                   (lambda e: e.tensor_copy(out=modsb[j][:, nb * 512:(nb + 1) * 512], in_=psA[s][:])),
                   reads=[f"psA{s}"], writes=[f"modsb{j}"])
        for j in range(2):
            m = modsb[j]
            op("dve", lambda e: e.scalar_tensor_tensor(out=m[:, D:2 * D], in0=m[:, D:2 * D], scalar=1.0,
                                                       in1=grep[:, 0:D], op0=ALU.add, op1=ALU.mult),
               reads=[f"modsb{j}", "grep"], writes=[f"modsb{j}"])
            op("dve", lambda e: e.scalar_tensor_tensor(out=m[:, 4 * D:5 * D], in0=m[:, 4 * D:5 * D], scalar=1.0,
                                                       in1=grep[:, D:2 * D], op0=ALU.add, op1=ALU.mult),
               reads=[f"modsb{j}", "grep"], writes=[f"modsb{j}"])
            for dst, src in ((0, 1), (1, 0), (2, 2), (3, 4), (4, 3), (5, 5)):
                dma(MOD[j, dst], m[:, src * D:(src + 1) * D], reads=[f"modsb{j}"], writes=[f"D:MOD{j}_{dst}"],
                    q="pool")
        fw.release(mk)
        if stop_after == ("A", li):
            break

        mk = fw.mark()
        hT = fw.sbuf("hT", [128, 8, T], BF16)
        winb = fw.sbuf("winb", [128, 8, 2224], BF16)
        wstg = [fw.sbuf(f"wstg{s}", [128, 8, 555]) for s in range(2)]
        A1 = [fw.sbuf(f"A1_{j}", [128, D]) for j in range(2)]
        B1 = [fw.sbuf(f"B1_{j}", [128, D]) for j in range(2)]
        xt = [fw.sbuf(f"xt{s}", [128, D]) for s in range(2)]
        tmp = [fw.sbuf(f"tmp{s}", [128, D]) for s in range(2)]
        hb = [fw.sbuf(f"hb{s}", [128, D], BF16) for s in range(2)]
        junk = fw.sbuf("junk", [128, D], BF16)
        ss = fw.sbuf("ss", [128, NCH])
        rstd = fw.sbuf("rstd", [128, NCH])
        dtst = fw.sbuf("dtst", [128, NCH, 12])
        osb = [fw.sbuf(f"osb{s}", [128, 512]) for s in range(3)]
        pst = [fw.psum(f"pst{s}", [128, 8, 128], BF16) for s in range(2)]
        psd = [fw.psum(f"psd{s}", [128, 16]) for s in range(2)]
        pso = [fw.psum(f"pso{s}", [128, 512]) for s in range(3)]

        for j in range(2):
            dma(A1[j][:], MOD[j, 0], reads=[f"D:MOD{j}_0"], writes=[f"A1_{j}"])
            dma(B1[j][:], MOD[j, 1], reads=[f"D:MOD{j}_1"], writes=[f"B1_{j}"])
        for q4 in range(4):
            s = q4 % 2
            dma(wstg[s][:], w_in[li, :, q4 * 555:(q4 + 1) * 555].rearrange("(k p) n -> p k n", p=128),
                writes=[f"wstg{s}"], q="pool")
            op("any", lambda e: (e.copy if e is nc.scalar else e.tensor_copy)(
                out=winb[:, :, q4 * 555:(q4 + 1) * 555], in_=wstg[s][:]),
               reads=[f"wstg{s}"], writes=["winb"])

        for ci in range(NCH):
            s = ci % 2
            kind = 1 if ci < 2 else 0
            dma(xt[s][:], XR[ci * 128:(ci + 1) * 128, :], reads=[f"D:XR{ci}"], writes=[f"xt{s}"])
            op("act", lambda e: e.activation(out=junk[:], in_=xt[s][:], func=AF.Square,
                                             accum_out=ss[:, ci:ci + 1]),
               reads=[f"xt{s}"], writes=["junk", f"ss{ci}"])
            op("dve", lambda e: e.tensor_scalar(out=rstd[:, ci:ci + 1], in0=ss[:, ci:ci + 1],
                                                scalar1=1.0 / D, scalar2=EPS, op0=ALU.mult, op1=ALU.add),
               reads=[f"ss{ci}"], writes=[f"rstd{ci}"])
            op("act", lambda e: e.sqrt(out=rstd[:, ci:ci + 1], in_=rstd[:, ci:ci + 1]),
               reads=[f"rstd{ci}"], writes=[f"rstd{ci}"])
            op("dve", lambda e: e.reciprocal(out=rstd[:, ci:ci + 1], in_=rstd[:, ci:ci + 1]),
               reads=[f"rstd{ci}"], writes=[f"rstd{ci}"])
            op("dve", lambda e: e.scalar_tensor_tensor(out=tmp[s][:], in0=xt[s][:], scalar=rstd[:, ci:ci + 1],
                                                       in1=A1[kind][:], op0=ALU.mult, op1=ALU.mult),
               reads=[f"xt{s}", f"rstd{ci}", f"A1_{kind}"], writes=[f"tmp{s}"])
            op("pool", lambda e: e.tensor_tensor(out=hb[s][:], in0=tmp[s][:], in1=B1[kind][:], op=ALU.add),
               reads=[f"tmp{s}", f"B1_{kind}"], writes=[f"hb{s}"])
            for k in range(8):
                op("pe", lambda e: e.transpose(out=pst[s][:, k, :], in_=hb[s][:, k * 128:(k + 1) * 128],
                                               identity=ident_b[:]),
                   reads=[f"hb{s}", "ident_b"], writes=[f"pst{s}"], pe_acc=(k > 0))
            op("act", lambda e: e.copy(out=hT[:, :, ci * 128:(ci + 1) * 128], in_=pst[s][:]),
               reads=[f"pst{s}"], writes=[f"hT{ci}"])
            for k in range(8):
                op("pe", lambda e: e.matmul(psd[s][:, 0:12], lhsT=hT[:, k, ci * 128:(ci + 1) * 128],
                                            rhs=winb[:, k, O_DT:O_DT + 12], start=(k == 0), stop=(k == 7)),
                   reads=[f"hT{ci}", "winb"], writes=[f"psd{s}"], pe_acc=(k > 0))
            op("dve", lambda e: e.tensor_copy(out=dtst[:, ci, :], in_=psd[s][:, 0:12]),
               reads=[f"psd{s}"], writes=["dtst"])
        dma(DTRAW.rearrange("(c p) n -> p c n", p=128), dtst[:], reads=["dtst"], writes=["D:DTRAW"], q="pool")

        col_tiles = [(0, 128), (128, 128), (256, 128), (384, 32)]
        col_tiles += [(O_Z + 128 * i, 128) for i in range(3)]
        col_tiles += [(O_XBC + 128 * i, 128) for i in range(5)]
        col_tiles += [(O_HY + 128 * i, 128) for i in range(6)]
        tblocks = [(t0, min(512, T - t0)) for t0 in range(0, T, 512)]
        cnt = 0
        for (c0, cw) in col_tiles:
            for (t0, tn) in tblocks:
                s = cnt % 3
                cnt += 1
                rk = [f"hT{ci}" for ci in range(t0 // 128, (t0 + tn) // 128)] + ["winb"]
                for k in range(8):
                    op("pe", lambda e: e.matmul(pso[s][0:cw, 0:tn], lhsT=winb[:, k, c0:c0 + cw],
                                                rhs=hT[:, k, t0:t0 + tn], start=(k == 0), stop=(k == 7)),
                       reads=rk, writes=[f"pso{s}"], pe_acc=(k > 0))
                if cnt % 2:
                    op("act", lambda e: e.copy(out=osb[s][0:cw, 0:tn], in_=pso[s][0:cw, 0:tn]),
                       reads=[f"pso{s}"], writes=[f"osb{s}"])
                else:
                    op("dve", lambda e: e.tensor_copy(out=osb[s][0:cw, 0:tn], in_=pso[s][0:cw, 0:tn]),
                       reads=[f"pso{s}"], writes=[f"osb{s}"])
                dma(PT[c0:c0 + cw, t0:t0 + tn], osb[s][0:cw, 0:tn], reads=[f"osb{s}"],
                    writes=[f"D:PT{c0}_{t0}"], q="pool" if cnt % 2 else "sp")
        fw.release(mk)
        if stop_after == ("B", li):
            break
        G = dict(P=P, fw=fw, nc=nc, li=li, XR=XR, MOD=MOD, PT=PT, DTRAW=DTRAW, OUT=OUT,
                 ident_f=ident_f, ident_b=ident_b, ones_f=ones_f, cst=cst, dstop=DSTOP[0])
        if not SKIP_C[0]:
            phase_C(G)
        if stop_after == ("C", li):
            break
        if ENABLE_SSD[0]:
            phase_D(G)
        else:
            _zero_fill(G, "SSMT", SSM_INNER)
        if stop_after == ("D", li):
            break
        if ENABLE_HYENA[0]:
            phase_E(G)
        else:
            _zero_fill(G, "HYT", HY_CH)
        if stop_after == ("E", li):
            break
        phase_F(G)
        if stop_after == ("F", li):
            break

    fw.barrier()
    fw.close()
    return P


def make_inputs(inputs, core):
    b = core % 4
    m = {}
    m["xcat"] = np.ascontiguousarray(np.concatenate([inputs["ctx"][b], inputs["x"][b]], axis=0))
    m["cvec"] = np.ascontiguousarray(np.stack([inputs["c"][b], inputs["c_ctx"]], axis=0))
    for k in ("w_mod", "b_mod", "g_norm_mix", "g_norm_mlp", "w_in", "w_out", "g_cq", "g_ckv", "w_uq", "w_ukv",
              "g_qhead", "g_khead", "w_conv_ssm", "b_conv_ssm", "a_log", "dt_bias", "d_skip_ssm", "g_ssm_out",
              "w_conv_hy", "b_conv_hy", "w_f1", "b_f1", "freq_f1", "w_f2", "b_f2", "freq_f2", "w_f3",
              "d_skip_hy", "w_ff1", "w_ff2"):
        m[k] = np.ascontiguousarray(inputs[k])
    m["a_log"] = m["a_log"].reshape(DEPTH, 12); m["dt_bias"] = m["dt_bias"].reshape(DEPTH, 12)
    m["hy_vecs"] = np.ascontiguousarray(np.stack([inputs["b_f1"], inputs["freq_f1"], inputs["b_f2"], inputs["freq_f2"]], axis=1))
    m["d_skip_col"] = np.ascontiguousarray(np.repeat(inputs["d_skip_ssm"], 64, axis=1))
    m.update(_consts())
    return m


def kernel(**inputs):
    inputs = {k: np.asarray(v) for k, v in inputs.items()}
    P = build()
    in_maps = [make_inputs(inputs, c) for c in range(8)]
    in_maps = [{k: v for k, v in m.items() if k in P.din} for m in in_maps]
    res = run_bass_kernel_spmd(P.nc, in_maps, core_ids=list(range(8)))
    return np.stack([res.results[b]["out"] for b in range(4)], axis=0).astype(np.float32)


def _inp(P, name, shape, dt=F32):
    if name in P.din:
        return P.din[name]
    return P.inp(name, shape, dt)


def _scr(P, name, shape, dt=F32):
    if name in P.dscr:
        return P.dscr[name]
    return P.scratch(name, shape, dt)


def _interleave(gens):
    gens = list(gens)
    while gens:
        for g in list(gens):
            try:
                next(g)
            except StopIteration:
                gens.remove(g)


def _rsqrt(fw, dst, src, scale, rk, wk, reads_extra=()):
    fw.op("dve", lambda e: e.tensor_scalar(out=dst, in0=src, scalar1=scale, scalar2=EPS,
                                           op0=ALU.mult, op1=ALU.add), reads=[rk] + list(reads_extra), writes=[wk])
    fw.op("act", lambda e: e.sqrt(out=dst, in_=dst), reads=[wk], writes=[wk])
    fw.op("dve", lambda e: e.reciprocal(out=dst, in_=dst), reads=[wk], writes=[wk])


def _load_cast(fw, dst_bf, src_ap, stg, key_dst, key_stg, q="sp"):
    fw.dma(stg, src_ap, writes=[key_stg], q=q)
    fw.op("any", lambda e: (e.copy if e is fw.nc.scalar else e.tensor_copy)(out=dst_bf, in_=stg),
          reads=[key_stg], writes=[key_dst])


def phase_C(G):
    P, fw, nc, li = G["P"], G["fw"], G["nc"], G["li"]
    op, dma = fw.op, fw.dma
    PT = G["PT"]
    ones_f = G["ones_f"]
    last = li == DEPTH - 1
    g_cq = _inp(P, "g_cq", [DEPTH, 256]); g_ckv = _inp(P, "g_ckv", [DEPTH, 128])
    w_uq = _inp(P, "w_uq", [DEPTH, 256, 576]); w_ukv = _inp(P, "w_ukv", [DEPTH, 128, 768])
    g_qh = _inp(P, "g_qhead", [DEPTH, 96]); g_kh = _inp(P, "g_khead", [DEPTH, 96])
    COS = _inp(P, "rope_cos", [96, T]); SIN = _inp(P, "rope_sin", [96, T])
    RTd = _inp(P, "rope_RT", [96, 96]); PKd = _inp(P, "pk_sel", [32, 96])
    ATT = _scr(P, "ATT", [NH, 64, T], BF16)
    scale = 1.0 / math.sqrt(QK)

    mk0 = fw.mark()
    qT = fw.sbuf("qT", [96, NH, T], BF16)
    kT = fw.sbuf("kT", [96, NH, T], BF16)
    V1 = fw.sbuf("V1", [128, NCH, NH, 65], BF16)
    op("pool", lambda e: e.memset(V1[:], 1.0), writes=["V1"])

    mk = fw.mark()
    wuq_b = fw.sbuf("wuq_b", [128, 2, 576], BF16)
    wk_b = fw.sbuf("wk_b", [128, NH, 96], BF16)
    wv_b = fw.sbuf("wv_b", [128, NH, 64], BF16)
    pk_b = fw.sbuf("pk_b", [32, 96], BF16)
    stg = fw.sbuf("stg", [128, 2, 768])
    RT = fw.sbuf("RT", [96, 96])
    gcq = fw.sbuf("gcq", [128, 2]); gckv = fw.sbuf("gckv", [128, 1])
    gq = fw.sbuf("gq", [96, 1]); gk = fw.sbuf("gk", [96, 1])
    dma(stg[:, :, 0:576], w_uq[li].rearrange("(k p) n -> p k n", p=128), writes=["stg"])
    op("dve", lambda e: e.tensor_copy(out=wuq_b[:], in_=stg[:, :, 0:576]), reads=["stg"], writes=["wuq_b"])
    dma(stg[:, 0, :], w_ukv[li], reads=[], writes=["stg"])
    op("pool", lambda e: e.memset(wk_b[:], 0.0), writes=["wk_b"])
    op("dve", lambda e: e.tensor_copy(out=wk_b[:, :, 0:64],
                                      in_=stg[:, 0, :].rearrange("p (h c) -> p h c", c=128)[:, :, 0:64]),
       reads=["stg"], writes=["wk_b"])
    op("dve", lambda e: e.tensor_copy(out=wv_b[:],
                                      in_=stg[:, 0, :].rearrange("p (h c) -> p h c", c=128)[:, :, 64:128]),
       reads=["stg"], writes=["wv_b"])
    dma(stg[0:32, 1, 0:96], PKd, writes=["stg1"])
    op("dve", lambda e: e.tensor_copy(out=pk_b[:], in_=stg[0:32, 1, 0:96]), reads=["stg1"], writes=["pk_b"])
    dma(RT[:], RTd, writes=["RT"])
    dma(gcq[:], g_cq[li].rearrange("(k p) -> p k", p=128), writes=["gcq"], allow_slow_non_contiguous=True)
    dma(gckv[:], g_ckv[li].rearrange("(p o) -> p o", o=1), writes=["gckv"], allow_slow_non_contiguous=True)
    dma(gq[:], g_qh[li].rearrange("(p o) -> p o", o=1), writes=["gq"], allow_slow_non_contiguous=True)
    dma(gk[:], g_kh[li].rearrange("(p o) -> p o", o=1), writes=["gk"], allow_slow_non_contiguous=True)

    cqb = [fw.sbuf(f"cqb{s}", [128, 2, 512]) for s in range(2)]
    ckb = [fw.sbuf(f"ckb{s}", [128, 512]) for s in range(2)]
    krb = [fw.sbuf(f"krb{s}", [32, 512]) for s in range(2)]
    krb_b = fw.sbuf("krb_b", [32, 512], BF16)
    sq = fw.sbuf("sq", [128, 2, 512])
    rs = fw.sbuf("rs", [128, 512])
    cqn = fw.sbuf("cqn", [128, 2, 512], BF16)
    ckn = fw.sbuf("ckn", [128, 512], BF16)
    cosb = [fw.sbuf(f"cosb{s}", [96, 512]) for s in range(2)]
    sinb = [fw.sbuf(f"sinb{s}", [96, 512]) for s in range(2)]
    hsq = [fw.sbuf(f"hsq{s}", [96, 512]) for s in range(2)]
    hrs = [fw.sbuf(f"hrs{s}", [96, 512]) for s in range(2)]
    hn = [fw.sbuf(f"hn{s}", [96, 512]) for s in range(2)]
    ht1 = [fw.sbuf(f"ht1{s}", [96, 512]) for s in range(2)]
    ht2 = [fw.sbuf(f"ht2{s}", [96, 512]) for s in range(2)]
    ps1 = fw.psum("ps1", [128, 512])
    psh = [fw.psum(f"psh{s}", [96, 512]) for s in range(2)]
    ps2 = [fw.psum(f"ps2{s}", [96, 512]) for s in range(2)]
    psr = [fw.psum(f"psr{s}", [96, 512]) for s in range(2)]
    psv = fw.psum("psv", [128, 384])

    def head_chain(kind, h, s, t0, tn, bs, bi):
        pk = f"psh{s}"
        if kind == "q":
            for k in range(2):
                op("pe", lambda e: e.matmul(psh[s][:, 0:tn], lhsT=wuq_b[:, k, h * 96:(h + 1) * 96],
                                            rhs=cqn[:, k, 0:tn], start=(k == 0), stop=(k == 1)),
                   reads=["cqn", "wuq_b"], writes=[pk], pe_acc=(k > 0))
            gcol, gkey, dst, dkey = gq, "gq", qT[:, h, t0:t0 + tn], f"qT{h}_{bi}"
        else:
            op("pe", lambda e: e.matmul(psh[s][:, 0:tn], lhsT=wk_b[:, h, :], rhs=ckn[:, 0:tn], start=True, stop=False),
               reads=["ckn", "wk_b"], writes=[pk])
            op("pe", lambda e: e.matmul(psh[s][:, 0:tn], lhsT=pk_b[:], rhs=krb_b[:, 0:tn], start=False, stop=True),
               reads=["krb_b", "pk_b"], writes=[pk], pe_acc=True)
            gcol, gkey, dst, dkey = gk, "gk", kT[:, h, t0:t0 + tn], f"kT{h}_{bi}"
        yield
        op("act", lambda e: e.activation(out=hsq[s][:, 0:tn], in_=psh[s][:, 0:tn], func=AF.Square),
           reads=[pk], writes=[f"hsq{s}"])
        yield
        op("pe", lambda e: e.matmul(ps2[s][:, 0:tn], lhsT=ones_f[0:96, 0:96], rhs=hsq[s][:, 0:tn],
                                    start=True, stop=True), reads=[f"hsq{s}", "ones_f"], writes=[f"ps2{s}"])
        yield
        op("dve", lambda e: e.tensor_scalar(out=hrs[s][:, 0:tn], in0=ps2[s][:, 0:tn], scalar1=1.0 / QK, scalar2=EPS,
                                            op0=ALU.mult, op1=ALU.add), reads=[f"ps2{s}"], writes=[f"hrs{s}"])
        yield
        op("act", lambda e: e.sqrt(out=hrs[s][:, 0:tn], in_=hrs[s][:, 0:tn]), reads=[f"hrs{s}"], writes=[f"hrs{s}"])
        yield
        op("dve", lambda e: e.reciprocal(out=hrs[s][:, 0:tn], in_=hrs[s][:, 0:tn]), reads=[f"hrs{s}"], writes=[f"hrs{s}"])
        yield
        op("dve", lambda e: e.scalar_tensor_tensor(out=hn[s][:, 0:tn], in0=psh[s][:, 0:tn], scalar=gcol[:, 0:1],
                                                   in1=hrs[s][:, 0:tn], op0=ALU.mult, op1=ALU.mult),
           reads=[pk, gkey, f"hrs{s}"], writes=[f"hn{s}"])
        yield
        op("pe", lambda e: e.matmul(psr[s][:, 0:tn], lhsT=RT[:], rhs=hn[s][:, 0:tn], start=True, stop=True),
           reads=[f"hn{s}", "RT"], writes=[f"psr{s}"])
        op("pool", lambda e: e.tensor_tensor(out=ht1[s][:, 0:tn], in0=hn[s][:, 0:tn], in1=cosb[bs][:, 0:tn],
                                             op=ALU.mult), reads=[f"hn{s}", f"cosb{bs}"], writes=[f"ht1{s}"])
        yield
        op("dve", lambda e: e.tensor_tensor(out=ht2[s][:, 0:tn], in0=psr[s][:, 0:tn], in1=sinb[bs][:, 0:tn],
                                            op=ALU.mult), reads=[f"psr{s}", f"sinb{bs}"], writes=[f"ht2{s}"])
        yield
        op("pool", lambda e: e.tensor_tensor(out=dst, in0=ht1[s][:, 0:tn], in1=ht2[s][:, 0:tn], op=ALU.add),
           reads=[f"ht1{s}", f"ht2{s}"], writes=[dkey])
        yield

    tblocks = [(t0, min(512, T - t0)) for t0 in range(0, T, 512)]
    for bi, (t0, tn) in enumerate(tblocks):
        bs = bi % 2
        dma(cqb[bs][:, :, 0:tn], PT[0:256, t0:t0 + tn].rearrange("(k p) t -> p k t", p=128),
            reads=[f"D:PT{c}_{t0}" for c in (0, 128)], writes=[f"cqb{bs}"])
        dma(ckb[bs][:, 0:tn], PT[256:384, t0:t0 + tn], reads=[f"D:PT256_{t0}"], writes=[f"ckb{bs}"])
        dma(krb[bs][:, 0:tn], PT[384:416, t0:t0 + tn], reads=[f"D:PT384_{t0}"], writes=[f"krb{bs}"])
        dma(cosb[bs][:, 0:tn], COS[:, t0:t0 + tn], writes=[f"cosb{bs}"], q="pool")
        dma(sinb[bs][:, 0:tn], SIN[:, t0:t0 + tn], writes=[f"sinb{bs}"], q="pool")
        op("act", lambda e: e.activation(out=sq[:, :, 0:tn], in_=cqb[bs][:, :, 0:tn], func=AF.Square),
           reads=[f"cqb{bs}"], writes=["sq"])
        for k in range(2):
            op("pe", lambda e: e.matmul(ps1[:, 0:tn], lhsT=ones_f[:], rhs=sq[:, k, 0:tn], start=(k == 0), stop=(k == 1)),
               reads=["sq", "ones_f"], writes=["ps1"], pe_acc=(k > 0))
        _rsqrt(fw, rs[:, 0:tn], ps1[:, 0:tn], 1.0 / 256, "ps1", "rs")
        for k in range(2):
            op("dve", lambda e: e.scalar_tensor_tensor(out=cqn[:, k, 0:tn], in0=cqb[bs][:, k, 0:tn],
                                                       scalar=gcq[:, k:k + 1], in1=rs[:, 0:tn],
                                                       op0=ALU.mult, op1=ALU.mult),
               reads=[f"cqb{bs}", "gcq", "rs"], writes=["cqn"])
        op("act", lambda e: e.activation(out=sq[:, 0, 0:tn], in_=ckb[bs][:, 0:tn], func=AF.Square),
           reads=[f"ckb{bs}"], writes=["sq"])
        op("pe", lambda e: e.matmul(ps1[:, 0:tn], lhsT=ones_f[:], rhs=sq[:, 0, 0:tn], start=True, stop=True),
           reads=["sq", "ones_f"], writes=["ps1"])
        _rsqrt(fw, rs[:, 0:tn], ps1[:, 0:tn], 1.0 / 128, "ps1", "rs")
        op("dve", lambda e: e.scalar_tensor_tensor(out=ckn[:, 0:tn], in0=ckb[bs][:, 0:tn], scalar=gckv[:, 0:1],
                                                   in1=rs[:, 0:tn], op0=ALU.mult, op1=ALU.mult),
           reads=[f"ckb{bs}", "gckv", "rs"], writes=["ckn"])
        op("act", lambda e: e.copy(out=krb_b[:, 0:tn], in_=krb[bs][:, 0:tn]), reads=[f"krb{bs}"], writes=["krb_b"])
        for kind in ("q", "k"):
            for h in range(0, NH, 2):
                _interleave([head_chain(kind, h, 0, t0, tn, bs, bi), head_chain(kind, h + 1, 1, t0, tn, bs, bi)])
        for c in range(tn // 128):
            ci = t0 // 128 + c
            op("pe", lambda e: e.matmul(psv[:], lhsT=ckn[:, c * 128:(c + 1) * 128],
                                        rhs=wv_b[:].rearrange("p h c -> p (h c)"), start=True, stop=True),
               reads=["ckn", "wv_b"], writes=["psv"])
            op("act", lambda e: e.copy(out=V1[:, ci, :, 0:64], in_=psv[:].rearrange("p (h c) -> p h c", c=64)),
               reads=["psv"], writes=["V1", f"V1_{ci}"])
    fw.release(mk)

    mk = fw.mark()
    NS = 4
    psS = [fw.psum(f"psS{s}", [128, 512]) for s in range(NS)]
    acc = [fw.psum(f"acc{s}", [65, 512]) for s in range(2)]
    psb = fw.psum("psb", [64, 512])
    pT = [fw.sbuf(f"pT{s}", [128, 512], BF16) for s in range(NS)]
    osb = [fw.sbuf(f"aosb{s}", [65, 512]) for s in range(2)]
    rden = [fw.sbuf(f"rden{s}", [65, 512]) for s in range(2)]
    att = [fw.sbuf(f"att{s}", [64, 512], BF16) for s in range(2)]

    groups = []
    if not last:
        groups.append((0, 256, [0, 1]))
    for g in range(4):
        groups.append((CTX + g * 1024, 1024, list(range(NCH))))
    it = 0
    oc = 0
    for (q0, qn, tks) in groups:
        halves = [(q0 + o, min(512, qn - o)) for o in range(0, qn, 512)]
        for h in range(NH):
            items = [(ti, tkc, hi, qs, hn_) for ti, tkc in enumerate(tks) for hi, (qs, hn_) in enumerate(halves)]
            LA = 2
            for idx in range(len(items) + LA):
                if idx < len(items):
                    ti, tkc, hi, qs, hn_ = items[idx]
                    s = (it + idx) % NS
                    op("pe", lambda e: e.matmul(psS[s][:, 0:hn_], lhsT=kT[:, h, tkc * 128:(tkc + 1) * 128],
                                                rhs=qT[:, h, qs:qs + hn_], start=True, stop=True),
                       reads=[], writes=[f"psS{s}"])
                if idx >= LA:
                    ti, tkc, hi, qs, hn_ = items[idx - LA]
                    s = (it + idx - LA) % NS
                    op("act", lambda e: e.activation(out=pT[s][:, 0:hn_], in_=psS[s][:, 0:hn_], func=AF.Exp,
                                                     scale=scale), reads=[f"psS{s}"], writes=[f"pT{s}"])
                    op("pe", lambda e: e.matmul(acc[hi][:, 0:hn_], lhsT=V1[:, tkc, h, :], rhs=pT[s][:, 0:hn_],
                                                start=(ti == 0), stop=(ti == len(tks) - 1)),
                       reads=[f"pT{s}"], writes=[f"acc{hi}"], pe_acc=(ti > 0))
            it += len(items)
            for hi, (qs, hn_) in enumerate(halves):
                o = oc % 2
                oc += 1
                op("dve", lambda e: e.tensor_copy(out=osb[o][:, 0:hn_], in_=acc[hi][:, 0:hn_]),
                   reads=[f"acc{hi}"], writes=[f"aosb{o}"])
                op("dve", lambda e: e.reciprocal(out=rden[o][64:65, 0:hn_], in_=osb[o][64:65, 0:hn_]),
                   reads=[f"aosb{o}"], writes=[f"rden{o}"])
                op("pe", lambda e: e.matmul(psb[:, 0:hn_], lhsT=ones_f[64:65, 0:64], rhs=rden[o][64:65, 0:hn_],
                                            start=True, stop=True), reads=[f"rden{o}", "ones_f"], writes=["psb"])
                op("dve", lambda e: e.tensor_tensor(out=att[o][:, 0:hn_], in0=osb[o][0:64, 0:hn_],
                                                    in1=psb[:, 0:hn_], op=ALU.mult),
                   reads=[f"aosb{o}", "psb"], writes=[f"att{o}"])
                dma(ATT[h, :, qs:qs + hn_], att[o][:, 0:hn_], reads=[f"att{o}"], writes=[f"D:ATT{h}_{qs}"],
                    q="pool")
    fw.release(mk)
    fw.release(mk0)


def _zero_fill(G, name, rows):
    P, fw = G["P"], G["fw"]
    dst = _scr(P, name, [rows, T], BF16)
    mk = fw.mark()
    zt = fw.sbuf("zfill", [128, 512], BF16)
    fw.op("pool", lambda e: e.memset(zt[:], 0.0), writes=["zfill"])
    for r in range(rows // 128):
        for t0 in range(0, T, 512):
            tn = min(512, T - t0)
            fw.dma(dst[r * 128:(r + 1) * 128, t0:t0 + tn], zt[:, 0:tn], reads=["zfill"],
                   writes=[f"D:{name}{r}_{t0}"], q="pool")
    fw.release(mk)


def phase_D(G):
    P, fw, nc, li = G["P"], G["fw"], G["nc"], G["li"]
    op, dma = fw.op, fw.dma
    PT, DTRAW = G["PT"], G["DTRAW"]
    ones_f, ident_f, ident_b = G["ones_f"], G["ident_f"], G["ident_b"]
    w_conv = _inp(P, "w_conv_ssm", [DEPTH, 3, 640]); b_conv = _inp(P, "b_conv_ssm", [DEPTH, 640])
    a_log = _inp(P, "a_log", [DEPTH, 12]); dt_bias = _inp(P, "dt_bias", [DEPTH, 12])
    dcol_d = _inp(P, "d_skip_col", [DEPTH, 384]); g_so = _inp(P, "g_ssm_out", [DEPTH, 384])
    TRI = _inp(P, "tri", [2, 128, 128]); NEG = _inp(P, "negmask", [2, 128, 128])
    GMd = _inp(P, "gmask", [9, 128, 128])
    YF = _scr(P, "YF", [2, 384, T])
    SSMT = _scr(P, "SSMT", [SSM_INNER, T], BF16)
    W = T + 4

    mk0 = fw.mark()
    xsT = fw.sbuf("xsT", [128, 3, T])
    BC = [fw.sbuf(f"BC{i}", [64, T], BF16) for i in range(4)]
    xs_tok = fw.sbuf("xs_tok", [128, NCH, 384], BF16)
    B_tok = fw.sbuf("B_tok", [128, NCH, 128], BF16)
    wc = fw.sbuf("wc", [128, 5, 3]); bc = fw.sbuf("bc", [128, 5])
    wc2 = fw.sbuf("wc2", [64, 4, 3]); bc2 = fw.sbuf("bc2", [64, 4])
    for k in range(3):
        dma(wc[:, :, k], w_conv[li, k].rearrange("(r p) -> p r", p=128), writes=["wc"], allow_slow_non_contiguous=True)
        dma(wc2[:, :, k], w_conv[li, k, 384:640].rearrange("(r p) -> p r", p=64), writes=["wc2"],
            allow_slow_non_contiguous=True)
    dma(bc[:], b_conv[li].rearrange("(r p) -> p r", p=128), writes=["bc"], allow_slow_non_contiguous=True)
    dma(bc2[:], b_conv[li, 384:640].rearrange("(r p) -> p r", p=64), writes=["bc2"], allow_slow_non_contiguous=True)

    mk = fw.mark()
    xb = fw.sbuf("xb", [128, W]); acc = fw.sbuf("cacc", [128, W])
    op("pool", lambda e: e.memset(xb[:], 0.0), writes=["xb"])
    jobs = [(128, O_XBC + 128 * r, wc[:, r, :], bc[:, r:r + 1], ("xs", r)) for r in range(3)]
    jobs += [(64, O_XBC + 384 + 64 * i, wc2[:, i, :], bc2[:, i:i + 1], ("bc", i)) for i in range(4)]
    for (np_, r0, wv, bv, dst) in jobs:
        rk = [k for k in fw.lastw if k.startswith("D:PT") and r0 - 127 <= int(k[4:].split("_")[0]) <= r0 + np_ - 1]
        dma(xb[0:np_, 1:1 + CTX], PT[r0:r0 + np_, 0:CTX], reads=rk, writes=["xb"])
        dma(xb[0:np_, 3 + CTX:3 + T], PT[r0:r0 + np_, CTX:T], reads=rk, writes=["xb"], q="pool")
        op("dve", lambda e: e.tensor_scalar(out=acc[0:np_, 0:W - 2], in0=xb[0:np_, 0:W - 2], scalar1=wv[:, 0:1],
                                            scalar2=None, op0=ALU.mult), reads=["xb", "wc", "wc2"], writes=["cacc"])
        for k in (1, 2):
            op("dve", lambda e: e.scalar_tensor_tensor(out=acc[0:np_, 0:W - 2], in0=xb[0:np_, k:W - 2 + k],
                                                       scalar=wv[:, k:k + 1], in1=acc[0:np_, 0:W - 2],
                                                       op0=ALU.mult, op1=ALU.add),
               reads=["xb", "cacc", "wc", "wc2"], writes=["cacc"])
        for (a0, a1, d0) in ((0, CTX, 0), (2 + CTX, 2 + T, CTX)):
            if dst[0] == "xs":
                o_ = xsT[:, dst[1], d0:d0 + (a1 - a0)]
            else:
                o_ = BC[dst[1]][:, d0:d0 + (a1 - a0)]
            op("act", lambda e: e.activation(out=o_, in_=acc[0:np_, a0:a1], func=AF.Silu, bias=bv),
               reads=["cacc", "bc", "bc2"], writes=["xsT" if dst[0] == "xs" else f"BC{dst[1]}"])
    fw.release(mk)

    if G.get("dstop") == 1:
        fw.release(mk0); return
    mk = fw.mark()
    ptx_ = [fw.psum(f"ptx{s}", [128, 512]) for s in range(2)]
    ptx = [t[:, 0:384].rearrange("p (r c) -> p r c", c=128) for t in ptx_]
    ptb_ = [fw.psum(f"ptb{s}", [128, 1024], BF16) for s in range(2)]
    ptb = [t[:, 0:128].rearrange("p (g c) -> p g c", c=64) for t in ptb_]
    for ci in range(NCH):
        s = ci % 2
        for r in range(3):
            op("pe", lambda e: e.transpose(out=ptx[s][:, r, :], in_=xsT[:, r, ci * 128:(ci + 1) * 128],
                                           identity=ident_f[:]), reads=["xsT", "ident_f"], writes=[f"ptx{s}"],
               pe_acc=(r > 0))
        op("act", lambda e: e.copy(out=xs_tok[:, ci, :], in_=ptx_[s][:, 0:384]),
           reads=[f"ptx{s}"], writes=["xs_tok"])
        for g in range(2):
            op("pe", lambda e: e.transpose(out=ptb[s][:, g, :], in_=BC[g][:, ci * 128:(ci + 1) * 128],
                                           identity=ident_b[0:64, 0:64]), reads=[f"BC{g}", "ident_b"],
               writes=[f"ptb{s}"], pe_acc=(g > 0))
        op("dve", lambda e: e.tensor_copy(out=B_tok[:, ci, :], in_=ptb_[s][:, 0:128]),
           reads=[f"ptb{s}"], writes=["B_tok"])
    fw.release(mk)

    tri = fw.sbuf("tri", [128, 2, 128]); neg = fw.sbuf("neg", [128, 2, 128])
    dma(tri[:], TRI.rearrange("d p i -> p d i"), writes=["tri"])
    dma(neg[:], NEG.rearrange("d p i -> p d i"), writes=["neg"])
    rows = fw.sbuf("rows", [1, 24])
    dma(rows[:, 0:12], a_log[li:li + 1, :], writes=["rows"])
    dma(rows[:, 12:24], dt_bias[li:li + 1, :], writes=["rows"])
    arep = fw.sbuf("arep", [128, 24])
    dtr = fw.sbuf("dtr", [128, NCH, 12]); dt = fw.sbuf("dt", [128, NCH, 12]); dta = fw.sbuf("dta", [128, NCH, 12])
    cum = fw.sbuf("cum", [128, NCH, 12]); tot = fw.sbuf("tot", [128, NCH, 12])
    wend = fw.sbuf("wend", [128, NCH, 12]); cdec = fw.sbuf("cdec", [128, NCH, 12]); ncum = fw.sbuf("ncum", [128, NCH, 12])
    mk = fw.mark()
    psr_ = fw.psum("psrow", [128, 512])[:, 0:24]
    psc = fw.psum("psc", [128, 512])[:, 0:NCH * 12].rearrange("p (c n) -> p c n", n=12)
    op("pe", lambda e: e.matmul(psr_[:], lhsT=ones_f[0:1, :], rhs=rows[0:1, :], start=True, stop=True),
       reads=["rows", "ones_f"], writes=["psrow"])
    op("act", lambda e: e.activation(out=arep[:, 0:12], in_=psr_[:, 0:12], func=AF.Exp), reads=["psrow"], writes=["arep"])
    op("dve", lambda e: e.tensor_scalar(out=arep[:, 0:12], in0=arep[:, 0:12], scalar1=-1.0, scalar2=None, op0=ALU.mult),
       reads=["arep"], writes=["arep"])
    op("dve", lambda e: e.tensor_copy(out=arep[:, 12:24], in_=psr_[:, 12:24]), reads=["psrow"], writes=["arep"])
    dma(dtr[:], DTRAW.rearrange("(c p) n -> p c n", p=128), reads=["D:DTRAW"], writes=["dtr"])
    for ci in range(NCH):
        op("dve", lambda e: e.tensor_tensor(out=dtr[:, ci, :], in0=dtr[:, ci, :], in1=arep[:, 12:24], op=ALU.add),
           reads=["dtr", "arep"], writes=["dtr"])
    op("act", lambda e: e.activation(out=dt[:], in_=dtr[:], func=AF.Exp), reads=["dtr"], writes=["dt"])
    op("act", lambda e: e.activation(out=dt[:], in_=dt[:], func=AF.Ln, bias=1.0), reads=["dt"], writes=["dt"])
    for ci in range(NCH):
        op("dve", lambda e: e.tensor_tensor(out=dta[:, ci, :], in0=dt[:, ci, :], in1=arep[:, 0:12], op=ALU.mult),
           reads=["dt", "arep"], writes=["dta"])
    for d in range(2):
        op("pe", lambda e: e.matmul(psc[:], lhsT=tri[:, d, :], rhs=dta[:], start=True, stop=True),
           reads=["tri", "dta", "cum"], writes=["psc"])
        op("dve", lambda e: e.tensor_copy(out=cum[:, :, d * 6:(d + 1) * 6], in_=psc[:, :, d * 6:(d + 1) * 6]),
           reads=["psc"], writes=["cum"])
    op("pe", lambda e: e.matmul(psc[:], lhsT=ones_f[:], rhs=dta[:], start=True, stop=True),
       reads=["ones_f", "dta", "cum"], writes=["psc"])
    op("dve", lambda e: e.tensor_copy(out=tot[:], in_=psc[:]), reads=["psc"], writes=["tot"])
    op("dve", lambda e: e.tensor_tensor(out=wend[:], in0=tot[:], in1=cum[:], op=ALU.subtract),
       reads=["tot", "cum"], writes=["wend"])
    op("act", lambda e: e.activation(out=wend[:], in_=wend[:], func=AF.Exp), reads=["wend"], writes=["wend"])
    op("dve", lambda e: e.tensor_tensor(out=wend[:], in0=wend[:], in1=dt[:], op=ALU.mult), reads=["wend", "dt"], writes=["wend"])
    op("act", lambda e: e.activation(out=cdec[:], in_=tot[:], func=AF.Exp), reads=["tot"], writes=["cdec"])
    op("dve", lambda e: e.tensor_scalar(out=ncum[:], in0=cum[:], scalar1=-1.0, scalar2=None, op0=ALU.mult),
       reads=["cum"], writes=["ncum"])
    fw.release(mk)

    if G.get("dstop") == 2:
        fw.release(mk0); return
    mk = fw.mark()
    S = fw.sbuf("S", [64, NH, 64]); Sb = fw.sbuf("Sb", [64, NH, 64], BF16)
    Rm = [fw.sbuf(f"Rm{s}", [128, 128]) for s in range(2)]
    Em = [fw.sbuf(f"Em{s}", [64, 128]) for s in range(2)]
    sg = [fw.sbuf(f"sg{s}", [128, 128]) for s in range(2)]
    dc = [fw.sbuf(f"dc{s}", [128, 128]) for s in range(2)]
    MT = [fw.sbuf(f"MT{s}", [128, 128], BF16) for s in range(2)]
    CdT = [fw.sbuf(f"CdT{s}", [64, 128], BF16) for s in range(2)]
    Bw = [fw.sbuf(f"Bw{s}", [128, 64], BF16) for s in range(2)]
    ysb = [fw.sbuf(f"ysb{s}", [64, 128]) for s in range(2)]
    psG = [fw.psum(f"psG{g}", [128, 512])[:, 0:128] for g in range(2)]
    psA = [fw.psum(f"psA{s}", [128, 512])[:, 0:128] for s in range(2)]
    psY = [fw.psum(f"psY{s}", [128, 512])[0:64, 0:128] for s in range(2)]
    psS = [fw.psum(f"psSt{s}", [128, 512])[0:64, 0:64] for s in range(2)]
    it = 0
    for d in range(2):
        order = list(range(NCH)) if d == 0 else [1, 0] + list(range(NCH - 1, 1, -1))
        if DFLAG[0] & 16:
            order = order[:2]
        if DFLAG[0] & 32:
            order = order[:10]
        op("pool", lambda e: e.memset(S[:], 0.0), writes=["S"] + [f"S{h}" for h in range(NH)])
        op("pool", lambda e: e.memset(Sb[:], 0.0), writes=["Sb"] + [f"Sb{h}" for h in range(NH)])
        for ci in order:
            c0 = ci * 128
            for g in range(2):
                op("pe", lambda e: e.matmul(psG[g][:], lhsT=BC[g][:, c0:c0 + 128], rhs=BC[2 + g][:, c0:c0 + 128],
                                            start=True, stop=True), reads=[], writes=[f"psG{g}"])
            def ssd_chain(h, s, ci=ci, c0=c0, d=d):
                g = h // 3
                dh = d * 6 + h
                op("dve", lambda e: e.tensor_scalar(out=Rm[s][:], in0=tri[:, d, :], scalar1=dta[:, ci, dh:dh + 1],
                                                    scalar2=None, op0=ALU.mult), reads=[], writes=[f"Rm{s}"])
                yield
                op("pe", lambda e: e.matmul(psA[s][:], lhsT=ones_f[:], rhs=Rm[s][:], start=True, stop=True),
                   reads=[f"Rm{s}"], writes=[f"psA{s}"])
                yield
                op("act", lambda e: e.activation(out=Em[s][:], in_=psA[s][0:64, :], func=AF.Exp),
                   reads=[f"psA{s}"], writes=[f"Em{s}"])
                yield
                op("dve", lambda e: e.tensor_tensor(out=sg[s][:], in0=psA[s][:], in1=neg[:, d, :], op=ALU.add),
                   reads=[f"psA{s}"], writes=[f"sg{s}"])
                yield
                op("act", lambda e: e.activation(out=dc[s][:], in_=sg[s][:], func=AF.Exp, bias=ncum[:, ci, dh:dh + 1]),
                   reads=[f"sg{s}"], writes=[f"dc{s}"])
                op("pool", lambda e: e.tensor_scalar(out=Bw[s][:], in0=B_tok[:, ci, g * 64:(g + 1) * 64],
                                                     scalar1=wend[:, ci, dh:dh + 1], scalar2=None, op0=ALU.mult),
                   reads=[], writes=[f"Bw{s}"])
                yield
                op("dve", lambda e: e.scalar_tensor_tensor(out=MT[s][:], in0=dc[s][:], scalar=dt[:, ci, dh:dh + 1],
                                                           in1=psG[g][:], op0=ALU.mult, op1=ALU.mult),
                   reads=[f"dc{s}", f"psG{g}"], writes=[f"MT{s}"])
                yield
                op("dve", lambda e: e.tensor_tensor(out=CdT[s][:], in0=BC[2 + g][:, c0:c0 + 128], in1=Em[s][:],
                                                    op=ALU.mult), reads=[f"Em{s}"], writes=[f"CdT{s}"])
                yield
                op("pe", lambda e: e.matmul(psY[s][:], lhsT=xs_tok[:, ci, h * 64:(h + 1) * 64], rhs=MT[s][:],
                                            start=True, stop=False), reads=[f"MT{s}"], writes=[f"psY{s}"])
                op("pe", lambda e: e.matmul(psY[s][:], lhsT=Sb[:, h, :], rhs=CdT[s][:], start=False, stop=True),
                   reads=[f"CdT{s}", f"Sb{h}"], writes=[f"psY{s}"], pe_acc=True)
                op("pe", lambda e: e.matmul(psS[s][:], lhsT=Bw[s][:], rhs=xs_tok[:, ci, h * 64:(h + 1) * 64],
                                            start=True, stop=True), reads=[f"Bw{s}"], writes=[f"psSt{s}"])
                yield
                op("act", lambda e: e.copy(out=ysb[s][:], in_=psY[s][:]), reads=[f"psY{s}"], writes=[f"ysb{s}"])
                yield
                dma(YF[d, h * 64:(h + 1) * 64, c0:c0 + 128], ysb[s][:], reads=[f"ysb{s}"],
                    writes=[f"D:YF{d}_{h}_{ci}"], q="sp" if h % 2 else "pool")
                op("dve", lambda e: e.scalar_tensor_tensor(out=S[:, h, :], in0=S[:, h, :],
                                                           scalar=cdec[0:64, ci, dh:dh + 1], in1=psS[s][:],
                                                           op0=ALU.mult, op1=ALU.add),
                   reads=[f"psSt{s}", f"S{h}"], writes=[f"S{h}"])
                yield
                op("act", lambda e: e.copy(out=Sb[:, h, :], in_=S[:, h, :]), reads=[f"S{h}"], writes=[f"Sb{h}"])
                yield

            for h in range(0, NH, 2):
                _interleave([ssd_chain(h, 0), ssd_chain(h + 1, 1)])
    fw.release(mk)

    if G.get("dstop") == 3:
        fw.release(mk0); return
    mk = fw.mark()
    gm = fw.sbuf("gm", [128, 9, 128])
    dcol = fw.sbuf("dcol", [128, 3]); gso = fw.sbuf("gso", [128, 3])
    dma(gm[:], GMd.rearrange("a p i -> p a i"), writes=["gm"])
    dma(dcol[:], dcol_d[li].rearrange("(r p) -> p r", p=128), writes=["dcol"], allow_slow_non_contiguous=True)
    dma(gso[:], g_so[li].rearrange("(r p) -> p r", p=128), writes=["gso"], allow_slow_non_contiguous=True)
    y0 = [fw.sbuf(f"y0_{r}", [128, 512]) for r in range(3)]
    y1 = [fw.sbuf(f"y1_{r}", [128, 512]) for r in range(3)]
    zt = [fw.sbuf(f"zt_{r}", [128, 512]) for r in range(3)]
    sq = [fw.sbuf(f"dsq_{r}", [128, 512]) for r in range(3)]
    rs = fw.sbuf("drs", [128, 512])
    so = [fw.sbuf(f"so{s}", [128, 512], BF16) for s in range(2)]
    psn = [fw.psum(f"psn{s}", [128, 512]) for s in range(2)]
    tblocks = [(t0, min(512, T - t0)) for t0 in range(0, T, 512)]
    n = 0
    for (t0, tn) in tblocks:
        for r in range(3):
            rk = [f"D:YF{d}_{h}_{ci}" for d in range(2) for h in (2 * r, 2 * r + 1)
                  for ci in range(t0 // 128, (t0 + tn) // 128)]
            dma(y0[r][:, 0:tn], YF[0, r * 128:(r + 1) * 128, t0:t0 + tn], reads=rk, writes=[f"y0_{r}"])
            dma(y1[r][:, 0:tn], YF[1, r * 128:(r + 1) * 128, t0:t0 + tn], reads=rk, writes=[f"y1_{r}"], q="pool")
            dma(zt[r][:, 0:tn], PT[O_Z + r * 128:O_Z + (r + 1) * 128, t0:t0 + tn],
                reads=[f"D:PT{O_Z + r * 128}_{t0}"], writes=[f"zt_{r}"])
            op("pool", lambda e: e.tensor_tensor(out=y0[r][:, 0:tn], in0=y0[r][:, 0:tn], in1=y1[r][:, 0:tn], op=ALU.add),
               reads=[f"y0_{r}", f"y1_{r}"], writes=[f"y0_{r}"])
            op("dve", lambda e: e.scalar_tensor_tensor(out=y0[r][:, 0:tn], in0=xsT[:, r, t0:t0 + tn],
                                                       scalar=dcol[:, r:r + 1], in1=y0[r][:, 0:tn],
                                                       op0=ALU.mult, op1=ALU.add),
               reads=["xsT", "dcol", f"y0_{r}"], writes=[f"y0_{r}"])
            op("act", lambda e: e.activation(out=zt[r][:, 0:tn], in_=zt[r][:, 0:tn], func=AF.Silu),
               reads=[f"zt_{r}"], writes=[f"zt_{r}"])
            op("dve", lambda e: e.tensor_tensor(out=y0[r][:, 0:tn], in0=y0[r][:, 0:tn], in1=zt[r][:, 0:tn], op=ALU.mult),
               reads=[f"y0_{r}", f"zt_{r}"], writes=[f"y0_{r}"])
            op("act", lambda e: e.activation(out=sq[r][:, 0:tn], in_=y0[r][:, 0:tn], func=AF.Square),
               reads=[f"y0_{r}"], writes=[f"dsq_{r}"])
        for r2 in range(3):
            s = n % 2; n += 1
            for r in range(3):
                op("pe", lambda e: e.matmul(psn[s][:, 0:tn], lhsT=gm[:, r * 3 + r2, :], rhs=sq[r][:, 0:tn],
                                            start=(r == 0), stop=(r == 2)),
                   reads=[f"dsq_{r}", "gm"], writes=[f"psn{s}"], pe_acc=(r > 0))
            _rsqrt(fw, rs[:, 0:tn], psn[s][:, 0:tn], 1.0 / 192, f"psn{s}", "drs")
            op("dve", lambda e: e.scalar_tensor_tensor(out=so[s][:, 0:tn], in0=y0[r2][:, 0:tn],
                                                       scalar=gso[:, r2:r2 + 1], in1=rs[:, 0:tn],
                                                       op0=ALU.mult, op1=ALU.mult),
               reads=[f"y0_{r2}", "gso", "drs"], writes=[f"so{s}"])
            dma(SSMT[r2 * 128:(r2 + 1) * 128, t0:t0 + tn], so[s][:, 0:tn], reads=[f"so{s}"],
                writes=[f"D:SSMT{r2}_{t0}"], q="pool")
    fw.release(mk)
    fw.release(mk0)


def _sin5(fw, dst, ps, pskey, f5, fb5, tmps, n, tag):
    s_, s2, t_ = tmps
    fw.op("act", lambda e: e.activation(out=s_[:, 0:n], in_=ps[:, 0:n], func=AF.Sin, scale=f5, bias=fb5),
          reads=[pskey, "hyvec"], writes=[tag + "s"])
    fw.op("act", lambda e: e.activation(out=s2[:, 0:n], in_=s_[:, 0:n], func=AF.Square), reads=[tag + "s"], writes=[tag + "s2"])
    fw.op("dve", lambda e: e.tensor_scalar(out=t_[:, 0:n], in0=s2[:, 0:n], scalar1=16.0, scalar2=-20.0,
                                           op0=ALU.mult, op1=ALU.add), reads=[tag + "s2"], writes=[tag + "t"])
    fw.op("dve", lambda e: e.tensor_tensor(out=t_[:, 0:n], in0=t_[:, 0:n], in1=s2[:, 0:n], op=ALU.mult),
          reads=[tag + "t", tag + "s2"], writes=[tag + "t"])
    fw.op("dve", lambda e: e.scalar_tensor_tensor(out=dst, in0=t_[:, 0:n], scalar=5.0, in1=s_[:, 0:n],
                                                  op0=ALU.add, op1=ALU.mult), reads=[tag + "t", tag + "s"], writes=[tag + "h"])


def phase_E(G):
    P, fw, nc, li = G["P"], G["fw"], G["nc"], G["li"]
    op, dma = fw.op, fw.dma
    PT = G["PT"]
    ones_f, ident_f = G["ones_f"], G["ident_f"]
    last = li == DEPTH - 1
    w_conv = _inp(P, "w_conv_hy", [DEPTH, 3, 768]); b_conv = _inp(P, "b_conv_hy", [DEPTH, 768])
    w_f1 = _inp(P, "w_f1", [DEPTH, 33, 64]); w_f2 = _inp(P, "w_f2", [DEPTH, 64, 64]); w_f3 = _inp(P, "w_f3", [DEPTH, 64, 1024])
    hyvec_d = _inp(P, "hy_vecs", [DEPTH, 4, 64])
    d_skip = _inp(P, "d_skip_hy", [DEPTH, 2, 256])
    ndl_d = _inp(P, "hy_ndelta", [128, 2])
    JR = _inp(P, "jrev", [128, 128])
    HYT = _scr(P, "HYT", [HY_CH, T], BF16)

    seqs = [(SEQ, CTX)] if last else [(CTX, 0), (SEQ, CTX)]
    for (L, tok0) in seqs:
        NJ = L // 128
        FWD = 2 * L
        feats_d = _inp(P, f"hy_feats_{L}", [2, 33, L])
        t01_d = _inp(P, f"hy_t01_{L}", [2, 128, L])
        if f"FD{L}" not in P.dscr:
            P.dh = getattr(P, "dh", {})
            P.dh[f"FD{L}"] = nc.dram_tensor(f"FD{L}", [2, 256, FWD], BF16, kind="ExternalOutput" if "FILT" in P.debug and L == SEQ else "Internal")
            P.dscr[f"FD{L}"] = P.dh[f"FD{L}"].ap()
        FDh = P.dh[f"FD{L}"]; FD = P.dscr[f"FD{L}"]

        mk = fw.mark()
        w1 = fw.sbuf("w1", [33, 64]); w2 = fw.sbuf("w2", [64, 64]); w3 = fw.sbuf("w3", [64, 1024])
        hv = fw.sbuf("hv", [64, 4]); f5 = fw.sbuf("f5", [64, 2]); fb5 = fw.sbuf("fb5", [64, 2])
        ndl = fw.sbuf("ndl", [128, 2])
        dma(w1[:], w_f1[li], writes=["w1"]); dma(w2[:], w_f2[li], writes=["w2"]); dma(w3[:], w_f3[li], writes=["w3"])
        dma(hv[:], hyvec_d[li].rearrange("v p -> p v"), writes=["hv"], allow_slow_non_contiguous=True)
        dma(ndl[:], ndl_d, writes=["ndl"])
        for i in range(2):
            op("dve", lambda e: e.tensor_scalar(out=f5[:, i:i + 1], in0=hv[:, 2 * i + 1:2 * i + 2], scalar1=0.2,
                                                scalar2=None, op0=ALU.mult), reads=["hv"], writes=["hyvec"])
            op("dve", lambda e: e.tensor_tensor(out=fb5[:, i:i + 1], in0=f5[:, i:i + 1], in1=hv[:, 2 * i:2 * i + 1],
                                                op=ALU.mult), reads=["hyvec", "hv"], writes=["hyvec"])
        FB = [[fw.sbuf(f"FB{o}{ch}", [128, FWD]) for ch in range(2)] for o in range(2)]
        featb = [fw.sbuf(f"featb{s}", [33, 512]) for s in range(2)]
        t01b = [fw.sbuf(f"t01b{s}", [128, 512]) for s in range(2)]
        tm = [[fw.sbuf(f"tm{a}{b}", [64, 512]) for b in range(3)] for a in range(2)]
        h1 = fw.sbuf("h1", [64, 512]); h2 = fw.sbuf("h2", [64, 512])
        dec = [fw.sbuf(f"dec{ch}", [128, 512]) for ch in range(2)]
        pm1 = fw.psum("pm1", [64, 512]); pm2 = fw.psum("pm2", [64, 512])
        pm3 = [fw.psum(f"pm3{s}", [128, 512]) for s in range(2)]
        nrm = fw.sbuf("nrm", [128, 4])
        FBb = [fw.sbuf(f"FBb{s}", [128, FWD], BF16) for s in range(2)]
        BL = min(512, L)
        n3 = 0
        for dr in (1, 0):
            for bi, c0 in enumerate(range(0, L, BL)):
                s = bi % 2
                dma(featb[s][:, 0:BL], feats_d[dr, :, c0:c0 + BL], writes=[f"featb{s}"])
                dma(t01b[s][:, 0:BL], t01_d[dr, :, c0:c0 + BL], writes=[f"t01b{s}"], q="pool")
                op("pe", lambda e: e.matmul(pm1[:, 0:BL], lhsT=w1[:], rhs=featb[s][:, 0:BL], start=True, stop=True),
                   reads=["w1", f"featb{s}"], writes=["pm1"])
                _sin5(fw, h1[:, 0:BL], pm1, "pm1", f5[:, 0:1], fb5[:, 0:1], tm[0], BL, "a")
                op("pe", lambda e: e.matmul(pm2[:, 0:BL], lhsT=w2[:], rhs=h1[:, 0:BL], start=True, stop=True),
                   reads=["w2", "ah"], writes=["pm2"])
                _sin5(fw, h2[:, 0:BL], pm2, "pm2", f5[:, 1:2], fb5[:, 1:2], tm[1], BL, "b")
                col0 = (L - 1 + c0) if dr == 0 else c0
                for ch in range(2):
                    op("act", lambda e: e.activation(out=dec[ch][:, 0:BL], in_=t01b[s][:, 0:BL], func=AF.Exp,
                                                     scale=ndl[:, ch:ch + 1]),
                       reads=[f"t01b{s}", "ndl"], writes=[f"dec{ch}"])
                    for o in range(2):
                        p3 = n3 % 2; n3 += 1
                        cb = dr * 512 + o * 256 + ch * 128
                        op("pe", lambda e: e.matmul(pm3[p3][:, 0:BL], lhsT=w3[:, cb:cb + 128], rhs=h2[:, 0:BL],
                                                    start=True, stop=True), reads=["w3", "bh"], writes=[f"pm3{p3}"])
                        op("dve", lambda e: e.tensor_tensor(out=FB[o][ch][:, col0:col0 + BL], in0=pm3[p3][:, 0:BL],
                                                            in1=dec[ch][:, 0:BL], op=ALU.mult),
                           reads=[f"pm3{p3}", f"dec{ch}"], writes=[f"FB{o}{ch}"])
        for o in range(2):
            for ch in range(2):
                k = o * 2 + ch
                op("dve", lambda e: e.tensor_reduce(out=nrm[:, k:k + 1], in_=FB[o][ch][:, 0:2 * L - 1], axis=AX.X,
                                                    op=ALU.add, apply_absolute_value=True),
                   reads=[f"FB{o}{ch}"], writes=[f"nrm{k}"])
                op("dve", lambda e: e.tensor_scalar(out=nrm[:, k:k + 1], in0=nrm[:, k:k + 1], scalar1=EPS, scalar2=None,
                                                    op0=ALU.add), reads=[f"nrm{k}"], writes=[f"nrm{k}"])
                op("dve", lambda e: e.reciprocal(out=nrm[:, k:k + 1], in_=nrm[:, k:k + 1]), reads=[f"nrm{k}"], writes=[f"nrm{k}"])
                op("pool" if k % 2 else "dve",
                   lambda e: e.tensor_scalar(out=FBb[k % 2][:, 0:2 * L - 1], in0=FB[o][ch][:, 0:2 * L - 1],
                                             scalar1=nrm[:, k:k + 1], scalar2=None, op0=ALU.mult),
                   reads=[f"nrm{k}", f"FB{o}{ch}"], writes=[f"FBb{k % 2}"])
                dma(FD[o, ch * 128:(ch + 1) * 128, 0:2 * L - 1], FBb[k % 2][:, 0:2 * L - 1], reads=[f"FBb{k % 2}"],
                    writes=[f"D:FD{L}"], q="sp" if k % 2 else "pool")
        fw.release(mk)

        GW = (2 * NJ - 1) * 128
        CG = 512 // NJ if NJ <= 32 else 16
        CG = min(CG, 16)
        for ch in range(2):
            mk = fw.mark()
            Pq = [fw.sbuf(f"Pq{q}", [128, NJ, 128]) for q in range(3)]
            zb = [fw.sbuf(f"zb{s}", [128, NJ, 128]) for s in range(2)]
            zrev = fw.sbuf("zrev", [128, NJ, 128], BF16)
            zd = fw.sbuf("zd", [128, NJ, 128])
            dsk = fw.sbuf("dsk", [128, 128])
            drow = fw.sbuf("drow", [1, 128])
            jr = fw.sbuf("jr", [128, 128])
            wch = fw.sbuf("wch", [128, 3, 3]); bch = fw.sbuf("bch", [128, 3])
            dma(jr[:], JR, writes=["jr"])
            for q in range(3):
                r0 = q * 256 + ch * 128
                for k in range(3):
                    dma(wch[:, q, k:k + 1], w_conv[li, k, r0:r0 + 128].rearrange("(p o) -> p o", o=1), writes=["wch"],
                        allow_slow_non_contiguous=True)
                dma(bch[:, q:q + 1], b_conv[li, r0:r0 + 128].rearrange("(p o) -> p o", o=1), writes=["bch"],
                    allow_slow_non_contiguous=True)
            mk2 = fw.mark()
            xb = fw.sbuf("hxb", [128, L + 2]); pT_ = fw.sbuf("hpT", [128, L])
            ptr = [fw.psum(f"hptr{s}", [128, 4, 128]) for s in range(2)]
            op("pool", lambda e: e.memset(xb[:], 0.0), writes=["hxb"])
            nt = 0
            for q in range(3):
                r0 = O_HY + q * 256 + ch * 128
                rk = [k for k in fw.lastw if k.startswith("D:PT") and r0 - 127 <= int(k[4:].split("_")[0]) <= r0 + 127]
                dma(xb[:, 1:1 + L], PT[r0:r0 + 128, tok0:tok0 + L], reads=rk, writes=["hxb"])
                op("dve", lambda e: e.tensor_scalar(out=pT_[:], in0=xb[:, 0:L], scalar1=wch[:, q, 0:1], scalar2=bch[:, q:q + 1],
                                                    op0=ALU.mult, op1=ALU.add), reads=["hxb", "wch", "bch"], writes=["hpT"])
                for k in (1, 2):
                    op("dve", lambda e: e.scalar_tensor_tensor(out=pT_[:], in0=xb[:, k:k + L], scalar=wch[:, q, k:k + 1],
                                                               in1=pT_[:], op0=ALU.mult, op1=ALU.add),
                       reads=["hxb", "hpT", "wch"], writes=["hpT"])
                for j0 in range(0, NJ, 4):
                    s = nt % 2; nt += 1
                    jn = min(4, NJ - j0)
                    for j in range(jn):
                        op("pe", lambda e: e.transpose(out=ptr[s][:, j, :], in_=pT_[:, (j0 + j) * 128:(j0 + j + 1) * 128],
                                                       identity=ident_f[:]), reads=["hpT", "ident_f"], writes=[f"hptr{s}"],
                           pe_acc=(j > 0))
                    op("act", lambda e: e.copy(out=Pq[q][:, j0:j0 + jn, :], in_=ptr[s][:, 0:jn, :]),
                       reads=[f"hptr{s}"], writes=[f"Pq{q}"])
            fw.release(mk2)

            NG = 4
            Gt = [fw.sbuf(f"Gt{s}", [128, GW], BF16) for s in range(NG)]
            psJ = [fw.psum(f"psJ{s}", [128, 4, 128]) for s in range(2)]
            Yb = [fw.psum(f"Yb{s}", [128, CG, NJ]) for s in range(2)]
            psd_ = fw.psum("hpsd", [128, 128])
            hout = [fw.sbuf(f"hout{s}", [128, 512], BF16) for s in range(2)]
            zcur = Pq[0]; zkey = "Pq0"
            ng = 0; ny = 0
            for o in range(2):
                dma(drow[:], d_skip[li, o:o + 1, ch * 128:(ch + 1) * 128], writes=["drow"])
                op("pe", lambda e: e.matmul(psd_[:], lhsT=ones_f[0:1, :], rhs=drow[0:1, :], start=True, stop=True),
                   reads=["drow", "ones_f"], writes=["hpsd"])
                op("act", lambda e: e.copy(out=dsk[:], in_=psd_[:]), reads=["hpsd"], writes=["dsk"])
                for j in range(NJ):
                    op("pool" if j % 2 else "dve", lambda e: e.tensor_tensor(out=zd[:, j, :], in0=zcur[:, j, :], in1=dsk[:],
                                                                             op=ALU.mult),
                       reads=[zkey, "dsk"], writes=["zd"])
                for j0 in range(0, NJ, 4):
                    s = (j0 // 4) % 2
                    jn = min(4, NJ - j0)
                    op("pe", lambda e: e.matmul(psJ[s][:, 0:jn, :], lhsT=jr[:], rhs=zcur[:, j0:j0 + jn, :], start=True, stop=True),
                       reads=[zkey, "jr"], writes=[f"psJ{s}"])
                    op("act", lambda e: e.copy(out=zrev[:, j0:j0 + jn, :], in_=psJ[s][:, 0:jn, :]),
                       reads=[f"psJ{s}"], writes=["zrev"])
                znext = zb[o]; nkey = f"zb{o}"
                for c0 in range(0, 128, CG):
                    yb = ny % 2; ny += 1
                    for cc in range(CG):
                        c = c0 + cc
                        gs = ng % NG; ng += 1
                        row = (o * 256 + ch * 128 + c) * FWD
                        dma(Gt[gs][:], bass.AP(tensor=FDh, offset=row, ap=[[1, 128], [1, GW]]),
                            reads=[f"D:FD{L}"], writes=[f"Gt{gs}"], q="sp" if ng % 2 else "act")
                        ds_ = [0] + [d for d in range(-(NJ - 1), NJ) if d != 0]
                        for di, d in enumerate(ds_):
                            J0 = max(0, -d); J1 = min(NJ, NJ - d)
                            op("pe", lambda e: e.matmul(Yb[yb][:, cc, J0 + d:J1 + d],
                                                        lhsT=Gt[gs][:, (d + NJ - 1) * 128:(d + NJ) * 128],
                                                        rhs=zrev[:, J0:J1, c], start=(di == 0), stop=(di == len(ds_) - 1)),
                               reads=[f"Gt{gs}", "zrev"], writes=[f"Yb{yb}"], pe_acc=(di > 0))
                    zv = znext[:, :, c0:c0 + CG].rearrange("p j c -> p c j")
                    op("dve", lambda e: e.tensor_tensor(out=zv, in0=Yb[yb][:], in1=zd[:, :, c0:c0 + CG].rearrange("p j c -> p c j"),
                                                        op=ALU.add), reads=[f"Yb{yb}", "zd"], writes=[nkey])
                    op("pool", lambda e: e.tensor_tensor(out=zv, in0=zv, in1=Pq[o + 1][:, :, c0:c0 + CG].rearrange("p j c -> p c j"),
                                                         op=ALU.mult), reads=[nkey, f"Pq{o + 1}"], writes=[nkey])
                zcur = znext; zkey = nkey
            for j0 in range(0, NJ, 4):
                s = (j0 // 4) % 2
                jn = min(4, NJ - j0)
                for j in range(jn):
                    op("pe", lambda e: e.transpose(out=psJ[s][:, j, :], in_=zcur[:, j0 + j, :], identity=ident_f[:]),
                       reads=[zkey, "ident_f"], writes=[f"psJ{s}"], pe_acc=(j > 0))
                op("act", lambda e: e.copy(out=hout[s][:, 0:jn * 128], in_=psJ[s][:, 0:jn, :].rearrange("p j t -> p (j t)")),
                   reads=[f"psJ{s}"], writes=[f"hout{s}"])
                dma(HYT[ch * 128:(ch + 1) * 128, tok0 + j0 * 128:tok0 + (j0 + jn) * 128], hout[s][:, 0:jn * 128],
                    reads=[f"hout{s}"], writes=[f"D:HYT{ch}_{tok0 + j0 * 128}"], q="pool")
            fw.release(mk)
    if last:
        pass


def phase_F(G):
    P, fw, nc, li = G["P"], G["fw"], G["nc"], G["li"]
    op, dma = fw.op, fw.dma
    XR, MOD, OUT = G["XR"], G["MOD"], G["OUT"]
    ident_b = G["ident_b"]
    last = li == DEPTH - 1
    w_out = _inp(P, "w_out", [DEPTH, D, D])
    w_ff1 = _inp(P, "w_ff1", [DEPTH, D, DFF]); w_ff2 = _inp(P, "w_ff2", [DEPTH, DFF, D])
    ATT = _scr(P, "ATT", [NH, 64, T], BF16)
    SSMT = _scr(P, "SSMT", [SSM_INNER, T], BF16)
    HYT = _scr(P, "HYT", [HY_CH, T], BF16)

    mk0 = fw.mark()
    w1b = fw.sbuf("w1b", [128, 8, DFF], BF16)
    w2b = fw.sbuf("w2b", [128, 32, D], BF16)
    wo_att = fw.sbuf("wo_att", [64, NH, D], BF16)
    wo_rest = fw.sbuf("wo_rest", [128, 5, D], BF16)
    mk = fw.mark()
    stg = [fw.sbuf(f"fstg{s}", [128, 8, 512]) for s in range(2)]
    n = 0
    for nb in range(8):
        s = n % 2; n += 1
        _load_cast(fw, w1b[:, :, nb * 512:(nb + 1) * 512], w_ff1[li, :, nb * 512:(nb + 1) * 512]
                   .rearrange("(k p) n -> p k n", p=128), stg[s][:], "w1b", f"fstg{s}", q="sp" if n % 2 else "pool")
    for kb in range(4):
        for half in range(2):
            s = n % 2; n += 1
            _load_cast(fw, w2b[:, kb * 8:(kb + 1) * 8, half * 512:(half + 1) * 512],
                       w_ff2[li, kb * 1024:(kb + 1) * 1024, half * 512:(half + 1) * 512]
                       .rearrange("(k p) n -> p k n", p=128), stg[s][:], "w2b", f"fstg{s}",
                       q="sp" if n % 2 else "pool")
    for half in range(2):
        s = n % 2; n += 1
        _load_cast(fw, wo_att[:, :, half * 512:(half + 1) * 512],
                   w_out[li, 0:384, half * 512:(half + 1) * 512].rearrange("(h c) n -> c h n", c=64),
                   stg[s][0:64, 0:6, :], "wo_att", f"fstg{s}")
        s = n % 2; n += 1
        _load_cast(fw, wo_rest[:, :, half * 512:(half + 1) * 512],
                   w_out[li, 384:1024, half * 512:(half + 1) * 512].rearrange("(k p) n -> p k n", p=128),
                   stg[s][:, 0:5, :], "wo_rest", f"fstg{s}")
    fw.release(mk)

    mods = [fw.sbuf(f"fmod{d}", [128, D]) for d in range(4)]
    attb = [fw.sbuf(f"attb{s}", [64, NH, 128], BF16) for s in range(2)]
    ssmb = [fw.sbuf(f"ssmb{s}", [128, 3, 128], BF16) for s in range(2)]
    hyb = [fw.sbuf(f"hyb{s}", [128, 2, 128], BF16) for s in range(2)]
    xt = [fw.sbuf(f"fxt{s}", [128, D]) for s in range(2)]
    xm = [fw.sbuf(f"fxm{s}", [128, D]) for s in range(2)]
    hb = fw.sbuf("fhb", [128, D], BF16)
    h2T = fw.sbuf("h2T", [128, 8, 128], BF16)
    aT = fw.sbuf("aT", [128, 32, 128], BF16)
    rr = [fw.sbuf(f"frr{s}", [128, 128]) for s in range(3)]
    ss = fw.sbuf("fss", [128, NCH]); rstd = fw.sbuf("frstd", [128, NCH])
    ot = [fw.sbuf(f"fot{s}", [128, 512]) for s in range(2)]
    psm = [fw.psum(f"psm{s}", [128, 512]) for s in range(2)]
    pst = fw.psum("fpst", [128, 8, 128], BF16)
    psf = [fw.psum(f"psf{s}", [128, 128]) for s in range(3)]
    pso = [fw.psum(f"fpso{s}", [128, 512]) for s in range(2)]

    chunks = list(range(2 if last else 0, NCH))
    state = {"kind": None, "n": 0}

    def front(ci, b):
        kind = 1 if ci < 2 else 0
        t0 = ci * 128
        if kind != state["kind"]:
            for d, src in enumerate((2, 3, 4, 5)):
                dma(mods[d][:], MOD[kind, src], reads=[f"D:MOD{kind}_{src}"], writes=[f"fmod{d}"])
            state["kind"] = kind
        dma(attb[b][:], ATT[:, :, t0:t0 + 128].rearrange("h c t -> c h t"),
            reads=[k for k in fw.lastw if k.startswith("D:ATT")], writes=[f"attb{b}"])
        dma(ssmb[b][:], SSMT[:, t0:t0 + 128].rearrange("(k p) t -> p k t", p=128),
            reads=[k for k in fw.lastw if k.startswith("D:SSMT")], writes=[f"ssmb{b}"], q="pool")
        dma(hyb[b][:], HYT[:, t0:t0 + 128].rearrange("(k p) t -> p k t", p=128),
            reads=[k for k in fw.lastw if k.startswith("D:HYT")], writes=[f"hyb{b}"], q="pool")
        dma(xt[b][:], XR[t0:t0 + 128, :], reads=[f"D:XR{ci}"], writes=[f"fxt{b}"])
        for half in range(2):
            ops_ = [(attb[b][:, h, :], wo_att[:, h, half * 512:(half + 1) * 512], f"attb{b}", "wo_att") for h in range(NH)]
            ops_ += [(ssmb[b][:, k, :], wo_rest[:, k, half * 512:(half + 1) * 512], f"ssmb{b}", "wo_rest") for k in range(3)]
            ops_ += [(hyb[b][:, k, :], wo_rest[:, 3 + k, half * 512:(half + 1) * 512], f"hyb{b}", "wo_rest") for k in range(2)]
            for i, (l_, r_, lk, rk) in enumerate(ops_):
                op("pe", lambda e: e.matmul(psm[half][:], lhsT=l_, rhs=r_, start=(i == 0), stop=(i == len(ops_) - 1)),
                   reads=[lk, rk], writes=[f"psm{half}"], pe_acc=(i > 0))
            op("dve", lambda e: e.tensor_tensor(out=xm[b][:, half * 512:(half + 1) * 512], in0=psm[half][:],
                                                in1=mods[0][:, half * 512:(half + 1) * 512], op=ALU.mult),
               reads=[f"psm{half}", "fmod0"], writes=[f"fxm{b}"])
        op("pool", lambda e: e.tensor_tensor(out=xm[b][:], in0=xm[b][:], in1=xt[b][:], op=ALU.add),
           reads=[f"fxm{b}", f"fxt{b}"], writes=[f"fxm{b}"])
        op("act", lambda e: e.activation(out=hb[:], in_=xm[b][:], func=AF.Square, accum_out=ss[:, ci:ci + 1]),
           reads=[f"fxm{b}"], writes=["fhb", f"fss{ci}"])
        _rsqrt(fw, rstd[:, ci:ci + 1], ss[:, ci:ci + 1], 1.0 / D, f"fss{ci}", f"frstd{ci}")
        op("dve", lambda e: e.scalar_tensor_tensor(out=xt[b][:], in0=xm[b][:], scalar=rstd[:, ci:ci + 1],
                                                   in1=mods[1][:], op0=ALU.mult, op1=ALU.mult),
           reads=[f"fxm{b}", f"frstd{ci}", "fmod1"], writes=[f"fxt{b}"])
        op("pool", lambda e: e.tensor_tensor(out=hb[:], in0=xt[b][:], in1=mods[2][:], op=ALU.add),
           reads=[f"fxt{b}", "fmod2"], writes=["fhb"])

    def mid(ci, b):
        for k in range(8):
            op("pe", lambda e: e.transpose(out=pst[:, k, :], in_=hb[:, k * 128:(k + 1) * 128], identity=ident_b[:]),
               reads=["fhb", "ident_b"], writes=["fpst"], pe_acc=(k > 0))
        op("act", lambda e: e.copy(out=h2T[:], in_=pst[:]), reads=["fpst"], writes=["h2T"])
        for j in range(32):
            s = j % 3
            for k in range(8):
                op("pe", lambda e: e.matmul(psf[s][:], lhsT=w1b[:, k, j * 128:(j + 1) * 128], rhs=h2T[:, k, :],
                                            start=(k == 0), stop=(k == 7)),
                   reads=["h2T", "w1b"], writes=[f"psf{s}"], pe_acc=(k > 0))
            op("act", lambda e: e.activation(out=rr[s][:], in_=psf[s][:], func=AF.Relu),
               reads=[f"psf{s}"], writes=[f"frr{s}"])
            op("dve" if j % 2 else "pool", lambda e: e.tensor_tensor(out=aT[:, j, :], in0=rr[s][:], in1=rr[s][:],
                                                                     op=ALU.mult),
               reads=[f"frr{s}"], writes=[f"aT{j}"])

    def back(ci, b):
        t0 = ci * 128
        for half in range(2):
            s = state["n"] % 2; state["n"] += 1
            for j in range(32):
                op("pe", lambda e: e.matmul(pso[s][:], lhsT=aT[:, j, :], rhs=w2b[:, j, half * 512:(half + 1) * 512],
                                            start=(j == 0), stop=(j == 31)),
                   reads=[f"aT{j}", "w2b"], writes=[f"fpso{s}"], pe_acc=(j > 0))
            op("dve", lambda e: e.tensor_tensor(out=ot[s][:], in0=pso[s][:],
                                                in1=mods[3][:, half * 512:(half + 1) * 512], op=ALU.mult),
               reads=[f"fpso{s}", "fmod3"], writes=[f"fot{s}"])
            op("pool", lambda e: e.tensor_tensor(out=ot[s][:], in0=ot[s][:], in1=xm[b][:, half * 512:(half + 1) * 512],
                                                 op=ALU.add), reads=[f"fot{s}", f"fxm{b}"], writes=[f"fot{s}"])
            if last:
                dma(OUT[t0 - CTX:t0 - CTX + 128, half * 512:(half + 1) * 512], ot[s][:], reads=[f"fot{s}"],
                    writes=[f"D:OUT{ci}_{half}"], q="pool")
            else:
                dma(XR[t0:t0 + 128, half * 512:(half + 1) * 512], ot[s][:], reads=[f"fot{s}"],
                    writes=[f"D:XR{ci}"], q="pool")

    front(chunks[0], 0)
    for i, ci in enumerate(chunks):
        b = i % 2
        mid(ci, b)
        nxt = chunks[i + 1] if i + 1 < len(chunks) else None
        same_kind = nxt is not None and ((nxt < 2) == (ci < 2))
        if nxt is not None and same_kind:
            front(nxt, 1 - b)
            back(ci, b)
        else:
            back(ci, b)
            if nxt is not None:
                front(nxt, 1 - b)
    fw.release(mk0)
```

```python
import math
import numpy as np
import ml_dtypes
import concourse.bass as bass
import concourse.mybir as mybir
from concourse.bass_utils import run_bass_kernel_spmd

F32 = mybir.dt.float32
BF16 = mybir.dt.bfloat16
AF = mybir.ActivationFunctionType
ALU = mybir.AluOpType
AX = mybir.AxisListType

D = 1024
SEQ = 4096
CTX = 256
T = SEQ + CTX
NCH = T // 128
DEPTH = 2
EPS = 1e-6
NH = 6
QK = 96
IN_COLS = 2220
O_CQ, O_CKV, O_KR, O_Z, O_XBC, O_DT, O_HY = 0, 256, 384, 416, 800, 1440, 1452
SSM_INNER = 384
HY_CH = 256
DFF = 4096


class FW:
    NDSEM = 48

    def __init__(self):
        self.nc = bass.Bass("TRN2", target_bir_lowering=False)
        nc = self.nc
        self.eng = {"pe": nc.tensor, "act": nc.scalar, "dve": nc.vector,
                    "pool": nc.gpsimd, "sp": nc.sync}
        self._ctx = []
        self.sems = {}
        for e in self.eng:
            self.sems[e] = self._enter(nc.semaphore("s_" + e))
        self.dq = {"sp": 36, "pool": 36, "act": 8}
        self.dsems = {}
        self.dcnt = {}
        self.dnext = {q: 0 for q in self.dq}
        for q, n in self.dq.items():
            for i in range(n):
                self.dsems[(q, i)] = self._enter(nc.semaphore(f"d_{q}{i}"))
                self.dcnt[(q, i)] = 0
        self.seq = {e: 0 for e in self.eng}
        self.waited = {e: {} for e in self.eng}
        self.lastw = {}
        self.readers = {}
        self.n_inst = 0
        self.n_wait = 0
        self._rr = 0
        self.psum_keys = set()

    def _enter(self, cm):
        v = cm.__enter__()
        self._ctx.append(cm)
        return v

    def mark(self):
        return len(self._ctx)

    def release(self, mark):
        self.barrier()
        while len(self._ctx) > mark:
            self._ctx.pop().__exit__(None, None, None)
        self.lastw = {k: v for k, v in self.lastw.items() if k.startswith("D:")}
        self.readers = {k: v for k, v in self.readers.items() if k.startswith("D:")}

    def close(self):
        while self._ctx:
            self._ctx.pop().__exit__(None, None, None)

    def sbuf(self, name, shape, dt=F32):
        self._uid = getattr(self, "_uid", 0) + 1
        return self._enter(self.nc.sbuf_tensor(f"sb{self._uid}_{name}", list(shape), dt))

    def psum(self, name, shape, dt=F32):
        self._uid = getattr(self, "_uid", 0) + 1
        full = 512 if dt == F32 else 1024
        t = self._enter(self.nc.psum_tensor(f"ps{self._uid}_{name}", [128, full], dt))
        self.psum_keys.add(name)
        shape = list(shape)
        n = 1
        for d in shape[1:]:
            n *= d
        assert n <= full, (name, shape)
        v = t[0:shape[0], 0:n]
        if len(shape) == 3:
            v = v.rearrange("p (a b) -> p a b", b=shape[2])
        return v

    def _sem(self, key):
        return self.sems[key] if isinstance(key, str) else self.dsems[key]

    def _deps(self, reads, writes):
        deps = {}

        def add(k, v):
            if deps.get(k, 0) < v:
                deps[k] = v
        for r in reads:
            d = self.lastw.get(r)
            if d is not None:
                add(*d)
        for wk in writes:
            d = self.lastw.get(wk)
            if d is not None:
                add(*d)
            for rk, rv in self.readers.get(wk, {}).items():
                add(rk, rv)
        return deps

    def _emit_waits(self, e, deps, skip_self=False):
        w = self.waited[e]
        for k, v in deps.items():
            if skip_self and k == e:
                continue
            if w.get(k, 0) >= v:
                continue
            self.eng[e].wait_ge(self._sem(k), v)
            w[k] = v
            self.n_wait += 1

    def _record(self, dep, reads, writes):
        k, v = dep
        for r in reads:
            self.readers.setdefault(r, {})[k] = v
        for wk in writes:
            self.lastw[wk] = dep
            self.readers[wk] = {}

    def op(self, e, fn, reads=(), writes=(), pe_acc=False):
        if e == "any":
            e = ("dve", "pool", "act")[self._rr % 3]
            self._rr += 1
        px = [r for r in reads if r in self.psum_keys and r not in writes]
        if px:
            reads = [r for r in reads if r not in px]
            writes = list(writes) + px
        deps = self._deps(reads, writes)
        self._emit_waits(e, deps, skip_self=(e == "pe" and pe_acc))
        ins = fn(self.eng[e])
        self.seq[e] += 1
        ins.then_inc(self.sems[e], 1)
        self._record((e, self.seq[e]), reads, writes)
        self.n_inst += 1
        return ins

    def dma(self, out, in_, reads=(), writes=(), q="sp", **kw):
        deps = self._deps(reads, writes)
        i = (q, self.dnext[q])
        self.dnext[q] = (self.dnext[q] + 1) % self.dq[q]
        if self.dcnt[i] > 0:
            deps[i] = max(deps.get(i, 0), 16 * self.dcnt[i])
        self._emit_waits(q, deps)
        ins = self.eng[q].dma_start(out=out, in_=in_, **kw)
        self.dcnt[i] += 1
        ins.then_inc(self.dsems[i], 16)
        self._record((i, 16 * self.dcnt[i]), reads, writes)
        self.n_inst += 1
        return ins

    def barrier(self):
        deps = {e: self.seq[e] for e in self.eng if self.seq[e] > 0}
        for i, c in self.dcnt.items():
            if c > 0:
                deps[i] = 16 * c
        for e in self.eng:
            self._emit_waits(e, deps)

    def finish(self, keys, e="sp"):
        self._emit_waits(e, self._deps(keys, ()))


def _consts():
    c = {}
    c["ident_f"] = np.eye(128, dtype=np.float32)
    c["ones_f"] = np.ones((128, 128), np.float32)
    cos = np.ones((96, T), np.float32); sin = np.zeros((96, T), np.float32)
    t = np.arange(SEQ)
    pos = [(t // 64).astype(np.float32), (t % 64).astype(np.float32)]
    inv = (10000.0 ** (-np.arange(8, dtype=np.float32) / 8)).astype(np.float32)
    RT = np.zeros((96, 96), np.float32)
    for a in range(2):
        ang = pos[a][None, :] * inv[:, None]
        for b in range(2):
            for f in range(8):
                r = 64 + 16 * a + 8 * b + f
                cos[r, CTX:] = np.cos(ang[f]); sin[r, CTX:] = np.sin(ang[f])
        for f in range(8):
            i1 = 64 + 16 * a + f; i2 = i1 + 8
            RT[i2, i1] = -1.0
            RT[i1, i2] = 1.0
    c["rope_cos"] = cos; c["rope_sin"] = sin; c["rope_RT"] = RT
    pk = np.zeros((32, 96), np.float32); pk[np.arange(32), 64 + np.arange(32)] = 1.0
    c["pk_sel"] = pk
    i = np.arange(128)
    triF = (i[:, None] <= i[None, :]).astype(np.float32)
    c["tri"] = np.stack([triF, triF.T.copy()])
    NEGV = -1.0e30
    negF = np.where(i[:, None] <= i[None, :], 0.0, NEGV).astype(np.float32)
    negB = np.where(i[:, None] >= i[None, :], 0.0, NEGV).astype(np.float32)
    c["negmask"] = np.stack([negF, negB])
    gm = np.zeros((3, 3, 128, 128), np.float32)
    for r in range(3):
        for r2 in range(3):
            ga = (r * 128 + i) // 192; gb = (r2 * 128 + i) // 192
            gm[r, r2] = (ga[:, None] == gb[None, :]).astype(np.float32)
    c["gmask"] = gm.reshape(9, 128, 128)
    c["jrev"] = np.ascontiguousarray(np.eye(128, dtype=np.float32)[::-1])
    deltas = np.abs(np.linspace(math.log(1e-2) / 1.5, math.log(1e-2) / 0.3, HY_CH, dtype=np.float32))
    c["hy_ndelta"] = np.ascontiguousarray((-deltas).reshape(2, 128).T.astype(np.float32))
    for L in (CTX, SEQ):
        t01 = np.linspace(0.0, 1.0, L, dtype=np.float32)
        w = (np.float32(2.0 * math.pi / L) * np.arange(L, dtype=np.float32))
        bands = np.linspace(1e-4, 15, 16, dtype=np.float32)
        ang = (bands[None, :] * w[:, None]).astype(np.float32)
        feats = np.concatenate([t01[:, None], np.cos(ang), -np.sin(ang)], axis=1).astype(np.float32)
        fT = feats.T
        c[f"hy_feats_{L}"] = np.ascontiguousarray(np.stack([fT, fT[:, ::-1]]))
        t01r = np.broadcast_to(t01[None, :], (128, L))
        c[f"hy_t01_{L}"] = np.ascontiguousarray(np.stack([t01r, t01r[:, ::-1]]))
    return c


CONST_SPECS = {"ident_f": ([128, 128], F32), "ones_f": ([128, 128], F32)}


class Prog:
    def __init__(self, debug=()):
        self.fw = FW()
        self.nc = self.fw.nc
        self.debug = set(debug)
        self.din = {}
        self.dscr = {}

    def inp(self, name, shape, dt=F32):
        self.din[name] = self.nc.dram_tensor(name, list(shape), dt, kind="ExternalInput").ap()
        return self.din[name]

    def scratch(self, name, shape, dt=F32, out=False):
        kind = "ExternalOutput" if (out or name in self.debug) else "Internal"
        self.dscr[name] = self.nc.dram_tensor(name, list(shape), dt, kind=kind).ap()
        return self.dscr[name]


DSTOP = [None]
ENABLE_SSD = [True]
ENABLE_HYENA = [True]
DFLAG = [0]
SKIP_C = [False]


def build(debug=(), stop_after=None):
    P = Prog(debug)
    fw, nc = P.fw, P.nc
    op, dma = fw.op, fw.dma

    xcat = P.inp("xcat", [T, D])
    cvec = P.inp("cvec", [2, D])
    w_mod = P.inp("w_mod", [DEPTH, D, 6 * D])
    b_mod = P.inp("b_mod", [DEPTH, 6 * D])
    g_norm_mix = P.inp("g_norm_mix", [DEPTH, D])
    g_norm_mlp = P.inp("g_norm_mlp", [DEPTH, D])
    w_in = P.inp("w_in", [DEPTH, D, IN_COLS])
    cst = {k: P.inp(k, s, d) for k, (s, d) in CONST_SPECS.items()}

    XR = P.scratch("XR", [T, D])
    MOD = P.scratch("MOD", [2, 6, 128, D])
    PT = P.scratch("PT", [2224, T])
    DTRAW = P.scratch("DTRAW", [T, 12])
    OUT = P.scratch("out", [SEQ, D], out=True)

    ident_f = fw.sbuf("ident_f", [128, 128])
    ident_b = fw.sbuf("ident_b", [128, 128], BF16)
    ones_f = fw.sbuf("ones_f", [128, 128])
    dma(ident_f[:], cst["ident_f"], writes=["ident_f"])
    dma(ones_f[:], cst["ones_f"], writes=["ones_f"])
    op("dve", lambda e: e.tensor_copy(out=ident_b[:], in_=ident_f[:]), reads=["ident_f"], writes=["ident_b"])

    for ci in range(0, NCH, 2):
        dma(XR[ci * 128:(ci + 2) * 128, :], xcat[ci * 128:(ci + 2) * 128, :],
            writes=[f"D:XR{ci}", f"D:XR{ci + 1}"], q="pool")

    for li in range(DEPTH):
        mk = fw.mark()
        cT = fw.sbuf("cT", [128, 2, 8])
        sT = fw.sbuf("sT", [128, 2, 8])
        srep = fw.sbuf("srep", [128, 2, 8, 128])
        modsb = [fw.sbuf(f"modsb{j}", [128, 6 * D]) for j in range(2)]
        brow = fw.sbuf("brow", [1, 6 * D])
        grow = fw.sbuf("grow", [1, 2 * D])
        grep = fw.sbuf("grep", [128, 2 * D])
        wst = [fw.sbuf(f"wst{s}", [128, 8, 512]) for s in range(2)]
        psA = [fw.psum(f"psA{s}", [128, 512]) for s in range(2)]

        dma(cT[:], cvec.rearrange("j (k p) -> p j k", p=128), writes=["cT"],
            allow_slow_non_contiguous=True)
        dma(brow[:], b_mod[li:li + 1, :], writes=["brow"])
        dma(grow[:, 0:D], g_norm_mix[li:li + 1, :], writes=["grow"])
        dma(grow[:, D:2 * D], g_norm_mlp[li:li + 1, :], writes=["grow"])
        op("act", lambda e: e.activation(out=sT[:], in_=cT[:], func=AF.Silu), reads=["cT"], writes=["sT"])
        for j in range(2):
            for k in range(8):
                op("dve", lambda e: e.tensor_copy(out=srep[:, j, k, :],
                                                  in_=sT[:, j, k:k + 1].to_broadcast([128, 128])),
                   reads=["sT"], writes=["srep"])
        for nb in range(4):
            s = nb % 2
            op("pe", lambda e: e.matmul(psA[s][:], lhsT=ones_f[0:1, :], rhs=grow[0:1, nb * 512:(nb + 1) * 512],
                                        start=True, stop=True),
               reads=["ones_f", "grow"], writes=[f"psA{s}"])
            op("dve", lambda e: e.tensor_copy(out=grep[:, nb * 512:(nb + 1) * 512], in_=psA[s][:]),
               reads=[f"psA{s}"], writes=["grep"])
        cnt = 0
        for nb in range(12):
            ws = nb % 2
            dma(wst[ws][:], w_mod[li, :, nb * 512:(nb + 1) * 512].rearrange("(k p) n -> p k n", p=128),
                writes=[f"wst{ws}"])
            for j in range(2):
                s = cnt % 2
                cnt += 1
                for k in range(8):
                    op("pe", lambda e: e.matmul(psA[s][:], lhsT=srep[:, j, k, :], rhs=wst[ws][:, k, :],
                                                start=(k == 0), stop=False),
                       reads=["srep", f"wst{ws}"], writes=[f"psA{s}"], pe_acc=(k > 0))
                op("pe", lambda e: e.matmul(psA[s][:], lhsT=ones_f[0:1, :], rhs=brow[0:1, nb * 512:(nb + 1) * 512],
                                            start=False, stop=True),
                   reads=["ones_f", "brow"], writes=[f"psA{s}"], pe_acc=True)
                op("act" if j == 0 else "dve",
                   (lambda e: e.copy(out=modsb[j][:, nb * 512:(nb + 1) * 512], in_=psA[s][:])) if j == 0 else
                   (lambda e: e.tensor_copy(out=modsb[j][:, nb * 512:(nb + 1) * 512], in_=psA[s][:])),
                   reads=[f"psA{s}"], writes=[f"modsb{j}"])
        for j in range(2):
            m = modsb[j]
            op("dve", lambda e: e.scalar_tensor_tensor(out=m[:, D:2 * D], in0=m[:, D:2 * D], scalar=1.0,
                                                       in1=grep[:, 0:D], op0=ALU.add, op1=ALU.mult),
               reads=[f"modsb{j}", "grep"], writes=[f"modsb{j}"])
            op("dve", lambda e: e.scalar_tensor_tensor(out=m[:, 4 * D:5 * D], in0=m[:, 4 * D:5 * D], scalar=1.0,
                                                       in1=grep[:, D:2 * D], op0=ALU.add, op1=ALU.mult),
               reads=[f"modsb{j}", "grep"], writes=[f"modsb{j}"])
            for dst, src in ((0, 1), (1, 0), (2, 2), (3, 4), (4, 3), (5, 5)):
                dma(MOD[j, dst], m[:, src * D:(src + 1) * D], reads=[f"modsb{j}"], writes=[f"D:MOD{j}_{dst}"],
                    q="pool")
        fw.release(mk)
        if stop_after == ("A", li):
            break

        mk = fw.mark()
        hT = fw.sbuf("hT", [128, 8, T], BF16)
        winb = fw.sbuf("winb", [128, 8, 2224], BF16)
        wstg = [fw.sbuf(f"wstg{s}", [128, 8, 555]) for s in range(2)]
        A1 = [fw.sbuf(f"A1_{j}", [128, D]) for j in range(2)]
        B1 = [fw.sbuf(f"B1_{j}", [128, D]) for j in range(2)]
        xt = [fw.sbuf(f"xt{s}", [128, D]) for s in range(2)]
        tmp = [fw.sbuf(f"tmp{s}", [128, D]) for s in range(2)]
        hb = [fw.sbuf(f"hb{s}", [128, D], BF16) for s in range(2)]
        junk = fw.sbuf("junk", [128, D], BF16)
        ss = fw.sbuf("ss", [128, NCH])
        rstd = fw.sbuf("rstd", [128, NCH])
        dtst = fw.sbuf("dtst", [128, NCH, 12])
        osb = [fw.sbuf(f"osb{s}", [128, 512]) for s in range(3)]
        pst = [fw.psum(f"pst{s}", [128, 8, 128], BF16) for s in range(2)]
        psd = [fw.psum(f"psd{s}", [128, 16]) for s in range(2)]
        pso = [fw.psum(f"pso{s}", [128, 512]) for s in range(3)]

        for j in range(2):
            dma(A1[j][:], MOD[j, 0], reads=[f"D:MOD{j}_0"], writes=[f"A1_{j}"])
            dma(B1[j][:], MOD[j, 1], reads=[f"D:MOD{j}_1"], writes=[f"B1_{j}"])
        for q4 in range(4):
            s = q4 % 2
            dma(wstg[s][:], w_in[li, :, q4 * 555:(q4 + 1) * 555].rearrange("(k p) n -> p k n", p=128),
                writes=[f"wstg{s}"], q="pool")
            op("any", lambda e: (e.copy if e is nc.scalar else e.tensor_copy)(
                out=winb[:, :, q4 * 555:(q4 + 1) * 555], in_=wstg[s][:]),
               reads=[f"wstg{s}"], writes=["winb"])

        for ci in range(NCH):
            s = ci % 2
            kind = 1 if ci < 2 else 0
            dma(xt[s][:], XR[ci * 128:(ci + 1) * 128, :], reads=[f"D:XR{ci}"], writes=[f"xt{s}"])
            op("act", lambda e: e.activation(out=junk[:], in_=xt[s][:], func=AF.Square,
                                             accum_out=ss[:, ci:ci + 1]),
               reads=[f"xt{s}"], writes=["junk", f"ss{ci}"])
            op("dve", lambda e: e.tensor_scalar(out=rstd[:, ci:ci + 1], in0=ss[:, ci:ci + 1],
                                                scalar1=1.0 / D, scalar2=EPS, op0=ALU.mult, op1=ALU.add),
               reads=[f"ss{ci}"], writes=[f"rstd{ci}"])
            op("act", lambda e: e.sqrt(out=rstd[:, ci:ci + 1], in_=rstd[:, ci:ci + 1]),
               reads=[f"rstd{ci}"], writes=[f"rstd{ci}"])
            op("dve", lambda e: e.reciprocal(out=rstd[:, ci:ci + 1], in_=rstd[:, ci:ci + 1]),
               reads=[f"rstd{ci}"], writes=[f"rstd{ci}"])
            op("dve", lambda e: e.scalar_tensor_tensor(out=tmp[s][:], in0=xt[s][:], scalar=rstd[:, ci:ci + 1],
                                                       in1=A1[kind][:], op0=ALU.mult, op1=ALU.mult),
               reads=[f"xt{s}", f"rstd{ci}", f"A1_{kind}"], writes=[f"tmp{s}"])
            op("dve", lambda e: e.tensor_tensor(out=hb[s][:], in0=tmp[s][:], in1=B1[kind][:], op=ALU.add),
               reads=[f"tmp{s}", f"B1_{kind}"], writes=[f"hb{s}"])
            for k in range(8):
                op("pe", lambda e: e.transpose(out=pst[s][:, k, :], in_=hb[s][:, k * 128:(k + 1) * 128],
                                               identity=ident_b[:]),
                   reads=[f"hb{s}", "ident_b"], writes=[f"pst{s}"], pe_acc=(k > 0))
            op("act", lambda e: e.copy(out=hT[:, :, ci * 128:(ci + 1) * 128], in_=pst[s][:]),
               reads=[f"pst{s}"], writes=[f"hT{ci}"])
            for k in range(8):
                op("pe", lambda e: e.matmul(psd[s][:, 0:12], lhsT=hT[:, k, ci * 128:(ci + 1) * 128],
                                            rhs=winb[:, k, O_DT:O_DT + 12], start=(k == 0), stop=(k == 7)),
                   reads=[f"hT{ci}", "winb"], writes=[f"psd{s}"], pe_acc=(k > 0))
            op("dve", lambda e: e.tensor_copy(out=dtst[:, ci, :], in_=psd[s][:, 0:12]),
               reads=[f"psd{s}"], writes=["dtst"])
        dma(DTRAW.rearrange("(c p) n -> p c n", p=128), dtst[:], reads=["dtst"], writes=["D:DTRAW"], q="pool")

        col_tiles = [(0, 128), (128, 128), (256, 128), (384, 32)]
        col_tiles += [(O_Z + 128 * i, 128) for i in range(3)]
        col_tiles += [(O_XBC + 128 * i, 128) for i in range(5)]
        col_tiles += [(O_HY + 128 * i, 128) for i in range(6)]
        tblocks = [(t0, min(512, T - t0)) for t0 in range(0, T, 512)]
        cnt = 0
        for (c0, cw) in col_tiles:
            for (t0, tn) in tblocks:
                s = cnt % 3
                cnt += 1
                rk = [f"hT{ci}" for ci in range(t0 // 128, (t0 + tn) // 128)] + ["winb"]
                for k in range(8):
                    op("pe", lambda e: e.matmul(pso[s][0:cw, 0:tn], lhsT=winb[:, k, c0:c0 + cw],
                                                rhs=hT[:, k, t0:t0 + tn], start=(k == 0), stop=(k == 7)),
                       reads=rk, writes=[f"pso{s}"], pe_acc=(k > 0))
                if cnt % 2:
                    op("act", lambda e: e.copy(out=osb[s][0:cw, 0:tn], in_=pso[s][0:cw, 0:tn]),
                       reads=[f"pso{s}"], writes=[f"osb{s}"])
                else:
                    op("dve", lambda e: e.tensor_copy(out=osb[s][0:cw, 0:tn], in_=pso[s][0:cw, 0:tn]),
                       reads=[f"pso{s}"], writes=[f"osb{s}"])
                dma(PT[c0:c0 + cw, t0:t0 + tn], osb[s][0:cw, 0:tn], reads=[f"osb{s}"],
                    writes=[f"D:PT{c0}_{t0}"], q="pool" if cnt % 2 else "sp")
        fw.release(mk)
        if stop_after == ("B", li):
            break
        G = dict(P=P, fw=fw, nc=nc, li=li, XR=XR, MOD=MOD, PT=PT, DTRAW=DTRAW, OUT=OUT,
                 ident_f=ident_f, ident_b=ident_b, ones_f=ones_f, cst=cst, dstop=DSTOP[0])
        if not SKIP_C[0]:
            phase_C(G)
        if stop_after == ("C", li):
            break
        if ENABLE_SSD[0]:
            phase_D(G)
        else:
            _zero_fill(G, "SSMT", SSM_INNER)
        if stop_after == ("D", li):
            break
        if ENABLE_HYENA[0]:
            phase_E(G)
        else:
            _zero_fill(G, "HYT", HY_CH)
        if stop_after == ("E", li):
            break
        phase_F(G)
        if stop_after == ("F", li):
            break

    fw.barrier()
    fw.close()
    return P


def make_inputs(inputs, core):
    b = core % 4
    m = {}
    m["xcat"] = np.ascontiguousarray(np.concatenate([inputs["ctx"][b], inputs["x"][b]], axis=0))
    m["cvec"] = np.ascontiguousarray(np.stack([inputs["c"][b], inputs["c_ctx"]], axis=0))
    for k in ("w_mod", "b_mod", "g_norm_mix", "g_norm_mlp", "w_in", "w_out", "g_cq", "g_ckv", "w_uq", "w_ukv",
              "g_qhead", "g_khead", "w_conv_ssm", "b_conv_ssm", "a_log", "dt_bias", "d_skip_ssm", "g_ssm_out",
              "w_conv_hy", "b_conv_hy", "w_f1", "b_f1", "freq_f1", "w_f2", "b_f2", "freq_f2", "w_f3",
              "d_skip_hy", "w_ff1", "w_ff2"):
        m[k] = np.ascontiguousarray(inputs[k])
    m["a_log"] = m["a_log"].reshape(DEPTH, 12); m["dt_bias"] = m["dt_bias"].reshape(DEPTH, 12)
    m["hy_vecs"] = np.ascontiguousarray(np.stack([inputs["b_f1"], inputs["freq_f1"], inputs["b_f2"], inputs["freq_f2"]], axis=1))
    m["d_skip_col"] = np.ascontiguousarray(np.repeat(inputs["d_skip_ssm"], 64, axis=1))
    m.update(_consts())
    return m


def kernel(**inputs):
    inputs = {k: np.asarray(v) for k, v in inputs.items()}
    P = build()
    in_maps = [make_inputs(inputs, c) for c in range(8)]
    in_maps = [{k: v for k, v in m.items() if k in P.din} for m in in_maps]
    res = run_bass_kernel_spmd(P.nc, in_maps, core_ids=list(range(8)))
    return np.stack([res.results[b]["out"] for b in range(4)], axis=0).astype(np.float32)


def _inp(P, name, shape, dt=F32):
    if name in P.din:
        return P.din[name]
    return P.inp(name, shape, dt)


def _scr(P, name, shape, dt=F32):
    if name in P.dscr:
        return P.dscr[name]
    return P.scratch(name, shape, dt)


def _interleave(gens):
    gens = list(gens)
    while gens:
        for g in list(gens):
            try:
                next(g)
            except StopIteration:
                gens.remove(g)


def _rsqrt(fw, dst, src, scale, rk, wk, reads_extra=()):
    fw.op("dve", lambda e: e.tensor_scalar(out=dst, in0=src, scalar1=scale, scalar2=EPS,
                                           op0=ALU.mult, op1=ALU.add), reads=[rk] + list(reads_extra), writes=[wk])
    fw.op("act", lambda e: e.sqrt(out=dst, in_=dst), reads=[wk], writes=[wk])
    fw.op("dve", lambda e: e.reciprocal(out=dst, in_=dst), reads=[wk], writes=[wk])


def _load_cast(fw, dst_bf, src_ap, stg, key_dst, key_stg, q="sp"):
    fw.dma(stg, src_ap, writes=[key_stg], q=q)
    fw.op("any", lambda e: (e.copy if e is fw.nc.scalar else e.tensor_copy)(out=dst_bf, in_=stg),
          reads=[key_stg], writes=[key_dst])


def phase_C(G):
    P, fw, nc, li = G["P"], G["fw"], G["nc"], G["li"]
    op, dma = fw.op, fw.dma
    PT = G["PT"]
    ones_f = G["ones_f"]
    last = li == DEPTH - 1
    g_cq = _inp(P, "g_cq", [DEPTH, 256]); g_ckv = _inp(P, "g_ckv", [DEPTH, 128])
    w_uq = _inp(P, "w_uq", [DEPTH, 256, 576]); w_ukv = _inp(P, "w_ukv", [DEPTH, 128, 768])
    g_qh = _inp(P, "g_qhead", [DEPTH, 96]); g_kh = _inp(P, "g_khead", [DEPTH, 96])
    COS = _inp(P, "rope_cos", [96, T]); SIN = _inp(P, "rope_sin", [96, T])
    RTd = _inp(P, "rope_RT", [96, 96]); PKd = _inp(P, "pk_sel", [32, 96])
    ATT = _scr(P, "ATT", [NH, 64, T], BF16)
    scale = 1.0 / math.sqrt(QK)

    mk0 = fw.mark()
    qT = fw.sbuf("qT", [96, NH, T], BF16)
    kT = fw.sbuf("kT", [96, NH, T], BF16)
    V1 = fw.sbuf("V1", [128, NCH, NH, 65], BF16)
    op("pool", lambda e: e.memset(V1[:], 1.0), writes=["V1"])

    mk = fw.mark()
    wuq_b = fw.sbuf("wuq_b", [128, 2, 576], BF16)
    wk_b = fw.sbuf("wk_b", [128, NH, 96], BF16)
    wv_b = fw.sbuf("wv_b", [128, NH, 64], BF16)
    pk_b = fw.sbuf("pk_b", [32, 96], BF16)
    stg = fw.sbuf("stg", [128, 2, 768])
    RT = fw.sbuf("RT", [96, 96])
    gcq = fw.sbuf("gcq", [128, 2]); gckv = fw.sbuf("gckv", [128, 1])
    gq = fw.sbuf("gq", [96, 1]); gk = fw.sbuf("gk", [96, 1])
    dma(stg[:, :, 0:576], w_uq[li].rearrange("(k p) n -> p k n", p=128), writes=["stg"])
    op("dve", lambda e: e.tensor_copy(out=wuq_b[:], in_=stg[:, :, 0:576]), reads=["stg"], writes=["wuq_b"])
    dma(stg[:, 0, :], w_ukv[li], reads=[], writes=["stg"])
    op("pool", lambda e: e.memset(wk_b[:], 0.0), writes=["wk_b"])
    op("dve", lambda e: e.tensor_copy(out=wk_b[:, :, 0:64],
                                      in_=stg[:, 0, :].rearrange("p (h c) -> p h c", c=128)[:, :, 0:64]),
       reads=["stg"], writes=["wk_b"])
    op("dve", lambda e: e.tensor_copy(out=wv_b[:],
                                      in_=stg[:, 0, :].rearrange("p (h c) -> p h c", c=128)[:, :, 64:128]),
       reads=["stg"], writes=["wv_b"])
    dma(stg[0:32, 1, 0:96], PKd, writes=["stg1"])
    op("dve", lambda e: e.tensor_copy(out=pk_b[:], in_=stg[0:32, 1, 0:96]), reads=["stg1"], writes=["pk_b"])
    dma(RT[:], RTd, writes=["RT"])
    dma(gcq[:], g_cq[li].rearrange("(k p) -> p k", p=128), writes=["gcq"], allow_slow_non_contiguous=True)
    dma(gckv[:], g_ckv[li].rearrange("(p o) -> p o", o=1), writes=["gckv"], allow_slow_non_contiguous=True)
    dma(gq[:], g_qh[li].rearrange("(p o) -> p o", o=1), writes=["gq"], allow_slow_non_contiguous=True)
    dma(gk[:], g_kh[li].rearrange("(p o) -> p o", o=1), writes=["gk"], allow_slow_non_contiguous=True)

    cqb = [fw.sbuf(f"cqb{s}", [128, 2, 512]) for s in range(2)]
    ckb = [fw.sbuf(f"ckb{s}", [128, 512]) for s in range(2)]
    krb = [fw.sbuf(f"krb{s}", [32, 512]) for s in range(2)]
    krb_b = fw.sbuf("krb_b", [32, 512], BF16)
    sq = fw.sbuf("sq", [128, 2, 512])
    rs = fw.sbuf("rs", [128, 512])
    cqn = fw.sbuf("cqn", [128, 2, 512], BF16)
    ckn = fw.sbuf("ckn", [128, 512], BF16)
    cosb = [fw.sbuf(f"cosb{s}", [96, 512]) for s in range(2)]
    sinb = [fw.sbuf(f"sinb{s}", [96, 512]) for s in range(2)]
    hsq = [fw.sbuf(f"hsq{s}", [96, 512]) for s in range(2)]
    hrs = [fw.sbuf(f"hrs{s}", [96, 512]) for s in range(2)]
    hn = [fw.sbuf(f"hn{s}", [96, 512]) for s in range(2)]
    ht1 = [fw.sbuf(f"ht1{s}", [96, 512]) for s in range(2)]
    ht2 = [fw.sbuf(f"ht2{s}", [96, 512]) for s in range(2)]
    ps1 = fw.psum("ps1", [128, 512])
    psh = [fw.psum(f"psh{s}", [96, 512]) for s in range(2)]
    ps2 = [fw.psum(f"ps2{s}", [96, 512]) for s in range(2)]
    psr = [fw.psum(f"psr{s}", [96, 512]) for s in range(2)]
    psv = fw.psum("psv", [128, 384])

    def head_chain(kind, h, s, t0, tn, bs, bi):
        pk = f"psh{s}"
        if kind == "q":
            for k in range(2):
                op("pe", lambda e: e.matmul(psh[s][:, 0:tn], lhsT=wuq_b[:, k, h * 96:(h + 1) * 96],
                                            rhs=cqn[:, k, 0:tn], start=(k == 0), stop=(k == 1)),
                   reads=["cqn", "wuq_b"], writes=[pk], pe_acc=(k > 0))
            gcol, gkey, dst, dkey = gq, "gq", qT[:, h, t0:t0 + tn], f"qT{h}_{bi}"
        else:
            op("pe", lambda e: e.matmul(psh[s][:, 0:tn], lhsT=wk_b[:, h, :], rhs=ckn[:, 0:tn], start=True, stop=False),
               reads=["ckn", "wk_b"], writes=[pk])
            op("pe", lambda e: e.matmul(psh[s][:, 0:tn], lhsT=pk_b[:], rhs=krb_b[:, 0:tn], start=False, stop=True),
               reads=["krb_b", "pk_b"], writes=[pk], pe_acc=True)
            gcol, gkey, dst, dkey = gk, "gk", kT[:, h, t0:t0 + tn], f"kT{h}_{bi}"
        yield
        op("act", lambda e: e.activation(out=hsq[s][:, 0:tn], in_=psh[s][:, 0:tn], func=AF.Square),
           reads=[pk], writes=[f"hsq{s}"])
        yield
        op("pe", lambda e: e.matmul(ps2[s][:, 0:tn], lhsT=ones_f[0:96, 0:96], rhs=hsq[s][:, 0:tn],
                                    start=True, stop=True), reads=[f"hsq{s}", "ones_f"], writes=[f"ps2{s}"])
        yield
        op("dve", lambda e: e.tensor_scalar(out=hrs[s][:, 0:tn], in0=ps2[s][:, 0:tn], scalar1=1.0 / QK, scalar2=EPS,
                                            op0=ALU.mult, op1=ALU.add), reads=[f"ps2{s}"], writes=[f"hrs{s}"])
        yield
        op("act", lambda e: e.sqrt(out=hrs[s][:, 0:tn], in_=hrs[s][:, 0:tn]), reads=[f"hrs{s}"], writes=[f"hrs{s}"])
        yield
        op("dve", lambda e: e.reciprocal(out=hrs[s][:, 0:tn], in_=hrs[s][:, 0:tn]), reads=[f"hrs{s}"], writes=[f"hrs{s}"])
        yield
        op("dve", lambda e: e.scalar_tensor_tensor(out=hn[s][:, 0:tn], in0=psh[s][:, 0:tn], scalar=gcol[:, 0:1],
                                                   in1=hrs[s][:, 0:tn], op0=ALU.mult, op1=ALU.mult),
           reads=[pk, gkey, f"hrs{s}"], writes=[f"hn{s}"])
        yield
        op("pe", lambda e: e.matmul(psr[s][:, 0:tn], lhsT=RT[:], rhs=hn[s][:, 0:tn], start=True, stop=True),
           reads=[f"hn{s}", "RT"], writes=[f"psr{s}"])
        op("dve", lambda e: e.tensor_tensor(out=ht1[s][:, 0:tn], in0=hn[s][:, 0:tn], in1=cosb[bs][:, 0:tn],
                                             op=ALU.mult), reads=[f"hn{s}", f"cosb{bs}"], writes=[f"ht1{s}"])
        yield
        op("dve", lambda e: e.tensor_tensor(out=ht2[s][:, 0:tn], in0=psr[s][:, 0:tn], in1=sinb[bs][:, 0:tn],
                                            op=ALU.mult), reads=[f"psr{s}", f"sinb{bs}"], writes=[f"ht2{s}"])
        yield
        op("dve", lambda e: e.tensor_tensor(out=dst, in0=ht1[s][:, 0:tn], in1=ht2[s][:, 0:tn], op=ALU.add),
           reads=[f"ht1{s}", f"ht2{s}"], writes=[dkey])
        yield

    tblocks = [(t0, min(512, T - t0)) for t0 in range(0, T, 512)]
    for bi, (t0, tn) in enumerate(tblocks):
        bs = bi % 2
        dma(cqb[bs][:, :, 0:tn], PT[0:256, t0:t0 + tn].rearrange("(k p) t -> p k t", p=128),
            reads=[f"D:PT{c}_{t0}" for c in (0, 128)], writes=[f"cqb{bs}"])
        dma(ckb[bs][:, 0:tn], PT[256:384, t0:t0 + tn], reads=[f"D:PT256_{t0}"], writes=[f"ckb{bs}"])
        dma(krb[bs][:, 0:tn], PT[384:416, t0:t0 + tn], reads=[f"D:PT384_{t0}"], writes=[f"krb{bs}"])
        dma(cosb[bs][:, 0:tn], COS[:, t0:t0 + tn], writes=[f"cosb{bs}"], q="pool")
        dma(sinb[bs][:, 0:tn], SIN[:, t0:t0 + tn], writes=[f"sinb{bs}"], q="pool")
        op("act", lambda e: e.activation(out=sq[:, :, 0:tn], in_=cqb[bs][:, :, 0:tn], func=AF.Square),
           reads=[f"cqb{bs}"], writes=["sq"])
        for k in range(2):
            op("pe", lambda e: e.matmul(ps1[:, 0:tn], lhsT=ones_f[:], rhs=sq[:, k, 0:tn], start=(k == 0), stop=(k == 1)),
               reads=["sq", "ones_f"], writes=["ps1"], pe_acc=(k > 0))
        _rsqrt(fw, rs[:, 0:tn], ps1[:, 0:tn], 1.0 / 256, "ps1", "rs")
        for k in range(2):
            op("dve", lambda e: e.scalar_tensor_tensor(out=cqn[:, k, 0:tn], in0=cqb[bs][:, k, 0:tn],
                                                       scalar=gcq[:, k:k + 1], in1=rs[:, 0:tn],
                                                       op0=ALU.mult, op1=ALU.mult),
               reads=[f"cqb{bs}", "gcq", "rs"], writes=["cqn"])
        op("act", lambda e: e.activation(out=sq[:, 0, 0:tn], in_=ckb[bs][:, 0:tn], func=AF.Square),
           reads=[f"ckb{bs}"], writes=["sq"])
        op("pe", lambda e: e.matmul(ps1[:, 0:tn], lhsT=ones_f[:], rhs=sq[:, 0, 0:tn], start=True, stop=True),
           reads=["sq", "ones_f"], writes=["ps1"])
        _rsqrt(fw, rs[:, 0:tn], ps1[:, 0:tn], 1.0 / 128, "ps1", "rs")
        op("dve", lambda e: e.scalar_tensor_tensor(out=ckn[:, 0:tn], in0=ckb[bs][:, 0:tn], scalar=gckv[:, 0:1],
                                                   in1=rs[:, 0:tn], op0=ALU.mult, op1=ALU.mult),
           reads=[f"ckb{bs}", "gckv", "rs"], writes=["ckn"])
        op("act", lambda e: e.copy(out=krb_b[:, 0:tn], in_=krb[bs][:, 0:tn]), reads=[f"krb{bs}"], writes=["krb_b"])
        for kind in ("q", "k"):
            for h in range(0, NH, 2):
                _interleave([head_chain(kind, h, 0, t0, tn, bs, bi), head_chain(kind, h + 1, 1, t0, tn, bs, bi)])
        for c in range(tn // 128):
            ci = t0 // 128 + c
            op("pe", lambda e: e.matmul(psv[:], lhsT=ckn[:, c * 128:(c + 1) * 128],
                                        rhs=wv_b[:].rearrange("p h c -> p (h c)"), start=True, stop=True),
               reads=["ckn", "wv_b"], writes=["psv"])
            op("act", lambda e: e.copy(out=V1[:, ci, :, 0:64], in_=psv[:].rearrange("p (h c) -> p h c", c=64)),
               reads=["psv"], writes=["V1", f"V1_{ci}"])
    fw.release(mk)

    mk = fw.mark()
    NS = 4
    psS = [fw.psum(f"psS{s}", [128, 512]) for s in range(NS)]
    acc = [fw.psum(f"acc{s}", [65, 512]) for s in range(2)]
    psb = fw.psum("psb", [64, 512])
    pT = [fw.sbuf(f"pT{s}", [128, 512], BF16) for s in range(NS)]
    osb = [fw.sbuf(f"aosb{s}", [65, 512]) for s in range(2)]
    rden = [fw.sbuf(f"rden{s}", [65, 512]) for s in range(2)]
    att = [fw.sbuf(f"att{s}", [64, 512], BF16) for s in range(2)]

    groups = []
    if not last:
        groups.append((0, 256, [0, 1]))
    for g in range(4):
        groups.append((CTX + g * 1024, 1024, list(range(NCH))))
    it = 0
    oc = 0
    for (q0, qn, tks) in groups:
        halves = [(q0 + o, min(512, qn - o)) for o in range(0, qn, 512)]
        for h in range(NH):
            items = [(ti, tkc, hi, qs, hn_) for ti, tkc in enumerate(tks) for hi, (qs, hn_) in enumerate(halves)]
            LA = 2
            for idx in range(len(items) + LA):
                if idx < len(items):
                    ti, tkc, hi, qs, hn_ = items[idx]
                    s = (it + idx) % NS
                    op("pe", lambda e: e.matmul(psS[s][:, 0:hn_], lhsT=kT[:, h, tkc * 128:(tkc + 1) * 128],
                                                rhs=qT[:, h, qs:qs + hn_], start=True, stop=True),
                       reads=[], writes=[f"psS{s}"])
                if idx >= LA:
                    ti, tkc, hi, qs, hn_ = items[idx - LA]
                    s = (it + idx - LA) % NS
                    op("act", lambda e: e.activation(out=pT[s][:, 0:hn_], in_=psS[s][:, 0:hn_], func=AF.Exp,
                                                     scale=scale), reads=[f"psS{s}"], writes=[f"pT{s}"])
                    op("pe", lambda e: e.matmul(acc[hi][:, 0:hn_], lhsT=V1[:, tkc, h, :], rhs=pT[s][:, 0:hn_],
                                                start=(ti == 0), stop=(ti == len(tks) - 1)),
                       reads=[f"pT{s}"], writes=[f"acc{hi}"], pe_acc=(ti > 0))
            it += len(items)
            for hi, (qs, hn_) in enumerate(halves):
                o = oc % 2
                oc += 1
                op("dve", lambda e: e.tensor_copy(out=osb[o][:, 0:hn_], in_=acc[hi][:, 0:hn_]),
                   reads=[f"acc{hi}"], writes=[f"aosb{o}"])
                op("dve", lambda e: e.reciprocal(out=rden[o][64:65, 0:hn_], in_=osb[o][64:65, 0:hn_]),
                   reads=[f"aosb{o}"], writes=[f"rden{o}"])
                op("pe", lambda e: e.matmul(psb[:, 0:hn_], lhsT=ones_f[64:65, 0:64], rhs=rden[o][64:65, 0:hn_],
                                            start=True, stop=True), reads=[f"rden{o}", "ones_f"], writes=["psb"])
                op("dve", lambda e: e.tensor_tensor(out=att[o][:, 0:hn_], in0=osb[o][0:64, 0:hn_],
                                                    in1=psb[:, 0:hn_], op=ALU.mult),
                   reads=[f"aosb{o}", "psb"], writes=[f"att{o}"])
                dma(ATT[h, :, qs:qs + hn_], att[o][:, 0:hn_], reads=[f"att{o}"], writes=[f"D:ATT{h}_{qs}"],
                    q="pool")
    fw.release(mk)
    fw.release(mk0)


def _zero_fill(G, name, rows):
    P, fw = G["P"], G["fw"]
    dst = _scr(P, name, [rows, T], BF16)
    mk = fw.mark()
    zt = fw.sbuf("zfill", [128, 512], BF16)
    fw.op("pool", lambda e: e.memset(zt[:], 0.0), writes=["zfill"])
    for r in range(rows // 128):
        for t0 in range(0, T, 512):
            tn = min(512, T - t0)
            fw.dma(dst[r * 128:(r + 1) * 128, t0:t0 + tn], zt[:, 0:tn], reads=["zfill"],
                   writes=[f"D:{name}{r}_{t0}"], q="pool")
    fw.release(mk)


def phase_D(G):
    P, fw, nc, li = G["P"], G["fw"], G["nc"], G["li"]
    op, dma = fw.op, fw.dma
    PT, DTRAW = G["PT"], G["DTRAW"]
    ones_f, ident_f, ident_b = G["ones_f"], G["ident_f"], G["ident_b"]
    w_conv = _inp(P, "w_conv_ssm", [DEPTH, 3, 640]); b_conv = _inp(P, "b_conv_ssm", [DEPTH, 640])
    a_log = _inp(P, "a_log", [DEPTH, 12]); dt_bias = _inp(P, "dt_bias", [DEPTH, 12])
    dcol_d = _inp(P, "d_skip_col", [DEPTH, 384]); g_so = _inp(P, "g_ssm_out", [DEPTH, 384])
    TRI = _inp(P, "tri", [2, 128, 128]); NEG = _inp(P, "negmask", [2, 128, 128])
    GMd = _inp(P, "gmask", [9, 128, 128])
    YF = _scr(P, "YF", [2, 384, T])
    SSMT = _scr(P, "SSMT", [SSM_INNER, T], BF16)
    W = T + 4

    mk0 = fw.mark()
    xsT = fw.sbuf("xsT", [128, 3, T])
    BC = [fw.sbuf(f"BC{i}", [64, T], BF16) for i in range(4)]
    xs_tok = fw.sbuf("xs_tok", [128, NCH, 384], BF16)
    B_tok = fw.sbuf("B_tok", [128, NCH, 128], BF16)
    wc = fw.sbuf("wc", [128, 5, 3]); bc = fw.sbuf("bc", [128, 5])
    wc2 = fw.sbuf("wc2", [64, 4, 3]); bc2 = fw.sbuf("bc2", [64, 4])
    for k in range(3):
        dma(wc[:, :, k], w_conv[li, k].rearrange("(r p) -> p r", p=128), writes=["wc"], allow_slow_non_contiguous=True)
        dma(wc2[:, :, k], w_conv[li, k, 384:640].rearrange("(r p) -> p r", p=64), writes=["wc2"],
            allow_slow_non_contiguous=True)
    dma(bc[:], b_conv[li].rearrange("(r p) -> p r", p=128), writes=["bc"], allow_slow_non_contiguous=True)
    dma(bc2[:], b_conv[li, 384:640].rearrange("(r p) -> p r", p=64), writes=["bc2"], allow_slow_non_contiguous=True)

    mk = fw.mark()
    xb = fw.sbuf("xb", [128, W]); acc = fw.sbuf("cacc", [128, W])
    op("pool", lambda e: e.memset(xb[:], 0.0), writes=["xb"])
    jobs = [(128, O_XBC + 128 * r, wc[:, r, :], bc[:, r:r + 1], ("xs", r)) for r in range(3)]
    jobs += [(64, O_XBC + 384 + 64 * i, wc2[:, i, :], bc2[:, i:i + 1], ("bc", i)) for i in range(4)]
    for (np_, r0, wv, bv, dst) in jobs:
        rk = [k for k in fw.lastw if k.startswith("D:PT") and r0 - 127 <= int(k[4:].split("_")[0]) <= r0 + np_ - 1]
        dma(xb[0:np_, 1:1 + CTX], PT[r0:r0 + np_, 0:CTX], reads=rk, writes=["xb"])
        dma(xb[0:np_, 3 + CTX:3 + T], PT[r0:r0 + np_, CTX:T], reads=rk, writes=["xb"], q="pool")
        op("dve", lambda e: e.tensor_scalar(out=acc[0:np_, 0:W - 2], in0=xb[0:np_, 0:W - 2], scalar1=wv[:, 0:1],
                                            scalar2=None, op0=ALU.mult), reads=["xb", "wc", "wc2"], writes=["cacc"])
        for k in (1, 2):
            op("dve", lambda e: e.scalar_tensor_tensor(out=acc[0:np_, 0:W - 2], in0=xb[0:np_, k:W - 2 + k],
                                                       scalar=wv[:, k:k + 1], in1=acc[0:np_, 0:W - 2],
                                                       op0=ALU.mult, op1=ALU.add),
               reads=["xb", "cacc", "wc", "wc2"], writes=["cacc"])
        for (a0, a1, d0) in ((0, CTX, 0), (2 + CTX, 2 + T, CTX)):
            if dst[0] == "xs":
                o_ = xsT[:, dst[1], d0:d0 + (a1 - a0)]
            else:
                o_ = BC[dst[1]][:, d0:d0 + (a1 - a0)]
            op("act", lambda e: e.activation(out=o_, in_=acc[0:np_, a0:a1], func=AF.Silu, bias=bv),
               reads=["cacc", "bc", "bc2"], writes=["xsT" if dst[0] == "xs" else f"BC{dst[1]}"])
    fw.release(mk)

    if G.get("dstop") == 1:
        fw.release(mk0); return
    mk = fw.mark()
    ptx_ = [fw.psum(f"ptx{s}", [128, 512]) for s in range(2)]
    ptx = [t[:, 0:384].rearrange("p (r c) -> p r c", c=128) for t in ptx_]
    ptb_ = [fw.psum(f"ptb{s}", [128, 1024], BF16) for s in range(2)]
    ptb = [t[:, 0:128].rearrange("p (g c) -> p g c", c=64) for t in ptb_]
    for ci in range(NCH):
        s = ci % 2
        for r in range(3):
            op("pe", lambda e: e.transpose(out=ptx[s][:, r, :], in_=xsT[:, r, ci * 128:(ci + 1) * 128],
                                           identity=ident_f[:]), reads=["xsT", "ident_f"], writes=[f"ptx{s}"],
               pe_acc=(r > 0))
        op("act", lambda e: e.copy(out=xs_tok[:, ci, :], in_=ptx_[s][:, 0:384]),
           reads=[f"ptx{s}"], writes=["xs_tok"])
        for g in range(2):
            op("pe", lambda e: e.transpose(out=ptb[s][:, g, :], in_=BC[g][:, ci * 128:(ci + 1) * 128],
                                           identity=ident_b[0:64, 0:64]), reads=[f"BC{g}", "ident_b"],
               writes=[f"ptb{s}"], pe_acc=(g > 0))
        op("dve", lambda e: e.tensor_copy(out=B_tok[:, ci, :], in_=ptb_[s][:, 0:128]),
           reads=[f"ptb{s}"], writes=["B_tok"])
    fw.release(mk)

    tri = fw.sbuf("tri", [128, 2, 128]); neg = fw.sbuf("neg", [128, 2, 128])
    dma(tri[:], TRI.rearrange("d p i -> p d i"), writes=["tri"])
    dma(neg[:], NEG.rearrange("d p i -> p d i"), writes=["neg"])
    rows = fw.sbuf("rows", [1, 24])
    dma(rows[:, 0:12], a_log[li:li + 1, :], writes=["rows"])
    dma(rows[:, 12:24], dt_bias[li:li + 1, :], writes=["rows"])
    arep = fw.sbuf("arep", [128, 24])
    dtr = fw.sbuf("dtr", [128, NCH, 12]); dt = fw.sbuf("dt", [128, NCH, 12]); dta = fw.sbuf("dta", [128, NCH, 12])
    cum = fw.sbuf("cum", [128, NCH, 12]); tot = fw.sbuf("tot", [128, NCH, 12])
    wend = fw.sbuf("wend", [128, NCH, 12]); cdec = fw.sbuf("cdec", [128, NCH, 12]); ncum = fw.sbuf("ncum", [128, NCH, 12])
    mk = fw.mark()
    psr_ = fw.psum("psrow", [128, 512])[:, 0:24]
    psc = fw.psum("psc", [128, 512])[:, 0:NCH * 12].rearrange("p (c n) -> p c n", n=12)
    op("pe", lambda e: e.matmul(psr_[:], lhsT=ones_f[0:1, :], rhs=rows[0:1, :], start=True, stop=True),
       reads=["rows", "ones_f"], writes=["psrow"])
    op("act", lambda e: e.activation(out=arep[:, 0:12], in_=psr_[:, 0:12], func=AF.Exp), reads=["psrow"], writes=["arep"])
    op("dve", lambda e: e.tensor_scalar(out=arep[:, 0:12], in0=arep[:, 0:12], scalar1=-1.0, scalar2=None, op0=ALU.mult),
       reads=["arep"], writes=["arep"])
    op("dve", lambda e: e.tensor_copy(out=arep[:, 12:24], in_=psr_[:, 12:24]), reads=["psrow"], writes=["arep"])
    dma(dtr[:], DTRAW.rearrange("(c p) n -> p c n", p=128), reads=["D:DTRAW"], writes=["dtr"])
    for ci in range(NCH):
        op("dve", lambda e: e.tensor_tensor(out=dtr[:, ci, :], in0=dtr[:, ci, :], in1=arep[:, 12:24], op=ALU.add),
           reads=["dtr", "arep"], writes=["dtr"])
    op("act", lambda e: e.activation(out=dt[:], in_=dtr[:], func=AF.Exp), reads=["dtr"], writes=["dt"])
    op("act", lambda e: e.activation(out=dt[:], in_=dt[:], func=AF.Ln, bias=1.0), reads=["dt"], writes=["dt"])
    for ci in range(NCH):
        op("dve", lambda e: e.tensor_tensor(out=dta[:, ci, :], in0=dt[:, ci, :], in1=arep[:, 0:12], op=ALU.mult),
           reads=["dt", "arep"], writes=["dta"])
    for d in range(2):
        op("pe", lambda e: e.matmul(psc[:], lhsT=tri[:, d, :], rhs=dta[:], start=True, stop=True),
           reads=["tri", "dta", "cum"], writes=["psc"])
        op("dve", lambda e: e.tensor_copy(out=cum[:, :, d * 6:(d + 1) * 6], in_=psc[:, :, d * 6:(d + 1) * 6]),
           reads=["psc"], writes=["cum"])
    op("pe", lambda e: e.matmul(psc[:], lhsT=ones_f[:], rhs=dta[:], start=True, stop=True),
       reads=["ones_f", "dta", "cum"], writes=["psc"])
    op("dve", lambda e: e.tensor_copy(out=tot[:], in_=psc[:]), reads=["psc"], writes=["tot"])
    op("dve", lambda e: e.tensor_tensor(out=wend[:], in0=tot[:], in1=cum[:], op=ALU.subtract),
       reads=["tot", "cum"], writes=["wend"])
    op("act", lambda e: e.activation(out=wend[:], in_=wend[:], func=AF.Exp), reads=["wend"], writes=["wend"])
    op("dve", lambda e: e.tensor_tensor(out=wend[:], in0=wend[:], in1=dt[:], op=ALU.mult), reads=["wend", "dt"], writes=["wend"])
    op("act", lambda e: e.activation(out=cdec[:], in_=tot[:], func=AF.Exp), reads=["tot"], writes=["cdec"])
    op("dve", lambda e: e.tensor_scalar(out=ncum[:], in0=cum[:], scalar1=-1.0, scalar2=None, op0=ALU.mult),
       reads=["cum"], writes=["ncum"])
    fw.release(mk)

    if G.get("dstop") == 2:
        fw.release(mk0); return
    mk = fw.mark()
    S = fw.sbuf("S", [64, NH, 64]); Sb = fw.sbuf("Sb", [64, NH, 64], BF16)
    Rm = [fw.sbuf(f"Rm{s}", [128, 128]) for s in range(2)]
    Em = [fw.sbuf(f"Em{s}", [64, 128]) for s in range(2)]
    sg = [fw.sbuf(f"sg{s}", [128, 128]) for s in range(2)]
    dc = [fw.sbuf(f"dc{s}", [128, 128]) for s in range(2)]
    MT = [fw.sbuf(f"MT{s}", [128, 128], BF16) for s in range(2)]
    CdT = [fw.sbuf(f"CdT{s}", [64, 128], BF16) for s in range(2)]
    Bw = [fw.sbuf(f"Bw{s}", [128, 64], BF16) for s in range(2)]
    ysb = [fw.sbuf(f"ysb{s}", [64, 128]) for s in range(2)]
    psG = [fw.psum(f"psG{g}", [128, 512])[:, 0:128] for g in range(2)]
    psA = [fw.psum(f"psA{s}", [128, 512])[:, 0:128] for s in range(2)]
    psY = [fw.psum(f"psY{s}", [128, 512])[0:64, 0:128] for s in range(2)]
    psS = [fw.psum(f"psSt{s}", [128, 512])[0:64, 0:64] for s in range(2)]
    it = 0
    for d in range(2):
        order = list(range(NCH)) if d == 0 else [1, 0] + list(range(NCH - 1, 1, -1))
        if DFLAG[0] & 16:
            order = order[:2]
        if DFLAG[0] & 32:
            order = order[:10]
        op("pool", lambda e: e.memset(S[:], 0.0), writes=["S"] + [f"S{h}" for h in range(NH)])
        op("pool", lambda e: e.memset(Sb[:], 0.0), writes=["Sb"] + [f"Sb{h}" for h in range(NH)])
        for ci in order:
            c0 = ci * 128
            for g in range(2):
                op("pe", lambda e: e.matmul(psG[g][:], lhsT=BC[g][:, c0:c0 + 128], rhs=BC[2 + g][:, c0:c0 + 128],
                                            start=True, stop=True), reads=[], writes=[f"psG{g}"])
            def ssd_chain(h, s, ci=ci, c0=c0, d=d):
                g = h // 3
                dh = d * 6 + h
                op("dve", lambda e: e.tensor_scalar(out=Rm[s][:], in0=tri[:, d, :], scalar1=dta[:, ci, dh:dh + 1],
                                                    scalar2=None, op0=ALU.mult), reads=[], writes=[f"Rm{s}"])
                yield
                op("pe", lambda e: e.matmul(psA[s][:], lhsT=ones_f[:], rhs=Rm[s][:], start=True, stop=True),
                   reads=[f"Rm{s}"], writes=[f"psA{s}"])
                yield
                op("act", lambda e: e.activation(out=Em[s][:], in_=psA[s][0:64, :], func=AF.Exp),
                   reads=[f"psA{s}"], writes=[f"Em{s}"])
                yield
                op("dve", lambda e: e.tensor_tensor(out=sg[s][:], in0=psA[s][:], in1=neg[:, d, :], op=ALU.add),
                   reads=[f"psA{s}"], writes=[f"sg{s}"])
                yield
                op("act", lambda e: e.activation(out=dc[s][:], in_=sg[s][:], func=AF.Exp, bias=ncum[:, ci, dh:dh + 1]),
                   reads=[f"sg{s}"], writes=[f"dc{s}"])
                op("pool", lambda e: e.tensor_scalar(out=Bw[s][:], in0=B_tok[:, ci, g * 64:(g + 1) * 64],
                                                     scalar1=wend[:, ci, dh:dh + 1], scalar2=None, op0=ALU.mult),
                   reads=[], writes=[f"Bw{s}"])
                yield
                op("dve", lambda e: e.scalar_tensor_tensor(out=MT[s][:], in0=dc[s][:], scalar=dt[:, ci, dh:dh + 1],
                                                           in1=psG[g][:], op0=ALU.mult, op1=ALU.mult),
                   reads=[f"dc{s}", f"psG{g}"], writes=[f"MT{s}"])
                yield
                op("dve", lambda e: e.tensor_tensor(out=CdT[s][:], in0=BC[2 + g][:, c0:c0 + 128], in1=Em[s][:],
                                                    op=ALU.mult), reads=[f"Em{s}"], writes=[f"CdT{s}"])
                yield
                op("pe", lambda e: e.matmul(psY[s][:], lhsT=xs_tok[:, ci, h * 64:(h + 1) * 64], rhs=MT[s][:],
                                            start=True, stop=False), reads=[f"MT{s}"], writes=[f"psY{s}"])
                op("pe", lambda e: e.matmul(psY[s][:], lhsT=Sb[:, h, :], rhs=CdT[s][:], start=False, stop=True),
                   reads=[f"CdT{s}", f"Sb{h}"], writes=[f"psY{s}"], pe_acc=True)
                op("pe", lambda e: e.matmul(psS[s][:], lhsT=Bw[s][:], rhs=xs_tok[:, ci, h * 64:(h + 1) * 64],
                                            start=True, stop=True), reads=[f"Bw{s}"], writes=[f"psSt{s}"])
                yield
                op("act", lambda e: e.copy(out=ysb[s][:], in_=psY[s][:]), reads=[f"psY{s}"], writes=[f"ysb{s}"])
                yield
                dma(YF[d, h * 64:(h + 1) * 64, c0:c0 + 128], ysb[s][:], reads=[f"ysb{s}"],
                    writes=[f"D:YF{d}_{h}_{ci}"], q="sp" if h % 2 else "pool")
                op("dve", lambda e: e.scalar_tensor_tensor(out=S[:, h, :], in0=S[:, h, :],
                                                           scalar=cdec[0:64, ci, dh:dh + 1], in1=psS[s][:],
                                                           op0=ALU.mult, op1=ALU.add),
                   reads=[f"psSt{s}", f"S{h}"], writes=[f"S{h}"])
                yield
                op("act", lambda e: e.copy(out=Sb[:, h, :], in_=S[:, h, :]), reads=[f"S{h}"], writes=[f"Sb{h}"])
                yield

            for h in range(0, NH, 2):
                _interleave([ssd_chain(h, 0), ssd_chain(h + 1, 1)])
    fw.release(mk)

    if G.get("dstop") == 3:
        fw.release(mk0); return
    mk = fw.mark()
    gm = fw.sbuf("gm", [128, 9, 128])
    dcol = fw.sbuf("dcol", [128, 3]); gso = fw.sbuf("gso", [128, 3])
    dma(gm[:], GMd.rearrange("a p i -> p a i"), writes=["gm"])
    dma(dcol[:], dcol_d[li].rearrange("(r p) -> p r", p=128), writes=["dcol"], allow_slow_non_contiguous=True)
    dma(gso[:], g_so[li].rearrange("(r p) -> p r", p=128), writes=["gso"], allow_slow_non_contiguous=True)
    y0 = [fw.sbuf(f"y0_{r}", [128, 512]) for r in range(3)]
    y1 = [fw.sbuf(f"y1_{r}", [128, 512]) for r in range(3)]
    zt = [fw.sbuf(f"zt_{r}", [128, 512]) for r in range(3)]
    sq = [fw.sbuf(f"dsq_{r}", [128, 512]) for r in range(3)]
    rs = fw.sbuf("drs", [128, 512])
    so = [fw.sbuf(f"so{s}", [128, 512], BF16) for s in range(2)]
    psn = [fw.psum(f"psn{s}", [128, 512]) for s in range(2)]
    tblocks = [(t0, min(512, T - t0)) for t0 in range(0, T, 512)]
    n = 0
    for (t0, tn) in tblocks:
        for r in range(3):
            rk = [f"D:YF{d}_{h}_{ci}" for d in range(2) for h in (2 * r, 2 * r + 1)
                  for ci in range(t0 // 128, (t0 + tn) // 128)]
            dma(y0[r][:, 0:tn], YF[0, r * 128:(r + 1) * 128, t0:t0 + tn], reads=rk, writes=[f"y0_{r}"])
            dma(y1[r][:, 0:tn], YF[1, r * 128:(r + 1) * 128, t0:t0 + tn], reads=rk, writes=[f"y1_{r}"], q="pool")
            dma(zt[r][:, 0:tn], PT[O_Z + r * 128:O_Z + (r + 1) * 128, t0:t0 + tn],
                reads=[f"D:PT{O_Z + r * 128}_{t0}"], writes=[f"zt_{r}"])
            op("pool", lambda e: e.tensor_tensor(out=y0[r][:, 0:tn], in0=y0[r][:, 0:tn], in1=y1[r][:, 0:tn], op=ALU.add),
               reads=[f"y0_{r}", f"y1_{r}"], writes=[f"y0_{r}"])
            op("dve", lambda e: e.scalar_tensor_tensor(out=y0[r][:, 0:tn], in0=xsT[:, r, t0:t0 + tn],
                                                       scalar=dcol[:, r:r + 1], in1=y0[r][:, 0:tn],
                                                       op0=ALU.mult, op1=ALU.add),
               reads=["xsT", "dcol", f"y0_{r}"], writes=[f"y0_{r}"])
            op("act", lambda e: e.activation(out=zt[r][:, 0:tn], in_=zt[r][:, 0:tn], func=AF.Silu),
               reads=[f"zt_{r}"], writes=[f"zt_{r}"])
            op("dve", lambda e: e.tensor_tensor(out=y0[r][:, 0:tn], in0=y0[r][:, 0:tn], in1=zt[r][:, 0:tn], op=ALU.mult),
               reads=[f"y0_{r}", f"zt_{r}"], writes=[f"y0_{r}"])
            op("act", lambda e: e.activation(out=sq[r][:, 0:tn], in_=y0[r][:, 0:tn], func=AF.Square),
               reads=[f"y0_{r}"], writes=[f"dsq_{r}"])
        for r2 in range(3):
            s = n % 2; n += 1
            for r in range(3):
                op("pe", lambda e: e.matmul(psn[s][:, 0:tn], lhsT=gm[:, r * 3 + r2, :], rhs=sq[r][:, 0:tn],
                                            start=(r == 0), stop=(r == 2)),
                   reads=[f"dsq_{r}", "gm"], writes=[f"psn{s}"], pe_acc=(r > 0))
            _rsqrt(fw, rs[:, 0:tn], psn[s][:, 0:tn], 1.0 / 192, f"psn{s}", "drs")
            op("dve", lambda e: e.scalar_tensor_tensor(out=so[s][:, 0:tn], in0=y0[r2][:, 0:tn],
                                                       scalar=gso[:, r2:r2 + 1], in1=rs[:, 0:tn],
                                                       op0=ALU.mult, op1=ALU.mult),
               reads=[f"y0_{r2}", "gso", "drs"], writes=[f"so{s}"])
            dma(SSMT[r2 * 128:(r2 + 1) * 128, t0:t0 + tn], so[s][:, 0:tn], reads=[f"so{s}"],
                writes=[f"D:SSMT{r2}_{t0}"], q="pool")
    fw.release(mk)
    fw.release(mk0)


def _sin5(fw, dst, ps, pskey, f5, fb5, tmps, n, tag):
    s_, s2, t_ = tmps
    fw.op("act", lambda e: e.activation(out=s_[:, 0:n], in_=ps[:, 0:n], func=AF.Sin, scale=f5, bias=fb5),
          reads=[pskey, "hyvec"], writes=[tag + "s"])
    fw.op("act", lambda e: e.activation(out=s2[:, 0:n], in_=s_[:, 0:n], func=AF.Square), reads=[tag + "s"], writes=[tag + "s2"])
    fw.op("dve", lambda e: e.tensor_scalar(out=t_[:, 0:n], in0=s2[:, 0:n], scalar1=16.0, scalar2=-20.0,
                                           op0=ALU.mult, op1=ALU.add), reads=[tag + "s2"], writes=[tag + "t"])
    fw.op("dve", lambda e: e.tensor_tensor(out=t_[:, 0:n], in0=t_[:, 0:n], in1=s2[:, 0:n], op=ALU.mult),
          reads=[tag + "t", tag + "s2"], writes=[tag + "t"])
    fw.op("dve", lambda e: e.scalar_tensor_tensor(out=dst, in0=t_[:, 0:n], scalar=5.0, in1=s_[:, 0:n],
                                                  op0=ALU.add, op1=ALU.mult), reads=[tag + "t", tag + "s"], writes=[tag + "h"])


def phase_E(G):
    P, fw, nc, li = G["P"], G["fw"], G["nc"], G["li"]
    op, dma = fw.op, fw.dma
    PT = G["PT"]
    ones_f, ident_f = G["ones_f"], G["ident_f"]
    last = li == DEPTH - 1
    w_conv = _inp(P, "w_conv_hy", [DEPTH, 3, 768]); b_conv = _inp(P, "b_conv_hy", [DEPTH, 768])
    w_f1 = _inp(P, "w_f1", [DEPTH, 33, 64]); w_f2 = _inp(P, "w_f2", [DEPTH, 64, 64]); w_f3 = _inp(P, "w_f3", [DEPTH, 64, 1024])
    hyvec_d = _inp(P, "hy_vecs", [DEPTH, 4, 64])
    d_skip = _inp(P, "d_skip_hy", [DEPTH, 2, 256])
    ndl_d = _inp(P, "hy_ndelta", [128, 2])
    JR = _inp(P, "jrev", [128, 128])
    HYT = _scr(P, "HYT", [HY_CH, T], BF16)

    seqs = [(SEQ, CTX)] if last else [(CTX, 0), (SEQ, CTX)]
    for (L, tok0) in seqs:
        NJ = L // 128
        FWD = 2 * L
        feats_d = _inp(P, f"hy_feats_{L}", [2, 33, L])
        t01_d = _inp(P, f"hy_t01_{L}", [2, 128, L])
        if f"FD{L}" not in P.dscr:
            P.dh = getattr(P, "dh", {})
            P.dh[f"FD{L}"] = nc.dram_tensor(f"FD{L}", [2, 256, FWD], BF16, kind="ExternalOutput" if "FILT" in P.debug and L == SEQ else "Internal")
            P.dscr[f"FD{L}"] = P.dh[f"FD{L}"].ap()
        FDh = P.dh[f"FD{L}"]; FD = P.dscr[f"FD{L}"]

        mk = fw.mark()
        w1 = fw.sbuf("w1", [33, 64]); w2 = fw.sbuf("w2", [64, 64]); w3 = fw.sbuf("w3", [64, 1024])
        hv = fw.sbuf("hv", [64, 4]); f5 = fw.sbuf("f5", [64, 2]); fb5 = fw.sbuf("fb5", [64, 2])
        ndl = fw.sbuf("ndl", [128, 2])
        dma(w1[:], w_f1[li], writes=["w1"]); dma(w2[:], w_f2[li], writes=["w2"]); dma(w3[:], w_f3[li], writes=["w3"])
        dma(hv[:], hyvec_d[li].rearrange("v p -> p v"), writes=["hv"], allow_slow_non_contiguous=True)
        dma(ndl[:], ndl_d, writes=["ndl"])
        for i in range(2):
            op("dve", lambda e: e.tensor_scalar(out=f5[:, i:i + 1], in0=hv[:, 2 * i + 1:2 * i + 2], scalar1=0.2,
                                                scalar2=None, op0=ALU.mult), reads=["hv"], writes=["hyvec"])
            op("dve", lambda e: e.tensor_tensor(out=fb5[:, i:i + 1], in0=f5[:, i:i + 1], in1=hv[:, 2 * i:2 * i + 1],
                                                op=ALU.mult), reads=["hyvec", "hv"], writes=["hyvec"])
        FB = [[fw.sbuf(f"FB{o}{ch}", [128, FWD]) for ch in range(2)] for o in range(2)]
        featb = [fw.sbuf(f"featb{s}", [33, 512]) for s in range(2)]
        t01b = [fw.sbuf(f"t01b{s}", [128, 512]) for s in range(2)]
        tm = [[fw.sbuf(f"tm{a}{b}", [64, 512]) for b in range(3)] for a in range(2)]
        h1 = fw.sbuf("h1", [64, 512]); h2 = fw.sbuf("h2", [64, 512])
        dec = [fw.sbuf(f"dec{ch}", [128, 512]) for ch in range(2)]
        pm1 = fw.psum("pm1", [64, 512]); pm2 = fw.psum("pm2", [64, 512])
        pm3 = [fw.psum(f"pm3{s}", [128, 512]) for s in range(2)]
        nrm = fw.sbuf("nrm", [128, 4])
        FBb = [fw.sbuf(f"FBb{s}", [128, FWD], BF16) for s in range(2)]
        BL = min(512, L)
        n3 = 0
        for dr in (1, 0):
            for bi, c0 in enumerate(range(0, L, BL)):
                s = bi % 2
                dma(featb[s][:, 0:BL], feats_d[dr, :, c0:c0 + BL], writes=[f"featb{s}"])
                dma(t01b[s][:, 0:BL], t01_d[dr, :, c0:c0 + BL], writes=[f"t01b{s}"], q="pool")
                op("pe", lambda e: e.matmul(pm1[:, 0:BL], lhsT=w1[:], rhs=featb[s][:, 0:BL], start=True, stop=True),
                   reads=["w1", f"featb{s}"], writes=["pm1"])
                _sin5(fw, h1[:, 0:BL], pm1, "pm1", f5[:, 0:1], fb5[:, 0:1], tm[0], BL, "a")
                op("pe", lambda e: e.matmul(pm2[:, 0:BL], lhsT=w2[:], rhs=h1[:, 0:BL], start=True, stop=True),
                   reads=["w2", "ah"], writes=["pm2"])
                _sin5(fw, h2[:, 0:BL], pm2, "pm2", f5[:, 1:2], fb5[:, 1:2], tm[1], BL, "b")
                col0 = (L - 1 + c0) if dr == 0 else c0
                for ch in range(2):
                    op("act", lambda e: e.activation(out=dec[ch][:, 0:BL], in_=t01b[s][:, 0:BL], func=AF.Exp,
                                                     scale=ndl[:, ch:ch + 1]),
                       reads=[f"t01b{s}", "ndl"], writes=[f"dec{ch}"])
                    for o in range(2):
                        p3 = n3 % 2; n3 += 1
                        cb = dr * 512 + o * 256 + ch * 128
                        op("pe", lambda e: e.matmul(pm3[p3][:, 0:BL], lhsT=w3[:, cb:cb + 128], rhs=h2[:, 0:BL],
                                                    start=True, stop=True), reads=["w3", "bh"], writes=[f"pm3{p3}"])
                        op("dve", lambda e: e.tensor_tensor(out=FB[o][ch][:, col0:col0 + BL], in0=pm3[p3][:, 0:BL],
                                                            in1=dec[ch][:, 0:BL], op=ALU.mult),
                           reads=[f"pm3{p3}", f"dec{ch}"], writes=[f"FB{o}{ch}"])
        for o in range(2):
            for ch in range(2):
                k = o * 2 + ch
                op("dve", lambda e: e.tensor_reduce(out=nrm[:, k:k + 1], in_=FB[o][ch][:, 0:2 * L - 1], axis=AX.X,
                                                    op=ALU.add, apply_absolute_value=True),
                   reads=[f"FB{o}{ch}"], writes=[f"nrm{k}"])
                op("dve", lambda e: e.tensor_scalar(out=nrm[:, k:k + 1], in0=nrm[:, k:k + 1], scalar1=EPS, scalar2=None,
                                                    op0=ALU.add), reads=[f"nrm{k}"], writes=[f"nrm{k}"])
                op("dve", lambda e: e.reciprocal(out=nrm[:, k:k + 1], in_=nrm[:, k:k + 1]), reads=[f"nrm{k}"], writes=[f"nrm{k}"])
                op("pool" if k % 2 else "dve",
                   lambda e: e.tensor_scalar(out=FBb[k % 2][:, 0:2 * L - 1], in0=FB[o][ch][:, 0:2 * L - 1],
                                             scalar1=nrm[:, k:k + 1], scalar2=None, op0=ALU.mult),
                   reads=[f"nrm{k}", f"FB{o}{ch}"], writes=[f"FBb{k % 2}"])
                dma(FD[o, ch * 128:(ch + 1) * 128, 0:2 * L - 1], FBb[k % 2][:, 0:2 * L - 1], reads=[f"FBb{k % 2}"],
                    writes=[f"D:FD{L}"], q="sp" if k % 2 else "pool")
        fw.release(mk)

        GW = (2 * NJ - 1) * 128
        CG = 512 // NJ if NJ <= 32 else 16
        CG = min(CG, 16)
        for ch in range(2):
            mk = fw.mark()
            Pq = [fw.sbuf(f"Pq{q}", [128, NJ, 128]) for q in range(3)]
            zb = [fw.sbuf(f"zb{s}", [128, NJ, 128]) for s in range(2)]
            zrev = fw.sbuf("zrev", [128, NJ, 128], BF16)
            zd = fw.sbuf("zd", [128, NJ, 128])
            dsk = fw.sbuf("dsk", [128, 128])
            drow = fw.sbuf("drow", [1, 128])
            jr = fw.sbuf("jr", [128, 128])
            wch = fw.sbuf("wch", [128, 3, 3]); bch = fw.sbuf("bch", [128, 3])
            dma(jr[:], JR, writes=["jr"])
            for q in range(3):
                r0 = q * 256 + ch * 128
                for k in range(3):
                    dma(wch[:, q, k:k + 1], w_conv[li, k, r0:r0 + 128].rearrange("(p o) -> p o", o=1), writes=["wch"],
                        allow_slow_non_contiguous=True)
                dma(bch[:, q:q + 1], b_conv[li, r0:r0 + 128].rearrange("(p o) -> p o", o=1), writes=["bch"],
                    allow_slow_non_contiguous=True)
            mk2 = fw.mark()
            xb = fw.sbuf("hxb", [128, L + 2]); pT_ = fw.sbuf("hpT", [128, L])
            ptr = [fw.psum(f"hptr{s}", [128, 4, 128]) for s in range(2)]
            op("pool", lambda e: e.memset(xb[:], 0.0), writes=["hxb"])
            nt = 0
            for q in range(3):
                r0 = O_HY + q * 256 + ch * 128
                rk = [k for k in fw.lastw if k.startswith("D:PT") and r0 - 127 <= int(k[4:].split("_")[0]) <= r0 + 127]
                dma(xb[:, 1:1 + L], PT[r0:r0 + 128, tok0:tok0 + L], reads=rk, writes=["hxb"])
                op("dve", lambda e: e.tensor_scalar(out=pT_[:], in0=xb[:, 0:L], scalar1=wch[:, q, 0:1], scalar2=bch[:, q:q + 1],
                                                    op0=ALU.mult, op1=ALU.add), reads=["hxb", "wch", "bch"], writes=["hpT"])
                for k in (1, 2):
                    op("dve", lambda e: e.scalar_tensor_tensor(out=pT_[:], in0=xb[:, k:k + L], scalar=wch[:, q, k:k + 1],
                                                               in1=pT_[:], op0=ALU.mult, op1=ALU.add),
                       reads=["hxb", "hpT", "wch"], writes=["hpT"])
                for j0 in range(0, NJ, 4):
                    s = nt % 2; nt += 1
                    jn = min(4, NJ - j0)
                    for j in range(jn):
                        op("pe", lambda e: e.transpose(out=ptr[s][:, j, :], in_=pT_[:, (j0 + j) * 128:(j0 + j + 1) * 128],
                                                       identity=ident_f[:]), reads=["hpT", "ident_f"], writes=[f"hptr{s}"],
                           pe_acc=(j > 0))
                    op("act", lambda e: e.copy(out=Pq[q][:, j0:j0 + jn, :], in_=ptr[s][:, 0:jn, :]),
                       reads=[f"hptr{s}"], writes=[f"Pq{q}"])
            fw.release(mk2)

            NG = 4
            Gt = [fw.sbuf(f"Gt{s}", [128, GW], BF16) for s in range(NG)]
            psJ = [fw.psum(f"psJ{s}", [128, 4, 128]) for s in range(2)]
            Yb = [fw.psum(f"Yb{s}", [128, CG, NJ]) for s in range(2)]
            psd_ = fw.psum("hpsd", [128, 128])
            hout = [fw.sbuf(f"hout{s}", [128, 512], BF16) for s in range(2)]
            zcur = Pq[0]; zkey = "Pq0"
            ng = 0; ny = 0
            for o in range(2):
                dma(drow[:], d_skip[li, o:o + 1, ch * 128:(ch + 1) * 128], writes=["drow"])
                op("pe", lambda e: e.matmul(psd_[:], lhsT=ones_f[0:1, :], rhs=drow[0:1, :], start=True, stop=True),
                   reads=["drow", "ones_f"], writes=["hpsd"])
                op("act", lambda e: e.copy(out=dsk[:], in_=psd_[:]), reads=["hpsd"], writes=["dsk"])
                for j in range(NJ):
                    op("pool" if j % 2 else "dve", lambda e: e.tensor_tensor(out=zd[:, j, :], in0=zcur[:, j, :], in1=dsk[:],
                                                                             op=ALU.mult),
                       reads=[zkey, "dsk"], writes=["zd"])
                for j0 in range(0, NJ, 4):
                    s = (j0 // 4) % 2
                    jn = min(4, NJ - j0)
                    op("pe", lambda e: e.matmul(psJ[s][:, 0:jn, :], lhsT=jr[:], rhs=zcur[:, j0:j0 + jn, :], start=True, stop=True),
                       reads=[zkey, "jr"], writes=[f"psJ{s}"])
                    op("act", lambda e: e.copy(out=zrev[:, j0:j0 + jn, :], in_=psJ[s][:, 0:jn, :]),
                       reads=[f"psJ{s}"], writes=["zrev"])
                znext = zb[o]; nkey = f"zb{o}"
                for c0 in range(0, 128, CG):
                    yb = ny % 2; ny += 1
                    for cc in range(CG):
                        c = c0 + cc
                        gs = ng % NG; ng += 1
                        row = (o * 256 + ch * 128 + c) * FWD
                        dma(Gt[gs][:], bass.AP(tensor=FDh, offset=row, ap=[[1, 128], [1, GW]]),
                            reads=[f"D:FD{L}"], writes=[f"Gt{gs}"], q="sp" if ng % 2 else "act")
                        ds_ = [0] + [d for d in range(-(NJ - 1), NJ) if d != 0]
                        for di, d in enumerate(ds_):
                            J0 = max(0, -d); J1 = min(NJ, NJ - d)
                            op("pe", lambda e: e.matmul(Yb[yb][:, cc, J0 + d:J1 + d],
                                                        lhsT=Gt[gs][:, (d + NJ - 1) * 128:(d + NJ) * 128],
                                                        rhs=zrev[:, J0:J1, c], start=(di == 0), stop=(di == len(ds_) - 1)),
                               reads=[f"Gt{gs}", "zrev"], writes=[f"Yb{yb}"], pe_acc=(di > 0))
                    zv = znext[:, :, c0:c0 + CG].rearrange("p j c -> p c j")
                    op("dve", lambda e: e.tensor_tensor(out=zv, in0=Yb[yb][:], in1=zd[:, :, c0:c0 + CG].rearrange("p j c -> p c j"),
                                                        op=ALU.add), reads=[f"Yb{yb}", "zd"], writes=[nkey])
                    op("pool", lambda e: e.tensor_tensor(out=zv, in0=zv, in1=Pq[o + 1][:, :, c0:c0 + CG].rearrange("p j c -> p c j"),
                                                         op=ALU.mult), reads=[nkey, f"Pq{o + 1}"], writes=[nkey])
                zcur = znext; zkey = nkey
            for j0 in range(0, NJ, 4):
                s = (j0 // 4) % 2
                jn = min(4, NJ - j0)
                for j in range(jn):
                    op("pe", lambda e: e.transpose(out=psJ[s][:, j, :], in_=zcur[:, j0 + j, :], identity=ident_f[:]),
                       reads=[zkey, "ident_f"], writes=[f"psJ{s}"], pe_acc=(j > 0))
                op("act", lambda e: e.copy(out=hout[s][:, 0:jn * 128], in_=psJ[s][:, 0:jn, :].rearrange("p j t -> p (j t)")),
                   reads=[f"psJ{s}"], writes=[f"hout{s}"])
                dma(HYT[ch * 128:(ch + 1) * 128, tok0 + j0 * 128:tok0 + (j0 + jn) * 128], hout[s][:, 0:jn * 128],
                    reads=[f"hout{s}"], writes=[f"D:HYT{ch}_{tok0 + j0 * 128}"], q="pool")
            fw.release(mk)
    if last:
        pass


def phase_F(G):
    P, fw, nc, li = G["P"], G["fw"], G["nc"], G["li"]
    op, dma = fw.op, fw.dma
    XR, MOD, OUT = G["XR"], G["MOD"], G["OUT"]
    ident_b = G["ident_b"]
    last = li == DEPTH - 1
    w_out = _inp(P, "w_out", [DEPTH, D, D])
    w_ff1 = _inp(P, "w_ff1", [DEPTH, D, DFF]); w_ff2 = _inp(P, "w_ff2", [DEPTH, DFF, D])
    ATT = _scr(P, "ATT", [NH, 64, T], BF16)
    SSMT = _scr(P, "SSMT", [SSM_INNER, T], BF16)
    HYT = _scr(P, "HYT", [HY_CH, T], BF16)

    mk0 = fw.mark()
    w1b = fw.sbuf("w1b", [128, 8, DFF], BF16)
    w2b = fw.sbuf("w2b", [128, 32, D], BF16)
    wo_att = fw.sbuf("wo_att", [64, NH, D], BF16)
    wo_rest = fw.sbuf("wo_rest", [128, 5, D], BF16)
    mk = fw.mark()
    stg = [fw.sbuf(f"fstg{s}", [128, 8, 512]) for s in range(2)]
    n = 0
    for nb in range(8):
        s = n % 2; n += 1
        _load_cast(fw, w1b[:, :, nb * 512:(nb + 1) * 512], w_ff1[li, :, nb * 512:(nb + 1) * 512]
                   .rearrange("(k p) n -> p k n", p=128), stg[s][:], "w1b", f"fstg{s}", q="sp" if n % 2 else "pool")
    for kb in range(4):
        for half in range(2):
            s = n % 2; n += 1
            _load_cast(fw, w2b[:, kb * 8:(kb + 1) * 8, half * 512:(half + 1) * 512],
                       w_ff2[li, kb * 1024:(kb + 1) * 1024, half * 512:(half + 1) * 512]
                       .rearrange("(k p) n -> p k n", p=128), stg[s][:], "w2b", f"fstg{s}",
                       q="sp" if n % 2 else "pool")
    for half in range(2):
        s = n % 2; n += 1
        _load_cast(fw, wo_att[:, :, half * 512:(half + 1) * 512],
                   w_out[li, 0:384, half * 512:(half + 1) * 512].rearrange("(h c) n -> c h n", c=64),
                   stg[s][0:64, 0:6, :], "wo_att", f"fstg{s}")
        s = n % 2; n += 1
        _load_cast(fw, wo_rest[:, :, half * 512:(half + 1) * 512],
                   w_out[li, 384:1024, half * 512:(half + 1) * 512].rearrange("(k p) n -> p k n", p=128),
                   stg[s][:, 0:5, :], "wo_rest", f"fstg{s}")
    fw.release(mk)

    mods = [fw.sbuf(f"fmod{d}", [128, D]) for d in range(4)]
    attb = [fw.sbuf(f"attb{s}", [64, NH, 128], BF16) for s in range(2)]
    ssmb = [fw.sbuf(f"ssmb{s}", [128, 3, 128], BF16) for s in range(2)]
    hyb = [fw.sbuf(f"hyb{s}", [128, 2, 128], BF16) for s in range(2)]
    xt = [fw.sbuf(f"fxt{s}", [128, D]) for s in range(2)]
    xm = [fw.sbuf(f"fxm{s}", [128, D]) for s in range(2)]
    hb = fw.sbuf("fhb", [128, D], BF16)
    h2T = fw.sbuf("h2T", [128, 8, 128], BF16)
    aT = fw.sbuf("aT", [128, 32, 128], BF16)
    rr = [fw.sbuf(f"frr{s}", [128, 128]) for s in range(3)]
    ss = fw.sbuf("fss", [128, NCH]); rstd = fw.sbuf("frstd", [128, NCH])
    ot = [fw.sbuf(f"fot{s}", [128, 512]) for s in range(2)]
    psm = [fw.psum(f"psm{s}", [128, 512]) for s in range(2)]
    pst = fw.psum("fpst", [128, 8, 128], BF16)
    psf = [fw.psum(f"psf{s}", [128, 128]) for s in range(3)]
    pso = [fw.psum(f"fpso{s}", [128, 512]) for s in range(2)]

    chunks = list(range(2 if last else 0, NCH))
    state = {"kind": None, "n": 0}

    def front(ci, b):
        kind = 1 if ci < 2 else 0
        t0 = ci * 128
        if kind != state["kind"]:
            for d, src in enumerate((2, 3, 4, 5)):
                dma(mods[d][:], MOD[kind, src], reads=[f"D:MOD{kind}_{src}"], writes=[f"fmod{d}"])
            state["kind"] = kind
        dma(attb[b][:], ATT[:, :, t0:t0 + 128].rearrange("h c t -> c h t"),
            reads=[k for k in fw.lastw if k.startswith("D:ATT")], writes=[f"attb{b}"])
        dma(ssmb[b][:], SSMT[:, t0:t0 + 128].rearrange("(k p) t -> p k t", p=128),
            reads=[k for k in fw.lastw if k.startswith("D:SSMT")], writes=[f"ssmb{b}"], q="pool")
        dma(hyb[b][:], HYT[:, t0:t0 + 128].rearrange("(k p) t -> p k t", p=128),
            reads=[k for k in fw.lastw if k.startswith("D:HYT")], writes=[f"hyb{b}"], q="pool")
        dma(xt[b][:], XR[t0:t0 + 128, :], reads=[f"D:XR{ci}"], writes=[f"fxt{b}"])
        for half in range(2):
            ops_ = [(attb[b][:, h, :], wo_att[:, h, half * 512:(half + 1) * 512], f"attb{b}", "wo_att") for h in range(NH)]
            ops_ += [(ssmb[b][:, k, :], wo_rest[:, k, half * 512:(half + 1) * 512], f"ssmb{b}", "wo_rest") for k in range(3)]
            ops_ += [(hyb[b][:, k, :], wo_rest[:, 3 + k, half * 512:(half + 1) * 512], f"hyb{b}", "wo_rest") for k in range(2)]
            for i, (l_, r_, lk, rk) in enumerate(ops_):
                op("pe", lambda e: e.matmul(psm[half][:], lhsT=l_, rhs=r_, start=(i == 0), stop=(i == len(ops_) - 1)),
                   reads=[lk, rk], writes=[f"psm{half}"], pe_acc=(i > 0))
            op("dve", lambda e: e.tensor_tensor(out=xm[b][:, half * 512:(half + 1) * 512], in0=psm[half][:],
                                                in1=mods[0][:, half * 512:(half + 1) * 512], op=ALU.mult),
               reads=[f"psm{half}", "fmod0"], writes=[f"fxm{b}"])
        op("dve", lambda e: e.tensor_tensor(out=xm[b][:], in0=xm[b][:], in1=xt[b][:], op=ALU.add),
           reads=[f"fxm{b}", f"fxt{b}"], writes=[f"fxm{b}"])
        op("act", lambda e: e.activation(out=hb[:], in_=xm[b][:], func=AF.Square, accum_out=ss[:, ci:ci + 1]),
           reads=[f"fxm{b}"], writes=["fhb", f"fss{ci}"])
        _rsqrt(fw, rstd[:, ci:ci + 1], ss[:, ci:ci + 1], 1.0 / D, f"fss{ci}", f"frstd{ci}")
        op("dve", lambda e: e.scalar_tensor_tensor(out=xt[b][:], in0=xm[b][:], scalar=rstd[:, ci:ci + 1],
                                                   in1=mods[1][:], op0=ALU.mult, op1=ALU.mult),
           reads=[f"fxm{b}", f"frstd{ci}", "fmod1"], writes=[f"fxt{b}"])
        op("dve", lambda e: e.tensor_tensor(out=hb[:], in0=xt[b][:], in1=mods[2][:], op=ALU.add),
           reads=[f"fxt{b}", "fmod2"], writes=["fhb"])

    def mid(ci, b):
        for k in range(8):
            op("pe", lambda e: e.transpose(out=pst[:, k, :], in_=hb[:, k * 128:(k + 1) * 128], identity=ident_b[:]),
               reads=["fhb", "ident_b"], writes=["fpst"], pe_acc=(k > 0))
        op("act", lambda e: e.copy(out=h2T[:], in_=pst[:]), reads=["fpst"], writes=["h2T"])
        for j in range(32):
            s = j % 3
            for k in range(8):
                op("pe", lambda e: e.matmul(psf[s][:], lhsT=w1b[:, k, j * 128:(j + 1) * 128], rhs=h2T[:, k, :],
                                            start=(k == 0), stop=(k == 7)),
                   reads=["h2T", "w1b"], writes=[f"psf{s}"], pe_acc=(k > 0))
            op("act", lambda e: e.activation(out=rr[s][:], in_=psf[s][:], func=AF.Relu),
               reads=[f"psf{s}"], writes=[f"frr{s}"])
            op("dve", lambda e: e.tensor_tensor(out=aT[:, j, :], in0=rr[s][:], in1=rr[s][:],
                                                                     op=ALU.mult),
               reads=[f"frr{s}"], writes=[f"aT{j}"])

    def back(ci, b):
        t0 = ci * 128
        for half in range(2):
            s = state["n"] % 2; state["n"] += 1
            for j in range(32):
                op("pe", lambda e: e.matmul(pso[s][:], lhsT=aT[:, j, :], rhs=w2b[:, j, half * 512:(half + 1) * 512],
                                            start=(j == 0), stop=(j == 31)),
                   reads=[f"aT{j}", "w2b"], writes=[f"fpso{s}"], pe_acc=(j > 0))
            op("dve", lambda e: e.tensor_tensor(out=ot[s][:], in0=pso[s][:],
                                                in1=mods[3][:, half * 512:(half + 1) * 512], op=ALU.mult),
               reads=[f"fpso{s}", "fmod3"], writes=[f"fot{s}"])
            op("dve", lambda e: e.tensor_tensor(out=ot[s][:], in0=ot[s][:], in1=xm[b][:, half * 512:(half + 1) * 512],
                                                 op=ALU.add), reads=[f"fot{s}", f"fxm{b}"], writes=[f"fot{s}"])
            if last:
                dma(OUT[t0 - CTX:t0 - CTX + 128, half * 512:(half + 1) * 512], ot[s][:], reads=[f"fot{s}"],
                    writes=[f"D:OUT{ci}_{half}"], q="pool")
            else:
                dma(XR[t0:t0 + 128, half * 512:(half + 1) * 512], ot[s][:], reads=[f"fot{s}"],
                    writes=[f"D:XR{ci}"], q="pool")

    front(chunks[0], 0)
    for i, ci in enumerate(chunks):
        b = i % 2
        mid(ci, b)
        nxt = chunks[i + 1] if i + 1 < len(chunks) else None
        same_kind = nxt is not None and ((nxt < 2) == (ci < 2))
        if nxt is not None and same_kind:
            front(nxt, 1 - b)
            back(ci, b)
        else:
            back(ci, b)
            if nxt is not None:
                front(nxt, 1 - b)
    fw.release(mk0)
```
